# Optimizing a Trainium2 kernel written in Bass

```python
import math
import jax
import jax.numpy as jnp
from jax import lax
import numpy as np

D_MODEL = 1024
BATCH = 16
SEQ = 256
DEPTH = 2
DEC_BATCH = 2
DEC_SEQ = 4096
PAST_LEN = 256

GRID_W = 64
N_DIR = 2
GROUP_W = D_MODEL // 4
MIX_W = 4 * GROUP_W
HEAD_DIM = 64
N_HEADS_G = GROUP_W // HEAD_DIM
CONV_K = 3
RW_LORA = 64
SSM_N = 64
SSM_G = 2
SSM_CONV_CH = GROUP_W + 2 * SSM_G * SSM_N
CHUNK = 64
LRU_C = 8.0
PEER_HEADS = 8
N_KEYS = 128
N_EXPERTS = N_KEYS * N_KEYS
PEER_TOPK = 16
PEER_DQ = 128
PEER_BLOCK = 128
EPS = 1e-6

SPLIT_SIZES = (
    GROUP_W, GROUP_W, GROUP_W, GROUP_W,
    GROUP_W, 2 * SSM_G * SSM_N, GROUP_W, N_DIR * N_HEADS_G,
    GROUP_W, GROUP_W, GROUP_W, GROUP_W,
    N_DIR * N_HEADS_G, N_DIR * N_HEADS_G,
    GROUP_W, GROUP_W,
)
D_IN = 12 * GROUP_W + 2 * SSM_G * SSM_N + 3 * N_DIR * N_HEADS_G

kernel_name = "hybrid_bidir_recurrent_peer_dit_step"


def rmsnorm(x, gain):
    xf = x.astype(jnp.float32)
    y = xf * lax.rsqrt(jnp.mean(xf * xf, axis=-1, keepdims=True) + EPS)
    return (y * gain.astype(jnp.float32)).astype(x.dtype)


def head_rmsnorm(y, gain):
    y = y * lax.rsqrt(jnp.mean(y * y, axis=-1, keepdims=True) + EPS)
    return y * gain.astype(jnp.float32).reshape(y.shape[-2:])


def heads(t, n):
    return t.reshape(t.shape[:2] + (n, t.shape[-1] // n))


def split_cols(u):
    cuts = [int(v) for v in np.cumsum(SPLIT_SIZES)[:-1]]
    return jnp.split(u, cuts, axis=-1)


def conv_seq(u, w, b):
    n_ch = u.shape[-1]
    y = lax.conv_general_dilated(u, w[1][:, None, :], window_strides=(1,),
                                 padding=[(CONV_K // 2, CONV_K // 2)],
                                 dimension_numbers=("NWC", "WIO", "NWC"), feature_group_count=n_ch)
    return y + b


def conv_grid(u, w, b):
    bsz, n_tok, n_ch = u.shape
    rows = n_tok // GRID_W
    g = u.reshape(bsz, rows, GRID_W, n_ch)
    y = lax.conv_general_dilated(g, w[:, :, None, :], window_strides=(1, 1),
                                 padding=[(CONV_K // 2, CONV_K // 2)] * 2,
                                 dimension_numbers=("NHWC", "HWIO", "NHWC"), feature_group_count=n_ch)
    return y.reshape(bsz, n_tok, n_ch) + b


def to_col_major(u):
    bsz, n_tok, n_ch = u.shape
    rows = n_tok // GRID_W
    return u.reshape(bsz, rows, GRID_W, n_ch).transpose(0, 2, 1, 3).reshape(bsz, n_tok, n_ch)


def to_row_major(u):
    bsz, n_tok, n_ch = u.shape
    rows = n_tok // GRID_W
    return u.reshape(bsz, GRID_W, rows, n_ch).transpose(0, 2, 1, 3).reshape(bsz, n_tok, n_ch)


def to_chunks(t):
    bsz, n_tok = t.shape[:2]
    return jnp.moveaxis(t.reshape((bsz, n_tok // CHUNK, CHUNK) + t.shape[2:]), 1, 0)


def from_chunks(t):
    n_c, bsz, ln = t.shape[:3]
    return jnp.moveaxis(t, 0, 1).reshape((bsz, n_c * ln) + t.shape[3:])


def run_dir(scan_fn, seqs, init, backward):
    if backward:
        seqs = [jnp.flip(t, axis=1) for t in seqs]
    y, fin = scan_fn(*seqs, *init)
    return (jnp.flip(y, axis=1) if backward else y), fin


def rwkv7_scan(r, k, v, kk, a, w, S0):
    def step(S, inp):
        r_t, k_t, v_t, kk_t, a_t, w_t = inp
        s_kk = jnp.einsum("bhvk,bhk->bhv", S, kk_t)
        S = (S * w_t[:, :, None, :] - s_kk[..., None] * (a_t * kk_t)[:, :, None, :]
             + v_t[..., None] * k_t[:, :, None, :])
        return S, jnp.einsum("bhvk,bhk->bhv", S, r_t)
    xs = tuple(jnp.moveaxis(t, 1, 0) for t in (r, k, v, kk, a, w))
    S, ys = lax.scan(step, S0, xs)
    return jnp.moveaxis(ys, 0, 1), S


def rwkv7_mixer(h, r, k, v, g, lp, S0):
    f32 = jnp.float32
    bsz, n_tok, _ = h.shape
    H = N_HEADS_G
    r, k, v = (heads(t, H).astype(f32) for t in (r, k, v))
    kk = k * lp["rw_kk"].astype(f32).reshape(H, HEAD_DIM)
    kk = kk * lax.rsqrt(jnp.sum(kk * kk, axis=-1, keepdims=True) + EPS)
    ka = lp["rw_ka"].astype(f32).reshape(H, HEAD_DIM)
    S0 = S0.astype(f32)
    ys, fins = [], []
    for d in range(N_DIR):
        w_pre = (lp["rw_w0"][d] + jnp.tanh(h @ lp["rw_wA"][d]) @ lp["rw_wB"][d]).astype(f32)
        decay = jnp.exp(-jnp.exp(-jax.nn.softplus(-w_pre) - 0.5))
        a = jax.nn.sigmoid((lp["rw_a0"][d] + (h @ lp["rw_aA"][d]) @ lp["rw_aB"][d]).astype(f32))
        decay, a = heads(decay, H), heads(a, H)
        k_t = k * (1.0 + (a - 1.0) * ka)
        y, fin = run_dir(rwkv7_scan, [r, k_t, v, kk, a, decay], (S0[:, d],), d == 1)
        ys.append(y)
        fins.append(fin)
    y = head_rmsnorm(ys[0] + ys[1], lp["rw_ln"])
    bonus = jnp.sum(r * k * lp["rw_rk"].astype(f32), axis=-1, keepdims=True) * v
    out = (y + bonus).reshape(bsz, n_tok, GROUP_W) * jax.nn.sigmoid(g.astype(f32))
    return out, jnp.stack(fins, axis=1)


def ssd_scan(xdt, la, Bm, Cm, h0):
    mask = jnp.tril(jnp.ones((CHUNK, CHUNK), dtype=bool))[None, :, :, None]

    def step(hs, inp):
        x_c, a_c, b_c, c_c = inp
        cum = jnp.cumsum(a_c, axis=1)
        seg = cum[:, :, None, :] - cum[:, None, :, :]
        decay = jnp.where(mask, jnp.exp(jnp.where(mask, seg, 0.0)), 0.0)
        scores = jnp.einsum("blhn,bshn->bhls", c_c, b_c) * jnp.moveaxis(decay, 3, 1)
        y = jnp.einsum("bhls,bshp->blhp", scores, x_c)
        y = y + jnp.einsum("blhn,bhpn->blhp", c_c, hs) * jnp.exp(cum)[..., None]
        tail = jnp.exp(cum[:, -1:, :] - cum)
        hs = (hs * jnp.exp(cum[:, -1, :])[:, :, None, None]
              + jnp.einsum("blh,blhp,blhn->bhpn", tail, x_c, b_c))
        return hs, y

    hT, ys = lax.scan(step, h0, (to_chunks(xdt), to_chunks(la), to_chunks(Bm), to_chunks(Cm)))
    return from_chunks(ys), hT


def ssm_mixer(xbc, z, dt_raw, lp, h0):
    f32 = jnp.float32
    bsz, n_tok, _ = xbc.shape
    H = N_HEADS_G
    xbc = xbc.astype(f32)
    x = heads(xbc[..., :GROUP_W], H)
    Bm = xbc[..., GROUP_W:GROUP_W + SSM_G * SSM_N].reshape(bsz, n_tok, SSM_G, SSM_N)
    Cm = xbc[..., GROUP_W + SSM_G * SSM_N:].reshape(bsz, n_tok, SSM_G, SSM_N)
    Bm = jnp.repeat(Bm, H // SSM_G, axis=2)
    Cm = jnp.repeat(Cm, H // SSM_G, axis=2)
    dt_raw = dt_raw.astype(f32).reshape(bsz, n_tok, N_DIR, H)
    h0 = h0.astype(f32)
    ys, fins = [], []
    for d in range(N_DIR):
        dt = jax.nn.softplus(dt_raw[:, :, d] + lp["ssm_dt_bias"][d].astype(f32))
        la = -dt * jnp.exp(lp["ssm_A_log"][d].astype(f32))
        y, fin = run_dir(ssd_scan, [x * dt[..., None], la, Bm, Cm], (h0[:, d],), d == 1)
        ys.append(y)
        fins.append(fin)
    y = ys[0] + ys[1] + lp["ssm_D"].astype(f32)[:, None] * x
    y = y.reshape(bsz, n_tok, GROUP_W) * jax.nn.silu(z.astype(f32))
    return rmsnorm(y, lp["ssm_norm"]), jnp.stack(fins, axis=1)


def mlstm_scan(q, k, v, logi, logf, C0, n0, m0):
    mask = jnp.tril(jnp.ones((CHUNK, CHUNK), dtype=bool))[None, :, :, None]

    def step(carry, inp):
        C, n, m = carry
        q_c, k_c, v_c, i_c, f_c = inp
        b = jnp.cumsum(f_c, axis=1)
        dmat = b[:, :, None, :] - b[:, None, :, :] + i_c[:, None, :, :]
        dmat = jnp.where(mask, dmat, -jnp.inf)
        inter = b + m[:, None, :]
        m_t = jnp.maximum(inter, jnp.max(dmat, axis=2))
        wts = jnp.exp(dmat - m_t[:, :, None, :])
        s_inter = jnp.exp(inter - m_t)
        qk = jnp.einsum("blhd,bshd->blsh", q_c, k_c) * wts
        num = (jnp.einsum("blsh,bshd->blhd", qk, v_c)
               + s_inter[..., None] * jnp.einsum("bhvk,blhk->blhv", C, q_c))
        den = jnp.sum(qk, axis=2) + s_inter * jnp.einsum("bhk,blhk->blh", n, q_c)
        h_out = num / jnp.maximum(jnp.abs(den), jnp.exp(-m_t))[..., None]
        m_end = m_t[:, -1]
        w_end = jnp.exp(b[:, -1:, :] - b + i_c - m_end[:, None, :])
        s_end = jnp.exp(b[:, -1] + m - m_end)
        C = s_end[..., None, None] * C + jnp.einsum("blh,blhv,blhk->bhvk", w_end, v_c, k_c)
        n = s_end[..., None] * n + jnp.einsum("blh,blhk->bhk", w_end, k_c)
        return (C, n, m_end), h_out

    fin, ys = lax.scan(step, (C0, n0, m0),
                       tuple(to_chunks(t) for t in (q, k, v, logi, logf)))
    return from_chunks(ys), fin


def mlstm_mixer(q, k, v, o, i_raw, f_raw, lp, C0, n0, m0):
    f32 = jnp.float32
    bsz, n_tok, _ = q.shape
    H = N_HEADS_G
    q = heads(q, H).astype(f32)
    k = heads(k, H).astype(f32) * (HEAD_DIM ** -0.5)
    v = heads(v, H).astype(f32)
    i_raw = i_raw.astype(f32).reshape(bsz, n_tok, N_DIR, H)
    f_raw = f_raw.astype(f32).reshape(bsz, n_tok, N_DIR, H)
    ys, fins = [], []
    for d in range(N_DIR):
        logi = i_raw[:, :, d] + lp["ml_ib"][d].astype(f32)
        logf = jax.nn.log_sigmoid(f_raw[:, :, d] + lp["ml_fb"][d].astype(f32))
        y, fin = run_dir(mlstm_scan, [q, k, v, logi, logf],
                         (C0[:, d].astype(f32), n0[:, d].astype(f32), m0[:, d].astype(f32)), d == 1)
        ys.append(y)
        fins.append(fin)
    y = head_rmsnorm(ys[0] + ys[1], lp["ml_norm"]).reshape(bsz, n_tok, GROUP_W)
    y = y * jax.nn.sigmoid(o.astype(f32))
    C = jnp.stack([fins[0][0], fins[1][0]], axis=1)
    n = jnp.stack([fins[0][1], fins[1][1]], axis=1)
    m = jnp.stack([fins[0][2], fins[1][2]], axis=1)
    return y, (C, n, m)


def lru_scan(a, xin, h0):
    def combine(e1, e2):
        a1, b1 = e1
        a2, b2 = e2
        return a1 * a2, a2 * b1 + b2
    a_cum, b_cum = lax.associative_scan(combine, (a, xin), axis=1)
    h = a_cum * h0[:, None, :] + b_cum
    return h, h[:, -1]


def rglru_mixer(x, gate, lp, h0):
    f32 = jnp.float32
    bsz, n_tok, _ = x.shape
    xb = heads(x, N_HEADS_G).astype(f32)
    h0 = h0.astype(f32)
    ys, fins = [], []
    for d in range(N_DIR):
        r = jax.nn.sigmoid(jnp.einsum("bthi,hij->bthj", xb, lp["lru_wr"][d].astype(f32))
                           + lp["lru_br"][d].astype(f32))
        i = jax.nn.sigmoid(jnp.einsum("bthi,hij->bthj", xb, lp["lru_wi"][d].astype(f32))
                           + lp["lru_bi"][d].astype(f32))
        log_a = -LRU_C * r * jax.nn.softplus(-lp["lru_lam"][d].astype(f32)).reshape(N_HEADS_G, HEAD_DIM)
        xin = jnp.sqrt(-jnp.expm1(2.0 * log_a)) * i * xb
        y, fin = run_dir(lru_scan, [jnp.exp(log_a).reshape(bsz, n_tok, GROUP_W),
                                    xin.reshape(bsz, n_tok, GROUP_W)], (h0[:, d],), d == 1)
        ys.append(y)
        fins.append(fin)
    out = (ys[0] + ys[1]) * jax.nn.gelu(gate.astype(f32), approximate=False)
    return out, jnp.stack(fins, axis=1)


def token_mixing(h, lp, st, latent, col_major):
    S_rw, h_ssm, C_ml, n_ml, m_ml, h_lru = st
    u = h @ lp["w_in"]
    (rw_r, rw_k, rw_v, rw_g, ss_x, ss_bc, ss_z, ss_dt,
     ml_q, ml_k, ml_v, ml_o, ml_i, ml_f, lr_x, lr_gate) = split_cols(u)
    conv = conv_grid if latent else conv_seq
    ss_xbc = jax.nn.silu(conv(jnp.concatenate([ss_x, ss_bc], axis=-1), lp["ssm_conv_w"], lp["ssm_conv_b"]))
    lr_x = conv(lr_x, lp["lru_conv_w"], lp["lru_conv_b"])
    parts = [h, rw_r, rw_k, rw_v, rw_g, ss_xbc, ss_z, ss_dt, ml_q, ml_k, ml_v, ml_o, ml_i, ml_f, lr_x, lr_gate]
    if col_major:
        parts = [to_col_major(t) for t in parts]
    (h, rw_r, rw_k, rw_v, rw_g, ss_xbc, ss_z, ss_dt,
     ml_q, ml_k, ml_v, ml_o, ml_i, ml_f, lr_x, lr_gate) = parts
    o_rw, S_rw = rwkv7_mixer(h, rw_r, rw_k, rw_v, rw_g, lp, S_rw)
    o_ss, h_ssm = ssm_mixer(ss_xbc, ss_z, ss_dt, lp, h_ssm)
    o_ml, (C_ml, n_ml, m_ml) = mlstm_mixer(ml_q, ml_k, ml_v, ml_o, ml_i, ml_f, lp, C_ml, n_ml, m_ml)
    o_lr, h_lru = rglru_mixer(lr_x, lr_gate, lp, h_lru)
    out = jnp.concatenate([o_rw, o_ss, o_ml, o_lr], axis=-1).astype(h.dtype)
    if col_major:
        out = to_row_major(out)
    return out @ lp["w_out"], (S_rw, h_ssm, C_ml, n_ml, m_ml, h_lru)


def peer(h, lp):
    f32 = jnp.float32
    bsz, n_tok, dm = h.shape
    nt = bsz * n_tok
    xf = h.reshape(nt, dm)
    q = (xf @ lp["peer_wq"]).astype(f32).reshape(nt, PEER_HEADS, 2, PEER_DQ // 2)
    s1 = jnp.einsum("thd,kd->thk", q[:, :, 0], lp["peer_k1"].astype(f32))
    s2 = jnp.einsum("thd,kd->thk", q[:, :, 1], lp["peer_k2"].astype(f32))
    v1, i1 = lax.top_k(s1, PEER_TOPK)
    v2, i2 = lax.top_k(s2, PEER_TOPK)
    cand = (v1[..., :, None] + v2[..., None, :]).reshape(nt, PEER_HEADS, PEER_TOPK * PEER_TOPK)
    cidx = (i1[..., :, None] * N_KEYS + i2[..., None, :]).reshape(nt, PEER_HEADS, PEER_TOPK * PEER_TOPK)
    sc, pos = lax.top_k(cand, PEER_TOPK)
    eidx = jnp.take_along_axis(cidx, pos, axis=-1)
    gate = jax.nn.softmax(sc, axis=-1)
    nb = nt // PEER_BLOCK

    def expert_block(args):
        xb, eb, gb = args
        ub = jnp.take(lp["peer_u"], eb, axis=0)
        vb = jnp.take(lp["peer_v"], eb, axis=0)
        act = jax.nn.gelu(jnp.einsum("td,thkd->thk", xb, ub).astype(f32), approximate=False)
        return jnp.einsum("thk,thkd->td", (gb * act).astype(vb.dtype), vb)

    out = lax.map(expert_block, (xf.reshape(nb, PEER_BLOCK, dm),
                                 eidx.reshape(nb, PEER_BLOCK, PEER_HEADS, PEER_TOPK),
                                 gate.reshape(nb, PEER_BLOCK, PEER_HEADS, PEER_TOPK)))
    return out.reshape(bsz, n_tok, dm).astype(h.dtype)


def trunk_layer(x, cvec, lp, st, latent, col_major):
    mod = (jax.nn.silu(cvec) @ lp["mod_w"] + lp["mod_b"])[:, None, :]
    sh1, sc1, g1, sh2, sc2, g2 = jnp.split(mod, 6, axis=-1)
    h = rmsnorm(x, lp["norm1"]) * (1 + sc1) + sh1
    mix, st_new = token_mixing(h, lp, st, latent, col_major)
    x = x + g1 * mix
    h = rmsnorm(x, lp["norm2"]) * (1 + sc2) + sh2
    x = x + g2 * peer(h, lp)
    return x, st_new


def setup_inputs(seed: int = 0) -> dict:
    key = jax.random.key(seed)
    kit = iter(jax.random.split(key, 64))
    f32 = jnp.float32
    D, L, H = D_MODEL, DEPTH, N_HEADS_G

    def nrm(shape, scale):
        return jax.random.normal(next(kit), shape, f32) * scale

    def unif(shape, lo, hi):
        return jax.random.uniform(next(kit), shape, f32, lo, hi)

    def gain(shape):
        return 1.0 + nrm(shape, 0.02)

    inp = {}
    inp["x_prompt"] = nrm((BATCH, SEQ, D), 1.0)
    inp["x_sample"] = nrm((DEC_BATCH, DEC_SEQ, D), 1.0)
    inp["c"] = nrm((DEC_BATCH, D), 1.0)
    inp["state_rwkv"] = nrm((DEC_BATCH, L, N_DIR, H, HEAD_DIM, HEAD_DIM), 0.5)
    inp["state_ssm"] = nrm((DEC_BATCH, L, N_DIR, H, HEAD_DIM, SSM_N), 0.1)
    inp["state_mlstm_C"] = nrm((DEC_BATCH, L, N_DIR, H, HEAD_DIM, HEAD_DIM), 0.1)
    inp["state_mlstm_n"] = nrm((DEC_BATCH, L, N_DIR, H, HEAD_DIM), 0.1)
    inp["state_mlstm_m"] = nrm((DEC_BATCH, L, N_DIR, H), 1.0)
    inp["state_lru"] = nrm((DEC_BATCH, L, N_DIR, GROUP_W), 0.5)
    inp["c_ctx"] = nrm((D,), 1.0)
    inp["mod_w"] = nrm((L, D, 6 * D), 0.5 * D ** -0.5)
    inp["mod_b"] = nrm((L, 6 * D), 0.01)
    inp["norm1"] = gain((L, D))
    inp["norm2"] = gain((L, D))
    inp["w_in"] = nrm((L, D, D_IN), D ** -0.5)
    inp["w_out"] = nrm((L, MIX_W, D), MIX_W ** -0.5)
    inp["rw_w0"] = unif((L, N_DIR, GROUP_W), -6.0, -1.0)
    inp["rw_wA"] = nrm((L, N_DIR, D, RW_LORA), D ** -0.5)
    inp["rw_wB"] = nrm((L, N_DIR, RW_LORA, GROUP_W), 0.1 * RW_LORA ** -0.5)
    inp["rw_a0"] = nrm((L, N_DIR, GROUP_W), 0.1)
    inp["rw_aA"] = nrm((L, N_DIR, D, RW_LORA), D ** -0.5)
    inp["rw_aB"] = nrm((L, N_DIR, RW_LORA, GROUP_W), 0.1 * RW_LORA ** -0.5)
    inp["rw_kk"] = gain((L, GROUP_W))
    inp["rw_ka"] = gain((L, GROUP_W))
    inp["rw_rk"] = nrm((L, H, HEAD_DIM), 0.1)
    inp["rw_ln"] = gain((L, GROUP_W))
    inp["ssm_conv_w"] = nrm((L, CONV_K, CONV_K, SSM_CONV_CH), 1.0 / CONV_K)
    inp["ssm_conv_b"] = nrm((L, SSM_CONV_CH), 0.01)
    dt0 = jnp.exp(unif((L, N_DIR, H), math.log(1e-3), math.log(1e-1)))
    inp["ssm_dt_bias"] = dt0 + jnp.log(-jnp.expm1(-dt0))
    inp["ssm_A_log"] = jnp.log(unif((L, N_DIR, H), 1.0, 16.0))
    inp["ssm_D"] = gain((L, H))
    inp["ssm_norm"] = gain((L, GROUP_W))
    inp["ml_ib"] = nrm((L, N_DIR, H), 0.1)
    inp["ml_fb"] = unif((L, N_DIR, H), 3.0, 6.0)
    inp["ml_norm"] = gain((L, GROUP_W))
    inp["lru_conv_w"] = nrm((L, CONV_K, CONV_K, GROUP_W), 1.0 / CONV_K)
    inp["lru_conv_b"] = nrm((L, GROUP_W), 0.01)
    inp["lru_wr"] = nrm((L, N_DIR, H, HEAD_DIM, HEAD_DIM), HEAD_DIM ** -0.5)
    inp["lru_br"] = nrm((L, N_DIR, H, HEAD_DIM), 0.01)
    inp["lru_wi"] = nrm((L, N_DIR, H, HEAD_DIM, HEAD_DIM), HEAD_DIM ** -0.5)
    inp["lru_bi"] = nrm((L, N_DIR, H, HEAD_DIM), 0.01)
    a0 = unif((L, N_DIR, GROUP_W), 0.9, 0.999) ** (1.0 / LRU_C)
    inp["lru_lam"] = jnp.log(a0) - jnp.log1p(-a0)
    inp["peer_wq"] = nrm((L, D, PEER_HEADS * PEER_DQ), D ** -0.5)
    inp["peer_k1"] = nrm((L, N_KEYS, PEER_DQ // 2), (PEER_DQ // 2) ** -0.5)
    inp["peer_k2"] = nrm((L, N_KEYS, PEER_DQ // 2), (PEER_DQ // 2) ** -0.5)
    inp["peer_u"] = nrm((L, N_EXPERTS, D), D ** -0.5)
    inp["peer_v"] = nrm((L, N_EXPERTS, D), 0.25)
    inp["final_norm"] = gain((D,))
    return inp


def reference(x_prompt, x_sample, c, state_rwkv, state_ssm, state_mlstm_C, state_mlstm_n, state_mlstm_m,
              state_lru, c_ctx, mod_w, mod_b, norm1, norm2, w_in, w_out, rw_w0, rw_wA, rw_wB, rw_a0, rw_aA,
              rw_aB, rw_kk, rw_ka, rw_rk, rw_ln, ssm_conv_w, ssm_conv_b, ssm_dt_bias, ssm_A_log, ssm_D,
              ssm_norm, ml_ib, ml_fb, ml_norm, lru_conv_w, lru_conv_b, lru_wr, lru_br, lru_wi, lru_bi,
              lru_lam, peer_wq, peer_k1, peer_k2, peer_u, peer_v, final_norm):
    f32 = jnp.float32
    H = N_HEADS_G
    bp = x_prompt.shape[0]
    ctx_init = (jnp.zeros((bp, N_DIR, H, HEAD_DIM, HEAD_DIM), f32),
                jnp.zeros((bp, N_DIR, H, HEAD_DIM, SSM_N), f32),
                jnp.zeros((bp, N_DIR, H, HEAD_DIM, HEAD_DIM), f32),
                jnp.zeros((bp, N_DIR, H, HEAD_DIM), f32),
                jnp.zeros((bp, N_DIR, H), f32),
                jnp.zeros((bp, N_DIR, GROUP_W), f32))
    xp, xs = x_prompt, x_sample
    new = [[] for _ in range(6)]
    for l in range(DEPTH):
        lp = dict(mod_w=mod_w[l], mod_b=mod_b[l], norm1=norm1[l], norm2=norm2[l], w_in=w_in[l],
                  w_out=w_out[l], rw_w0=rw_w0[l], rw_wA=rw_wA[l], rw_wB=rw_wB[l], rw_a0=rw_a0[l],
                  rw_aA=rw_aA[l], rw_aB=rw_aB[l], rw_kk=rw_kk[l], rw_ka=rw_ka[l], rw_rk=rw_rk[l],
                  rw_ln=rw_ln[l], ssm_conv_w=ssm_conv_w[l], ssm_conv_b=ssm_conv_b[l],
                  ssm_dt_bias=ssm_dt_bias[l], ssm_A_log=ssm_A_log[l], ssm_D=ssm_D[l], ssm_norm=ssm_norm[l],
                  ml_ib=ml_ib[l], ml_fb=ml_fb[l], ml_norm=ml_norm[l], lru_conv_w=lru_conv_w[l],
                  lru_conv_b=lru_conv_b[l], lru_wr=lru_wr[l], lru_br=lru_br[l], lru_wi=lru_wi[l],
                  lru_bi=lru_bi[l], lru_lam=lru_lam[l], peer_wq=peer_wq[l], peer_k1=peer_k1[l],
                  peer_k2=peer_k2[l], peer_u=peer_u[l], peer_v=peer_v[l])
        xp, st = trunk_layer(xp, c_ctx[None, :], lp, ctx_init, False, False)
        for j in range(6):
            new[j].append(st[j].astype(x_prompt.dtype))
        cached = (state_rwkv[:, l], state_ssm[:, l], state_mlstm_C[:, l], state_mlstm_n[:, l],
                  state_mlstm_m[:, l], state_lru[:, l])
        xs, _ = trunk_layer(xs, c, lp, cached, True, l % 2 == 1)
    y_prompt = rmsnorm(xp, final_norm)
    y_sample = rmsnorm(xs, final_norm)
    new_state_rwkv = jnp.stack(new[0], axis=1)
    new_state_ssm = jnp.stack(new[1], axis=1)
    new_state_mlstm_C = jnp.stack(new[2], axis=1)
    new_state_mlstm_n = jnp.stack(new[3], axis=1)
    new_state_mlstm_m = jnp.stack(new[4], axis=1)
    new_state_lru = jnp.stack(new[5], axis=1)
    return (y_prompt, y_sample, new_state_rwkv, new_state_ssm, new_state_mlstm_C, new_state_mlstm_n,
            new_state_mlstm_m, new_state_lru)
```

```python
import contextlib
import math
import numpy as np
import concourse.bass as bass
import concourse.mybir as mybir
from concourse.bass_utils import run_bass_kernel_spmd

F32 = mybir.dt.float32; I32 = mybir.dt.int32; U32 = mybir.dt.uint32
AF = mybir.ActivationFunctionType; ALU = mybir.AluOpType; AX = mybir.AxisListType

D = 1024; L_ = 2; NP = 512; NS = 4096; NT = NP + NS; NTILE = NT // 128
DIN = 3352; DEXT = 3608; UW = 4376
C_RWR, C_RWK, C_RWV, C_RWG = 0, 256, 512, 768
C_SSX, C_SSBC, C_SSZ, C_SSDT = 1024, 1280, 1536, 1792
C_MLQ, C_MLK, C_MLV, C_MLO, C_MLI, C_MLF = 1800, 2056, 2312, 2568, 2824, 2832
C_LRX, C_LRG = 2840, 3096
C_DEC, C_A = 3352, 3864
EPS = 1e-6
CV_P0, CV_P1, CV_S = 1, 258, 640
CV_ROWS = 4864


class Res:
    __slots__ = ("w", "r", "excl")
    def __init__(self):
        self.w = None; self.r = {}; self.excl = False


class K:
    def __init__(self, nc, ndma=24):
        self.nc = nc
        self.eng = {"pe": nc.tensor, "act": nc.scalar, "dve": nc.vector, "pool": nc.gpsimd, "sp": nc.sync}
        self.sem = {e: nc.alloc_semaphore(name="c_" + e) for e in self.eng}
        self.cnt = {e: 0 for e in self.eng}
        self.seen = {e: {} for e in self.eng}
        self.rings = {}
        for q in ("sp", "pool"):
            n = ndma if q == "sp" else 8
            self.rings[q] = dict(sems=[nc.alloc_semaphore(name=f"d_{q}{i}") for i in range(n)], uses=[0] * n, nxt=0)
        self.ninst = 0
    def _wait(self, e, ev):
        if ev is None: return
        s, v = ev
        if self.seen[e].get(s.num, 0) >= v: return
        self.seen[e][s.num] = v
        self.eng[e].wait_ge(s, v); self.ninst += 1
    def _deps(self, e, reads, writes):
        for r in reads: self._wait(e, r.w)
        for w in writes:
            self._wait(e, w.w)
            for ev in w.r.values(): self._wait(e, ev)
    def _commit(self, ev, reads, writes):
        for r in reads: r.r[ev[0].num] = ev
        for w in writes:
            w.w = ev; w.r = {}
    def op(self, e, fn, reads=(), writes=()):
        ex = [r for r in reads if r.excl]
        if ex:
            reads = [r for r in reads if not r.excl]; writes = list(writes) + ex
        self._deps(e, reads, writes)
        inst = fn(self.eng[e])
        self.cnt[e] += 1; self.ninst += 1
        inst.then_inc(self.sem[e], 1)
        ev = (self.sem[e], self.cnt[e])
        self._commit(ev, reads, writes)
        return ev
    def dma(self, out, in_, reads=(), writes=(), q="sp", fn=None):
        self._deps(q, reads, writes)
        ring = self.rings[q]; i = ring["nxt"]; ring["nxt"] = (i + 1) % len(ring["sems"])
        s = ring["sems"][i]
        if ring["uses"][i] > 0: self._wait(q, (s, 16 * ring["uses"][i]))
        inst = self.eng[q].dma_start(out=out, in_=in_) if fn is None else fn(self.eng[q])
        inst.then_inc(s, 16); ring["uses"][i] += 1; self.ninst += 1
        ev = (s, 16 * ring["uses"][i])
        self._commit(ev, reads, writes)
        return ev
    def barrier(self):
        evs = [(self.sem[f], self.cnt[f]) for f in self.eng if self.cnt[f]]
        for q, ring in self.rings.items():
            evs += [(s_, 16 * u) for s_, u in zip(ring["sems"], ring["uses"]) if u]
        for e in self.eng:
            for ev in evs:
                if ev[0] is not self.sem[e]: self._wait(e, ev)
    def finish(self):
        for q, ring in self.rings.items():
            for s, u in zip(ring["sems"], ring["uses"]):
                if u: self._wait("sp", (s, 16 * u))
        for e in self.eng:
            if self.cnt[e]: self._wait("sp", (self.sem[e], self.cnt[e]))


class T:
    def __init__(self, ap):
        self.ap = ap; self.res = Res()
    def __getitem__(self, idx):
        return self.ap[idx]


class Builder:
    def __init__(self, debug=()):
        self.nc = nc = bass.Bass("TRN2", target_bir_lowering=False)
        self.k = K(nc)
        self.debug = debug
        self.ins = {}; self.outs = {}
        self.ps = [T(nc.alloc_psum_tensor(f"ps{i}", [128, 512], F32).ap()) for i in range(8)]
        for p in self.ps: p.res.excl = True
        self.stack = contextlib.ExitStack()
    def inp(self, name, shape, dt=F32):
        t = T(self.nc.dram_tensor(name, list(shape), dt, kind="ExternalInput").ap()); self.ins[name] = t; return t
    def out(self, name, shape, dt=F32):
        t = T(self.nc.dram_tensor(name, list(shape), dt, kind="ExternalOutput").ap()); self.outs[name] = t; return t
    def scratch(self, name, shape, dt=F32):
        if name in self.debug: return self.out(name, shape, dt)
        return T(self.nc.dram_tensor(name, list(shape), dt, kind="Internal").ap())
    def sb(self, st, name, shape, dt=F32):
        self._n = getattr(self, "_n", 0) + 1
        name = f"{name}_{self._n}"
        return T(st.enter_context(self.nc.sbuf_tensor(name, list(shape), dt)).ap())


def build(debug=(), stop_after=None, mixers=("lru", "ssd", "mlstm", "rwkv")):
    B = Builder(debug); nc = B.nc; k = B.k; ps = B.ps
    xin = B.inp("xin", [NT, D])
    cvecs = B.inp("cvecs", [128, 2, 8])
    ident_d = B.inp("ident", [128, 128])
    mod_w = B.inp("mod_w", [L_, D, 6 * D]); mod_b = B.inp("mod_b_rep", [L_, 128, 6 * D])
    norm_rep = B.inp("norm_rep", [L_, 128, 2, D])
    w_ext = B.inp("w_ext", [L_, D, DEXT])
    wB_d = B.inp("wB_sb", [L_, 128, 2, 256]); w0a0_d = B.inp("w0a0_rep", [L_, 128, 1024])
    cvw_d = B.inp("cvw_rep", [L_, 128, 9, 768]); cvb_d = B.inp("cvb_rep", [L_, 128, 768])
    cmask_d = B.inp("cmask", [128, 2])
    X = B.scratch("X", [NT, D]); U = B.scratch("U", [NT, UW]); CVI = B.scratch("CVI", [CV_ROWS, 768])
    CVO = B.scratch("CVO", [NT, 768])
    yp = B.out("y_tok", [NT, D])
    MIX = B.scratch("MIX", [NT, D]); YF = B.scratch("YF", [NT, 512]); YB = B.scratch("YB", [NT, 512])
    o_rw = B.out("st_rwkv", [2, L_, 2, 4, 64, 64]); o_ss = B.out("st_ssm", [2, L_, 2, 4, 64, 64]); o_mc = B.out("st_mlC", [2, L_, 2, 4, 64, 64])
    o_mn = B.out("st_mln", [2, L_, 2, 4, 64]); o_mm = B.out("st_mlm", [2, L_, 2, 4]); o_lr = B.out("st_lru", [2, L_, 2, 256])
    s_rw = B.inp("s_rw", [L_, 2, 4, 64, 64]); s_ss = B.inp("s_ss", [L_, 2, 4, 64, 64]); s_mc = B.inp("s_mc", [L_, 2, 4, 64, 64])
    s_mn = B.inp("s_mn", [L_, 2, 4, 64]); s_mm = B.inp("s_mm", [L_, 2, 4]); s_lr = B.inp("s_lr", [L_, 2, 256])
    tri_d = B.inp("tri", [2, 64, 64]); jmat_d = B.inp("jmat", [128, 128])
    lruw_d = B.inp("lruw", [L_, 2, 2, 2, 128, 128]); lrub_d = B.inp("lrub", [L_, 128, 2, 2, 3])
    rwp_d = B.inp("rwp_rep", [L_, 128, 4, 256]); jm_d = jmat_d
    rwm_d = B.inp("rwm", [2, 128, 128]); rws_d = B.inp("rws", [2, 128, 64]); tri2_d = B.inp("tri2", [2, 128, 128])
    wout_d = B.inp("w_out", [L_, D, D]); wq_d = B.inp("peer_wq", [L_, D, D]); pk_d = B.inp("peer_kT", [L_, 2, 64, 128])
    puv_d = [B.inp(f"peer_uv{i}", [16384, 2 * D]).ap for i in range(L_)]; fin_d = B.inp("fin_rep", [128, D]); iota_d = B.inp("iota128", [128, 128]); iota256_d = B.inp("iota256", [128, 256])
    RQ = B.scratch("RQ", [NT, 1536]); YR = [B.scratch(f"YR{d}", [NT, 256]) for d in range(2)]
    mlp_d = B.inp("mlp_rep", [L_, 128, 16]); mln_d = B.inp("mln_rep", [L_, 128, 256]); smm_d = B.inp("s_mm_rep", [L_, 2, 64, 4])
    ssp_d = B.inp("ssp_rep", [L_, 128, 16]); ssD_d = B.inp("ssD_rep", [L_, 128, 4]); ssn_d = B.inp("ssn_rep", [L_, 128, 256])


    def _r(ts): return [t.res for t in ts]
    def TT(o, a, b, op, rd, wr, eng="dve"): k.op(eng, lambda e: e.tensor_tensor(out=o, in0=a, in1=b, op=op), reads=_r(rd), writes=_r(wr))
    def TS(o, a, s1, s2, op0, op1, rd, wr): k.op("dve", lambda e: e.tensor_scalar(out=o, in0=a, scalar1=s1, scalar2=s2, op0=op0, op1=op1), reads=_r(rd), writes=_r(wr))
    def STT(o, a, sc, b, op0, op1, rd, wr): k.op("dve", lambda e: e.scalar_tensor_tensor(out=o, in0=a, scalar=sc, in1=b, op0=op0, op1=op1), reads=_r(rd), writes=_r(wr))
    def ACT(o, a, func, rd, wr, **kw): k.op("act", lambda e: e.activation(out=o, in_=a, func=func, **kw), reads=_r(rd), writes=_r(wr))
    def MM(o, lhsT, rhs, rd, wr, start=True, stop=True): k.op("pe", lambda e: e.matmul(o, lhsT=lhsT, rhs=rhs, start=start, stop=stop), reads=_r(rd), writes=_r(wr))
    def CP(o, a, rd, wr, eng="dve"):
        if eng == "act": k.op("act", lambda e: e.copy(out=o, in_=a), reads=_r(rd), writes=_r(wr))
        else: k.op(eng, lambda e: e.tensor_copy(out=o, in_=a), reads=_r(rd), writes=_r(wr))
    def DMA(o, a, rd, wr): k.dma(o, a, reads=_r(rd), writes=_r(wr))
    def rows(A, base, t0, n, cm, c0, c1):
        if not cm: return [(0, n, A[base + t0:base + t0 + n, c0:c1])]
        out = []
        for i in range(n // 64):
            c = t0 // 64 + i
            out.append((64 * i, 64, A[base + c:base + c + 63 * 64 + 1:64, c0:c1]))
        return out
    def LOADR(t, tap, A, base, t0, n, cm, c0, c1):
        for (p0, pn, ap) in rows(A, base, t0, n, cm, c0, c1): DMA(tap(p0, pn), ap, [A], [t])
    def STORER(A, base, t0, n, cm, c0, c1, t, tap):
        for (p0, pn, ap) in rows(A, base, t0, n, cm, c0, c1): DMA(ap, tap(p0, pn), [t], [A])
    SEQS = [(0, 256, False, 0), (256, 256, False, 1), (512, 4096, True, None)]

    with contextlib.ExitStack() as g:
        ident = B.sb(g, "identsb", [128, 128]); k.dma(ident.ap, ident_d.ap, writes=[ident.res])
        def TR(o, a, n, rd, wr): k.op("pe", lambda e: e.transpose(o, a, ident[0:n, 0:n]), reads=_r(rd) + [ident.res], writes=_r(wr))
        TRI = B.sb(g, "TRI", [64, 2, 64])
        for d in range(2): k.dma(TRI[:, d, :], tri_d[d], writes=[TRI.res])
        MOD = [B.sb(g, f"MOD{s}", [128, 6, D]) for s in range(2)]
        with contextlib.ExitStack() as zs:
            zt = B.sb(zs, "zt", [128, 768]); k.op("dve", lambda e: e.memset(zt.ap, 0.0), writes=[zt.res])
            for r0 in range(0, CV_ROWS, 128):
                k.dma(CVI[r0:r0 + 128, :], zt.ap, reads=[zt.res], writes=[CVI.res])
        k.barrier()

        for l in range(L_):
            Xsrc = xin if l == 0 else X
            with contextlib.ExitStack() as st:
                cv = B.sb(st, "cv", [128, 2, 8]); k.dma(cv.ap, cvecs.ap, writes=[cv.res])
                k.op("act", lambda e: e.activation(out=cv.ap, in_=cv.ap, func=AF.Silu), reads=[cv.res], writes=[cv.res])
                wb = [B.sb(st, f"modwb{i}", [128, 8, 512]) for i in range(2)]
                mb = B.sb(st, "modb", [128, 6 * D]); k.dma(mb.ap, mod_b[l], writes=[mb.res])
                nr = B.sb(st, "nr", [128, 2, D]); k.dma(nr.ap, norm_rep[l], writes=[nr.res])
                mw = mod_w[l].rearrange("(j p) n -> p j n", p=128)
                for nb in range(12):
                    w = wb[nb % 2]
                    k.dma(w.ap, mw[:, :, nb * 512:(nb + 1) * 512], writes=[w.res])
                    for s in range(2):
                        p = ps[(nb * 2 + s) % 4]
                        for j in range(8):
                            k.op("pe", lambda e, j=j, s=s, p=p, w=w: e.matmul(p.ap, lhsT=cv[:, s, j:j + 1].to_broadcast([128, 128]),
                                 rhs=w[:, j, :], start=(j == 0), stop=(j == 7)), reads=[cv.res, w.res], writes=[p.res])
                        sec, off = divmod(nb * 512, D)
                        k.op("dve", lambda e, s=s, p=p, sec=sec, off=off, nb=nb: e.tensor_tensor(out=MOD[s][:, sec, off:off + 512], in0=p.ap,
                             in1=mb[:, nb * 512:(nb + 1) * 512], op=ALU.add), reads=[p.res, mb.res], writes=[MOD[s].res])
                for s in range(2):
                    for sec, ni in ((1, 0), (4, 1)):
                        k.op("dve", lambda e, s=s, sec=sec, ni=ni: e.scalar_tensor_tensor(out=MOD[s][:, sec, :], in0=MOD[s][:, sec, :], scalar=1.0,
                             in1=nr[:, ni, :], op0=ALU.add, op1=ALU.mult), reads=[MOD[s].res, nr.res], writes=[MOD[s].res])
            k.barrier()
            with contextlib.ExitStack() as st:
                xt = B.sb(st, "xt", [128, D]); ht = B.sb(st, "ht", [128, D]); hT = B.sb(st, "hT", [128, 8, 128])
                ut = B.sb(st, "ut", [128, UW]); sq = B.sb(st, "sq", [128, 2]); junk = B.sb(st, "junk", [128, D])
                wb = [B.sb(st, f"winb{i}", [128, 8, 512]) for i in range(2)]
                LT = B.sb(st, "LT", [128, 2, 128]); wBs = B.sb(st, "wBs", [128, 2, 256]); w0a0 = B.sb(st, "w0a0", [128, 1024])
                k.dma(wBs.ap, wB_d[l], writes=[wBs.res]); k.dma(w0a0.ap, w0a0_d[l], writes=[w0a0.res])
                wx = w_ext[l].rearrange("(j p) n -> p j n", p=128)
                nwb = 0
                for ti in range(NTILE):
                    s = 0 if ti < NP // 128 else 1
                    k.dma(xt.ap, Xsrc[ti * 128:(ti + 1) * 128, :], reads=[Xsrc.res], writes=[xt.res])
                    k.op("act", lambda e: e.activation(out=junk.ap, in_=xt.ap, func=AF.Square, accum_out=sq[:, 0:1]), reads=[xt.res], writes=[junk.res, sq.res])
                    k.op("dve", lambda e: e.tensor_scalar(out=sq[:, 1:2], in0=sq[:, 0:1], scalar1=1.0 / D, scalar2=EPS, op0=ALU.mult, op1=ALU.add), reads=[sq.res], writes=[sq.res])
                    k.op("act", lambda e: e.activation(out=sq[:, 1:2], in_=sq[:, 1:2], func=AF.Sqrt), reads=[sq.res], writes=[sq.res])
                    k.op("dve", lambda e: e.reciprocal(out=sq[:, 1:2], in_=sq[:, 1:2]), reads=[sq.res], writes=[sq.res])
                    k.op("dve", lambda e, s=s: e.scalar_tensor_tensor(out=ht.ap, in0=xt.ap, scalar=sq[:, 1:2], in1=MOD[s][:, 1, :], op0=ALU.mult, op1=ALU.mult),
                         reads=[xt.res, sq.res, MOD[s].res], writes=[ht.res])
                    k.op("dve", lambda e, s=s: e.tensor_tensor(out=ht.ap, in0=ht.ap, in1=MOD[s][:, 0, :], op=ALU.add), reads=[ht.res, MOD[s].res], writes=[ht.res])
                    if "HDBG" in debug and ti == 0 and l == 0:
                        hd = B.out("HDBG", [128, D + 2])
                        k.dma(hd[:, 0:D], ht.ap, reads=[ht.res], writes=[hd.res]); k.dma(hd[:, D:D + 2], sq.ap, reads=[sq.res], writes=[hd.res])
                        md = B.out("MDBG", [128, 6, D]); k.dma(md.ap, MOD[0].ap, reads=[MOD[0].res], writes=[md.res])
                    for j in range(8):
                        p = ps[j // 4]
                        k.op("pe", lambda e, j=j, p=p: e.transpose(p[:, (j % 4) * 128:(j % 4 + 1) * 128], ht[:, j * 128:(j + 1) * 128], ident.ap),
                             reads=[ht.res, ident.res], writes=[p.res])
                    for hh in range(2):
                        k.op("act", lambda e, hh=hh: e.copy(out=hT[:, hh * 4:(hh + 1) * 4, :], in_=ps[hh].ap.rearrange("p (a b) -> p a b", a=4)),
                             reads=[ps[hh].res], writes=[hT.res])
                    for nb in range(8):
                        c0 = nb * 512; cw = min(512, DEXT - c0)
                        w = wb[nwb % 2]; p = ps[2 + nwb % 2]; nwb += 1
                        k.dma(w[:, :, 0:cw], wx[:, :, c0:c0 + cw], writes=[w.res])
                        for j in range(8):
                            k.op("pe", lambda e, j=j, p=p, w=w, cw=cw: e.matmul(p[:, 0:cw], lhsT=hT[:, j, :], rhs=w[:, j, 0:cw], start=(j == 0), stop=(j == 7)),
                                 reads=[hT.res, w.res], writes=[p.res])
                        if c0 + cw <= DIN or c0 >= DIN:
                            segs = [(c0, cw, 0)]
                        else:
                            segs = [(c0, DIN - c0, 0), (DIN, c0 + cw - DIN, DIN - c0)]
                        for (a0, aw, po) in segs:
                            eng = "dve" if nb % 2 == 0 else "act"
                            if eng == "dve":
                                k.op("dve", lambda e, a0=a0, aw=aw, po=po, p=p: e.tensor_copy(out=ut[:, a0:a0 + aw], in_=p[:, po:po + aw]), reads=[p.res], writes=[ut.res])
                            else:
                                k.op("act", lambda e, a0=a0, aw=aw, po=po, p=p: e.copy(out=ut[:, a0:a0 + aw], in_=p[:, po:po + aw]), reads=[p.res], writes=[ut.res])
                    k.op("act", lambda e: e.activation(out=ut[:, DIN:DIN + 128], in_=ut[:, DIN:DIN + 128], func=AF.Tanh), reads=[ut.res], writes=[ut.res])
                    for q in range(2):
                        k.op("pe", lambda e, q=q: e.transpose(ps[4][:, q * 128:(q + 1) * 128], ut[:, DIN + q * 128:DIN + (q + 1) * 128], ident.ap),
                             reads=[ut.res, ident.res], writes=[ps[4].res])
                    k.op("dve", lambda e: e.tensor_copy(out=LT.ap, in_=ps[4][:, 0:256].rearrange("p (a b) -> p a b", a=2)), reads=[ps[4].res], writes=[LT.res])
                    for q in range(2):
                        for d in range(2):
                            p = ps[5 + q]
                            k.op("pe", lambda e, q=q, d=d, p=p: e.matmul(p[:, d * 256:(d + 1) * 256], lhsT=LT[64 * d:64 * d + 64, q, :], rhs=wBs[64 * d:64 * d + 64, q, :],
                                 start=True, stop=True), reads=[LT.res, wBs.res], writes=[p.res])
                    for q in range(2):
                        k.op("dve", lambda e, q=q: e.tensor_tensor(out=ut[:, C_DEC + q * 512:C_DEC + (q + 1) * 512], in0=ps[5 + q].ap, in1=w0a0[:, q * 512:(q + 1) * 512], op=ALU.add),
                             reads=[ps[5 + q].res, w0a0.res], writes=[ut.res])
                    k.op("act", lambda e: e.activation(out=ut[:, C_DEC:C_DEC + 1024], in_=ut[:, C_DEC:C_DEC + 1024], func=AF.Sigmoid), reads=[ut.res], writes=[ut.res])
                    k.op("dve", lambda e: e.tensor_scalar(out=ut[:, C_DEC:C_DEC + 512], in0=ut[:, C_DEC:C_DEC + 512], scalar1=-math.exp(-0.5), scalar2=None, op0=ALU.mult), reads=[ut.res], writes=[ut.res])
                    k.dma(U[ti * 128:(ti + 1) * 128, :], ut.ap, reads=[ut.res], writes=[U.res])
                    if s == 0:
                        cr = (CV_P0 if ti < 2 else CV_P1) + (ti % 2) * 128
                    else:
                        cr = CV_S + (ti - 4) * 128
                    k.dma(CVI[cr:cr + 128, 0:512], ut[:, C_SSX:C_SSX + 512], reads=[ut.res], writes=[CVI.res])
                    k.dma(CVI[cr:cr + 128, 512:768], ut[:, C_LRX:C_LRX + 256], reads=[ut.res], writes=[CVI.res])
            k.barrier()
            with contextlib.ExitStack() as st:
                cw9 = B.sb(st, "cw9", [128, 9, 768]); cb = B.sb(st, "cb", [128, 768]); cm = B.sb(st, "cm", [128, 2])
                k.dma(cw9.ap, cvw_d[l], writes=[cw9.res]); k.dma(cb.ap, cvb_d[l], writes=[cb.res]); k.dma(cm.ap, cmask_d.ap, writes=[cm.res])
                sh = [B.sb(st, f"sh{i}", [128, 768]) for i in range(3)]
                acc = B.sb(st, "cacc", [128, 768]); tmp = B.sb(st, "ctmp", [128, 768])
                nsh = 0
                for ti in range(NTILE):
                    s = 0 if ti < 4 else 1
                    if s == 0:
                        cr = (CV_P0 if ti < 2 else CV_P1) + (ti % 2) * 128
                        taps = [(1, 0), (1, 1), (1, 2)]
                    else:
                        cr = CV_S + (ti - 4) * 128
                        taps = [(i, j) for i in range(3) for j in range(3)]
                    first = True
                    for (i, j) in taps:
                        off = (i - 1) * 64 + (j - 1) if s == 1 else (j - 1)
                        t_ = sh[nsh % 3]; nsh += 1
                        k.dma(t_.ap, CVI[cr + off:cr + off + 128, :], reads=[CVI.res], writes=[t_.res])
                        if first:
                            k.op("dve", lambda e, t_=t_, i=i, j=j: e.tensor_tensor(out=acc.ap, in0=t_.ap, in1=cw9[:, i * 3 + j, :], op=ALU.mult), reads=[t_.res, cw9.res], writes=[acc.res])
                            if s == 1 and j != 1:
                                k.op("dve", lambda e, j=j: e.tensor_scalar(out=acc.ap, in0=acc.ap, scalar1=cm[:, j // 2:j // 2 + 1], scalar2=None, op0=ALU.mult), reads=[acc.res, cm.res], writes=[acc.res])
                            first = False
                        else:
                            k.op("dve", lambda e, t_=t_, i=i, j=j: e.tensor_tensor(out=tmp.ap, in0=t_.ap, in1=cw9[:, i * 3 + j, :], op=ALU.mult), reads=[t_.res, cw9.res], writes=[tmp.res])
                            if s == 1 and j != 1:
                                k.op("dve", lambda e, j=j: e.scalar_tensor_tensor(out=acc.ap, in0=tmp.ap, scalar=cm[:, j // 2:j // 2 + 1], in1=acc.ap, op0=ALU.mult, op1=ALU.add),
                                     reads=[tmp.res, cm.res, acc.res], writes=[acc.res])
                            else:
                                k.op("dve", lambda e: e.tensor_tensor(out=acc.ap, in0=tmp.ap, in1=acc.ap, op=ALU.add), reads=[tmp.res, acc.res], writes=[acc.res])
                    k.op("dve", lambda e: e.tensor_tensor(out=acc.ap, in0=acc.ap, in1=cb.ap, op=ALU.add), reads=[acc.res, cb.res], writes=[acc.res])
                    k.op("act", lambda e: e.activation(out=acc[:, 0:512], in_=acc[:, 0:512], func=AF.Silu), reads=[acc.res], writes=[acc.res])
                    k.dma(CVO[ti * 128:(ti + 1) * 128, :], acc.ap, reads=[acc.res], writes=[CVO.res])
            k.barrier()
            if stop_after == ("B", l):
                break

            if "lru" in mixers:
              with contextlib.ExitStack() as st:
                lw = B.sb(st, "lw", [128, 2, 2, 2, 128]); lb = B.sb(st, "lb", [128, 2, 2, 3]); nsp = B.sb(st, "nsp", [128, 2, 2])
                for d in range(2):
                    for q in range(2):
                        for cb in range(2): DMA(lw[:, d, q, cb, :], lruw_d[l, d, q, cb], [], [lw])
                DMA(lb.ap, lrub_d[l], [], [lb])
                ACT(nsp.ap, lb[:, :, :, 2], AF.Exp, [lb], [nsp], scale=-1.0)
                ACT(nsp.ap, nsp.ap, AF.Ln, [nsp], [nsp], bias=1.0)
                TS(nsp.ap, nsp.ap, -8.0, None, ALU.mult, ALU.bypass, [nsp], [nsp])
                xT = B.sb(st, "xT", [128, 4096]); aT = B.sb(st, "aT", [128, 4096]); xiT = B.sb(st, "xiT", [128, 4096])
                hT = [B.sb(st, f"hT{d}", [128, 4096]) for d in range(2)]
                rT = B.sb(st, "rT", [128, 512]); iT = B.sb(st, "iT", [128, 512]); tmq = B.sb(st, "tmq", [128, 512])
                lt = [B.sb(st, f"lt{i}", [128, 128]) for i in range(2)]; gt = [B.sb(st, f"gt{i}", [128, 128]) for i in range(2)]
                ot = [B.sb(st, f"ot{i}", [128, 128]) for i in range(2)]; h0 = B.sb(st, "h0", [128, 2])
                for (base, Tn, samp, sidx) in SEQS:
                    cm = samp and (l % 2 == 1)
                    for cb in range(2):
                        for blk in range(Tn // 128):
                            t_ = lt[blk % 2]; p = ps[blk % 2]
                            LOADR(t_, lambda p0, pn: t_[p0:p0 + pn, :], CVO, base, blk * 128, 128, cm, 512 + cb * 128, 512 + (cb + 1) * 128)
                            TR(p[:, 0:128], t_.ap, 128, [t_], [p])
                            CP(xT[:, blk * 128:(blk + 1) * 128], p[:, 0:128], [p], [xT], eng="act")
                        for d in range(2):
                            if samp:
                                DMA(h0[:, d:d + 1], s_lr[l, d, cb * 128:(cb + 1) * 128].rearrange("(p o) -> p o", o=1), [], [h0])
                            for c0 in range(0, Tn, 512):
                                n = min(512, Tn)
                                MM(ps[2][:, 0:n], lw[:, d, 0, cb, :], xT[:, c0:c0 + n], [lw, xT], [ps[2]])
                                ACT(rT[:, 0:n], ps[2][:, 0:n], AF.Sigmoid, [ps[2], lb], [rT], bias=lb[:, cb, d, 0:1])
                                MM(ps[3][:, 0:n], lw[:, d, 1, cb, :], xT[:, c0:c0 + n], [lw, xT], [ps[3]])
                                ACT(iT[:, 0:n], ps[3][:, 0:n], AF.Sigmoid, [ps[3], lb], [iT], bias=lb[:, cb, d, 1:2])
                                ACT(aT[:, c0:c0 + n], rT[:, 0:n], AF.Exp, [rT, nsp], [aT], scale=nsp[:, cb, d:d + 1])
                                ACT(tmq[:, 0:n], aT[:, c0:c0 + n], AF.Square, [aT], [tmq])
                                ACT(tmq[:, 0:n], tmq[:, 0:n], AF.Sqrt, [tmq], [tmq], scale=-1.0, bias=1.0)
                                TT(tmq[:, 0:n], tmq[:, 0:n], iT[:, 0:n], ALU.mult, [tmq, iT], [tmq])
                                TT(xiT[:, c0:c0 + n], tmq[:, 0:n], xT[:, c0:c0 + n], ALU.mult, [tmq, xT], [xiT])
                            init = h0[:, d:d + 1] if samp else 0.0
                            if d == 0:
                                k.op("dve", lambda e, init=init: e.tensor_tensor_scan(out=hT[0][:, 0:Tn], data0=aT[:, 0:Tn], data1=xiT[:, 0:Tn], initial=init, op0=ALU.mult, op1=ALU.add),
                                     reads=_r([aT, xiT, h0]), writes=_r([hT[0]]))
                            else:
                                k.op("dve", lambda e, init=init: e.tensor_tensor_scan(out=hT[1][:, Tn - 1::-1] if False else hT[1][:, 0:Tn][:, ::-1], data0=aT[:, 0:Tn][:, ::-1], data1=xiT[:, 0:Tn][:, ::-1], initial=init, op0=ALU.mult, op1=ALU.add),
                                     reads=_r([aT, xiT, h0]), writes=_r([hT[1]]))
                        if not samp:
                            DMA(o_lr[sidx, l, 0, cb * 128:(cb + 1) * 128].rearrange("(p o) -> p o", o=1), hT[0][:, Tn - 1:Tn], [hT[0]], [o_lr])
                            DMA(o_lr[sidx, l, 1, cb * 128:(cb + 1) * 128].rearrange("(p o) -> p o", o=1), hT[1][:, 0:1], [hT[1]], [o_lr])
                        TT(hT[0][:, 0:Tn], hT[0][:, 0:Tn], hT[1][:, 0:Tn], ALU.add, [hT[0], hT[1]], [hT[0]])
                        for blk in range(Tn // 128):
                            g_ = gt[blk % 2]; o_ = ot[blk % 2]; p = ps[4 + blk % 2]
                            LOADR(g_, lambda p0, pn: g_[p0:p0 + pn, :], U, base, blk * 128, 128, cm, C_LRG + cb * 128, C_LRG + (cb + 1) * 128)
                            ACT(g_.ap, g_.ap, AF.Gelu, [g_], [g_])
                            TR(p[:, 0:128], hT[0][:, blk * 128:(blk + 1) * 128], 128, [hT[0]], [p])
                            TT(o_.ap, p[:, 0:128], g_.ap, ALU.mult, [p, g_], [o_])
                            STORER(MIX, base, blk * 128, 128, cm, 768 + cb * 128, 768 + (cb + 1) * 128, o_, lambda p0, pn: o_[p0:p0 + pn, :])
              k.barrier()
            if "ssd" in mixers:
              with contextlib.ExitStack() as st:
                ssp = B.sb(st, "ssp", [128, 16]); nA = B.sb(st, "nA", [128, 8])
                DMA(ssp.ap, ssp_d[l], [], [ssp])
                ACT(nA.ap, ssp[:, 8:16], AF.Exp, [ssp], [nA])
                TS(nA.ap, nA.ap, -1.0, None, ALU.mult, ALU.bypass, [nA], [nA])
                HST = [B.sb(st, f"HST{d}", [64, 4, 64]) for d in range(2)]
                def mk(nm, shp): return [B.sb(st, f"{nm}{d}", shp) for d in range(2)]
                xbc = mk("xbc", [64, 512]); dtr = mk("dtr", [64, 4]); dtt = mk("dtt", [64, 4]); la = mk("la", [64, 4]); xdt = mk("xdt", [64, 4, 64])
                cc = mk("cc", [64, 4]); seg = mk("seg", [64, 4, 64]); eR = mk("eR", [64, 4, 64]); BCT = mk("BCT", [64, 4, 64])
                scT = mk("scT", [64, 4, 64]); CTs = mk("CTs", [64, 4, 64]); ych = mk("ych", [64, 256]); xt2 = mk("xt2", [64, 4, 64])
                sin = B.sb(st, "sin", [64, 4, 64]); sout = B.sb(st, "sout", [64, 4, 64])
                for (base, Tn, samp, sidx) in SEQS:
                    cm = samp and (l % 2 == 1); nch = Tn // 64
                    for d in range(2):
                        if samp:
                            DMA(sin.ap, s_ss[l, d].rearrange("h p n -> p h n"), [], [sin])
                            for h in range(4): TR(ps[0][0:64, h * 64:(h + 1) * 64], sin[:, h, :], 64, [sin], [ps[0]])
                            CP(HST[d].ap, ps[0][0:64, 0:256].rearrange("p (h n) -> p h n", h=4), [ps[0]], [HST[d]])
                        else:
                            k.op("dve", lambda e, d=d: e.memset(HST[d].ap, 0.0), writes=_r([HST[d]]))
                    for j in range(nch):
                        for d in range(2):
                            jc = j if d == 0 else nch - 1 - j; t0 = jc * 64; ll = 63 if d == 0 else 0
                            pA, pB, pC, pD = ps[d * 4 + 0], ps[d * 4 + 1], ps[d * 4 + 2], ps[d * 4 + 3]
                            X_, dr = xbc[d], dtr[d]
                            LOADR(X_, lambda p0, pn: X_[p0:p0 + pn, :], CVO, base, t0, 64, cm, 0, 512)
                            LOADR(dr, lambda p0, pn: dr[p0:p0 + pn, :], U, base, t0, 64, cm, C_SSDT + 4 * d, C_SSDT + 4 * d + 4)
                            TT(dtt[d].ap, dr.ap, ssp[0:64, 4 * d:4 * d + 4], ALU.add, [dr, ssp], [dtt[d]])
                            ACT(dtt[d].ap, dtt[d].ap, AF.Exp, [dtt[d]], [dtt[d]])
                            ACT(dtt[d].ap, dtt[d].ap, AF.Ln, [dtt[d]], [dtt[d]], bias=1.0)
                            TT(la[d].ap, dtt[d].ap, nA[0:64, 4 * d:4 * d + 4], ALU.mult, [dtt[d], nA], [la[d]])
                            TT(xdt[d].ap, X_[:, 0:256].rearrange("p (h q) -> p h q", h=4), dtt[d].ap.unsqueeze(2).to_broadcast([64, 4, 64]), ALU.mult, [X_, dtt[d]], [xdt[d]])
                            MM(pA[0:64, 0:4], TRI[:, d, :], la[d].ap, [TRI, la[d]], [pA])
                            CP(cc[d].ap, pA[0:64, 0:4], [pA], [cc[d]], eng="act")
                            for h in range(4):
                                MM(pB[0:64, h * 64:(h + 1) * 64], la[d][:, h:h + 1].to_broadcast([64, 64]), TRI[:, d, :], [la[d], TRI], [pB])
                            for h in range(4):
                                TS(seg[d][:, h, :], pB[0:64, h * 64:(h + 1) * 64], cc[d][:, h:h + 1], 0.0, ALU.subtract, ALU.min, [pB, cc[d]], [seg[d]])
                            ACT(seg[d].ap, seg[d].ap, AF.Exp, [seg[d]], [seg[d]])
                            ACT(eR[d].ap, pB[0:64, 0:256].rearrange("p (h q) -> p h q", h=4), AF.Exp, [pB], [eR[d]])
                            for i in range(4): TR(pC[0:64, i * 64:(i + 1) * 64], X_[:, 256 + 64 * i:256 + 64 * (i + 1)], 64, [X_], [pC])
                            CP(BCT[d].ap, pC[0:64, 0:256].rearrange("p (h q) -> p h q", h=4), [pC], [BCT[d]], eng="act")
                            for g2 in range(2):
                                MM(pD[0:64, g2 * 64:(g2 + 1) * 64], BCT[d][:, g2, :], BCT[d][:, 2 + g2, :], [BCT[d]], [pD])
                            TT(seg[d].ap, seg[d].ap, TRI[:, d, :].unsqueeze(1).to_broadcast([64, 4, 64]), ALU.mult, [seg[d], TRI], [seg[d]])
                            for g2 in range(2):
                                TT(scT[d][:, 2 * g2:2 * g2 + 2, :], seg[d][:, 2 * g2:2 * g2 + 2, :], pD[0:64, g2 * 64:(g2 + 1) * 64].unsqueeze(1).to_broadcast([64, 2, 64]), ALU.mult, [seg[d], pD], [scT[d]])
                                TT(CTs[d][:, 2 * g2:2 * g2 + 2, :], eR[d][:, 2 * g2:2 * g2 + 2, :], BCT[d][:, 2 + g2, :].unsqueeze(1).to_broadcast([64, 2, 64]), ALU.mult, [eR[d], BCT[d]], [CTs[d]])
                            for h in range(4):
                                MM(pA[0:64, 64 + h * 64:64 + (h + 1) * 64], scT[d][:, h, :], xdt[d][:, h, :], [scT[d], xdt[d]], [pA], start=True, stop=False)
                                MM(pA[0:64, 64 + h * 64:64 + (h + 1) * 64], CTs[d][:, h, :], HST[d][:, h, :], [CTs[d], HST[d]], [pA], start=False, stop=True)
                            CP(ych[d].ap, pA[0:64, 64:320], [pA], [ych[d]])
                            Yd = YF if d == 0 else YB; yc = ych[d]
                            STORER(Yd, base, t0, 64, cm, 0, 256, yc, lambda p0, pn: yc[p0:p0 + pn, :])
                            TT(xt2[d].ap, xdt[d].ap, seg[d][:, :, ll:ll + 1].to_broadcast([64, 4, 64]), ALU.mult, [xdt[d], seg[d]], [xt2[d]])
                            for h in range(4):
                                MM(pC[0:64, 256 + h * 64:256 + (h + 1) * 64], X_[:, 256 + 64 * (h // 2):256 + 64 * (h // 2 + 1)], xt2[d][:, h, :], [X_, xt2[d]], [pC])
                            for h in range(4):
                                STT(HST[d][:, h, :], HST[d][:, h, :], eR[d][:, h, ll:ll + 1], pC[0:64, 256 + h * 64:256 + (h + 1) * 64], ALU.mult, ALU.add, [HST[d], eR[d], pC], [HST[d]])
                    if not samp:
                        for d in range(2):
                            for h in range(4): TR(ps[0][0:64, h * 64:(h + 1) * 64], HST[d][:, h, :], 64, [HST[d]], [ps[0]])
                            CP(sout.ap, ps[0][0:64, 0:256].rearrange("p (h n) -> p h n", h=4), [ps[0]], [sout])
                            DMA(o_ss[sidx, l, d].rearrange("h p n -> p h n"), sout.ap, [sout], [o_ss])
              k.barrier()
              with contextlib.ExitStack() as st:
                sD = B.sb(st, "sD", [128, 4]); sn = B.sb(st, "sn", [128, 256]); DMA(sD.ap, ssD_d[l], [], [sD]); DMA(sn.ap, ssn_d[l], [], [sn])
                yf = B.sb(st, "yf", [128, 256]); yb = B.sb(st, "yb", [128, 256]); xs_ = B.sb(st, "xs_", [128, 256]); zz = B.sb(st, "zz", [128, 256])
                q2 = B.sb(st, "q2", [128, 2]); jk = B.sb(st, "jk", [128, 256])
                for ti in range(NTILE):
                    r0 = ti * 128
                    DMA(yf.ap, YF[r0:r0 + 128, 0:256], [YF], [yf]); DMA(yb.ap, YB[r0:r0 + 128, 0:256], [YB], [yb])
                    DMA(xs_.ap, CVO[r0:r0 + 128, 0:256], [CVO], [xs_]); DMA(zz.ap, U[r0:r0 + 128, C_SSZ:C_SSZ + 256], [U], [zz])
                    TT(yf.ap, yf.ap, yb.ap, ALU.add, [yf, yb], [yf])
                    TT(xs_.ap.rearrange("p (h q) -> p h q", h=4), xs_.ap.rearrange("p (h q) -> p h q", h=4), sD.ap.unsqueeze(2).to_broadcast([128, 4, 64]), ALU.mult, [xs_, sD], [xs_])
                    TT(yf.ap, yf.ap, xs_.ap, ALU.add, [yf, xs_], [yf])
                    ACT(zz.ap, zz.ap, AF.Silu, [zz], [zz])
                    TT(yf.ap, yf.ap, zz.ap, ALU.mult, [yf, zz], [yf])
                    ACT(jk.ap, yf.ap, AF.Square, [yf], [jk, q2], accum_out=q2[:, 0:1])
                    TS(q2[:, 1:2], q2[:, 0:1], 1.0 / 256, EPS, ALU.mult, ALU.add, [q2], [q2])
                    ACT(q2[:, 1:2], q2[:, 1:2], AF.Sqrt, [q2], [q2])
                    k.op("dve", lambda e: e.reciprocal(out=q2[:, 1:2], in_=q2[:, 1:2]), reads=_r([q2]), writes=_r([q2]))
                    STT(yf.ap, yf.ap, q2[:, 1:2], sn.ap, ALU.mult, ALU.mult, [yf, q2, sn], [yf])
                    DMA(MIX[r0:r0 + 128, 256:512], yf.ap, [yf], [MIX])
              k.barrier()

            if "mlstm" in mixers:
              with contextlib.ExitStack() as st:
                mlp = B.sb(st, "mlp", [128, 16]); DMA(mlp.ap, mlp_d[l], [], [mlp])
                NEGM = B.sb(st, "NEGM", [64, 2, 64]); TS(NEGM.ap, TRI.ap, -1.0, 1e30, ALU.add, ALU.mult, [TRI], [NEGM])
                def mk(nm, shp): return [B.sb(st, f"{nm}{d}", shp) for d in range(2)]
                CT = mk("mCT", [64, 4, 65]); mS = mk("mS", [64, 4])
                qq = mk("mq", [64, 256]); kk_ = mk("mk", [64, 256]); vx = mk("mvx", [64, 4, 65]); gi = mk("mgi", [64, 4]); gf = mk("mgf", [64, 4])
                bb = mk("mb", [64, 4]); imb = mk("mimb", [64, 4]); dm = mk("mdm", [64, 4, 64]); mx = mk("mmx", [64, 4]); mt8 = mk("mt8", [64, 8])
                nm_ = mk("mnm", [64, 4]); si = mk("msi", [64, 4]); qkT_ = mk("mqkT", [64, 8, 64]); qk = mk("mqk", [64, 4, 64]); qkt = mk("mqkt", [64, 4, 64])
                intra = mk("mintra", [64, 4, 65]); tot = mk("mtot", [64, 4, 65]); dn = mk("mdn", [64, 4]); hch = mk("mhch", [64, 4, 64])
                me = mk("mme", [64, 8]); t1 = mk("mt1", [64, 4]); we = mk("mwe", [64, 4]); se = mk("mse", [64, 4]); vw = mk("mvw", [64, 4, 65])
                sin = B.sb(st, "msin", [64, 4, 64]); sout = B.sb(st, "msout", [64, 4, 64])
                for d in range(2):
                    k.op("dve", lambda e, d=d: e.memset(vx[d].ap, 1.0), writes=_r([vx[d]]))
                for (base, Tn, samp, sidx) in SEQS:
                    cm = samp and (l % 2 == 1); nch = Tn // 64
                    for d in range(2):
                        if samp:
                            DMA(sin.ap, s_mc[l, d].rearrange("h v k -> v h k"), [], [sin])
                            for h in range(4): TR(ps[0][0:64, h * 64:(h + 1) * 64], sin[:, h, :], 64, [sin], [ps[0]])
                            CP(CT[d][:, :, 0:64], ps[0][0:64, 0:256].rearrange("p (h n) -> p h n", h=4), [ps[0]], [CT[d]])
                            for h in range(4):
                                DMA(CT[d][:, h, 64:65], s_mn[l, d, h, :].rearrange("(p o) -> p o", o=1), [], [CT[d]])
                            DMA(mS[d].ap, smm_d[l, d], [], [mS[d]])
                        else:
                            k.op("dve", lambda e, d=d: e.memset(CT[d].ap, 0.0), writes=_r([CT[d]]))
                            k.op("dve", lambda e, d=d: e.memset(mS[d].ap, 0.0), writes=_r([mS[d]]))
                    for j in range(nch):
                        for d in range(2):
                            jc = j if d == 0 else nch - 1 - j; t0 = jc * 64; ll = 63 if d == 0 else 0
                            pA, pB, pC, pD = ps[d * 4 + 0], ps[d * 4 + 1], ps[d * 4 + 2], ps[d * 4 + 3]
                            Q_, K_, V_, GI, GF = qq[d], kk_[d], vx[d], gi[d], gf[d]
                            LOADR(Q_, lambda p0, pn: Q_[p0:p0 + pn, :], U, base, t0, 64, cm, C_MLQ, C_MLQ + 256)
                            LOADR(K_, lambda p0, pn: K_[p0:p0 + pn, :], U, base, t0, 64, cm, C_MLK, C_MLK + 256)
                            for h in range(4):
                                LOADR(V_, lambda p0, pn, h=h: V_[p0:p0 + pn, h, 0:64], U, base, t0, 64, cm, C_MLV + 64 * h, C_MLV + 64 * (h + 1))
                            LOADR(GI, lambda p0, pn: GI[p0:p0 + pn, :], U, base, t0, 64, cm, C_MLI + 4 * d, C_MLI + 4 * d + 4)
                            LOADR(GF, lambda p0, pn: GF[p0:p0 + pn, :], U, base, t0, 64, cm, C_MLF + 4 * d, C_MLF + 4 * d + 4)
                            TS(K_.ap, K_.ap, 0.125, None, ALU.mult, ALU.bypass, [K_], [K_])
                            TT(GI.ap, GI.ap, mlp[0:64, 4 * d:4 * d + 4], ALU.add, [GI, mlp], [GI])
                            TT(GF.ap, GF.ap, mlp[0:64, 8 + 4 * d:8 + 4 * d + 4], ALU.add, [GF, mlp], [GF])
                            ACT(GF.ap, GF.ap, AF.Exp, [GF], [GF], scale=-1.0)
                            ACT(GF.ap, GF.ap, AF.Ln, [GF], [GF], bias=1.0)
                            TS(GF.ap, GF.ap, -1.0, None, ALU.mult, ALU.bypass, [GF], [GF])
                            MM(pA[0:64, 0:4], TRI[:, d, :], GF.ap, [TRI, GF], [pA])
                            CP(bb[d].ap, pA[0:64, 0:4], [pA], [bb[d]], eng="act")
                            TT(imb[d].ap, GI.ap, bb[d].ap, ALU.subtract, [GI, bb[d]], [imb[d]])
                            for h in range(4):
                                MM(pA[0:64, 64 + h * 64:64 + (h + 1) * 64], imb[d][:, h:h + 1].to_broadcast([64, 64]), ident[0:64, 0:64], [imb[d], ident], [pA])
                            for h in range(4):
                                STT(dm[d][:, h, :], pA[0:64, 64 + h * 64:64 + (h + 1) * 64], bb[d][:, h:h + 1], NEGM[:, 1 - d, :], ALU.add, ALU.add, [pA, bb[d], NEGM], [dm[d]])
                            k.op("dve", lambda e, d=d: e.tensor_reduce(out=mx[d].ap, in_=dm[d].ap, op=ALU.max, axis=AX.X), reads=_r([dm[d]]), writes=_r([mx[d]]))
                            TT(mt8[d][:, 4:8], bb[d].ap, mS[d].ap, ALU.add, [bb[d], mS[d]], [mt8[d]])
                            TT(mt8[d][:, 0:4], mx[d].ap, mt8[d][:, 4:8], ALU.max, [mx[d], mt8[d]], [mt8[d]])
                            TS(nm_[d].ap, mt8[d][:, 0:4], -1.0, None, ALU.mult, ALU.bypass, [mt8[d]], [nm_[d]])
                            for h in range(4):
                                ACT(dm[d][:, h, :], dm[d][:, h, :], AF.Exp, [dm[d], nm_[d]], [dm[d]], bias=nm_[d][:, h:h + 1])
                            TT(si[d].ap, mt8[d][:, 4:8], mt8[d][:, 0:4], ALU.subtract, [mt8[d]], [si[d]])
                            ACT(si[d].ap, si[d].ap, AF.Exp, [si[d]], [si[d]])
                            for h in range(4):
                                TR(pB[0:64, h * 64:(h + 1) * 64], Q_[:, h * 64:(h + 1) * 64], 64, [Q_], [pB])
                                TR(pB[0:64, 256 + h * 64:256 + (h + 1) * 64], K_[:, h * 64:(h + 1) * 64], 64, [K_], [pB])
                            CP(qkT_[d].ap, pB[0:64, :].rearrange("p (a b) -> p a b", a=8), [pB], [qkT_[d]], eng="act")
                            for h in range(4):
                                MM(pC[0:64, h * 64:(h + 1) * 64], qkT_[d][:, h, :], qkT_[d][:, 4 + h, :], [qkT_[d]], [pC])
                            TT(qk[d].ap, pC[0:64, 0:256].rearrange("p (a b) -> p a b", a=4), dm[d].ap, ALU.mult, [pC, dm[d]], [qk[d]])
                            for h in range(4):
                                TR(pC[0:64, 256 + h * 64:256 + (h + 1) * 64], qk[d][:, h, :], 64, [qk[d]], [pC])
                            CP(qkt[d].ap, pC[0:64, 256:512].rearrange("p (a b) -> p a b", a=4), [pC], [qkt[d]], eng="act")
                            for h in range(4):
                                MM(pD[0:64, h * 65:(h + 1) * 65], qkt[d][:, h, :], V_[:, h, :], [qkt[d], V_], [pD])
                                MM(pB[0:64, h * 65:(h + 1) * 65], qkT_[d][:, h, :], CT[d][:, h, :], [qkT_[d], CT[d]], [pB])
                            CP(intra[d].ap, pD[0:64, 0:260].rearrange("p (a b) -> p a b", a=4), [pD], [intra[d]])
                            for h in range(4):
                                STT(tot[d][:, h, :], pB[0:64, h * 65:(h + 1) * 65], si[d][:, h:h + 1], intra[d][:, h, :], ALU.mult, ALU.add, [pB, si[d], intra[d]], [tot[d]])
                            ACT(dn[d].ap, tot[d][:, :, 64], AF.Abs, [tot[d]], [dn[d]])
                            ACT(nm_[d].ap, nm_[d].ap, AF.Exp, [nm_[d]], [nm_[d]])
                            TT(dn[d].ap, dn[d].ap, nm_[d].ap, ALU.max, [dn[d], nm_[d]], [dn[d]])
                            k.op("dve", lambda e, d=d: e.reciprocal(out=dn[d].ap, in_=dn[d].ap), reads=_r([dn[d]]), writes=_r([dn[d]]))
                            TT(hch[d].ap, tot[d][:, :, 0:64], dn[d].ap.unsqueeze(2).to_broadcast([64, 4, 64]), ALU.mult, [tot[d], dn[d]], [hch[d]])
                            Yd = YF if d == 0 else YB; hc = hch[d]
                            STORER(Yd, base, t0, 64, cm, 256, 512, hc, lambda p0, pn: hc[p0:p0 + pn, :, :].rearrange("p a b -> p (a b)"))
                            TT(mt8[d][:, 4:8], bb[d].ap, bb[d].ap, ALU.bypass, [bb[d]], [mt8[d]]) if False else CP(mt8[d][:, 4:8], bb[d].ap, [bb[d]], [mt8[d]])
                            MM(pA[0:64, 8:16], ident[0:64, ll:ll + 1].to_broadcast([64, 64]), mt8[d].ap, [ident, mt8[d]], [pA])
                            CP(me[d].ap, pA[0:64, 8:16], [pA], [me[d]])
                            TT(t1[d].ap, me[d][:, 4:8], me[d][:, 0:4], ALU.subtract, [me[d]], [t1[d]])
                            TT(we[d].ap, imb[d].ap, t1[d].ap, ALU.add, [imb[d], t1[d]], [we[d]])
                            ACT(we[d].ap, we[d].ap, AF.Exp, [we[d]], [we[d]])
                            TT(se[d].ap, t1[d].ap, mS[d].ap, ALU.add, [t1[d], mS[d]], [se[d]])
                            ACT(se[d].ap, se[d].ap, AF.Exp, [se[d]], [se[d]])
                            TT(vw[d].ap, V_.ap, we[d].ap.unsqueeze(2).to_broadcast([64, 4, 65]), ALU.mult, [V_, we[d]], [vw[d]])
                            for h in range(4):
                                MM(pD[0:64, h * 65:(h + 1) * 65], K_[:, h * 64:(h + 1) * 64], vw[d][:, h, :], [K_, vw[d]], [pD])
                            for h in range(4):
                                STT(CT[d][:, h, :], CT[d][:, h, :], se[d][:, h:h + 1], pD[0:64, h * 65:(h + 1) * 65], ALU.mult, ALU.add, [CT[d], se[d], pD], [CT[d]])
                            CP(mS[d].ap, me[d][:, 0:4], [me[d]], [mS[d]])
                    if not samp:
                        for d in range(2):
                            for h in range(4): TR(ps[0][0:64, h * 64:(h + 1) * 64], CT[d][:, h, 0:64], 64, [CT[d]], [ps[0]])
                            CP(sout.ap, ps[0][0:64, 0:256].rearrange("p (h n) -> p h n", h=4), [ps[0]], [sout])
                            DMA(o_mc[sidx, l, d].rearrange("h v k -> v h k"), sout.ap, [sout], [o_mc])
                            for h in range(4):
                                DMA(o_mn[sidx, l, d, h, :].rearrange("(p o) -> p o", o=1), CT[d][:, h, 64:65], [CT[d]], [o_mn])
                            DMA(o_mm[sidx, l, d:d + 1, :], mS[d][0:1, :], [mS[d]], [o_mm])
              k.barrier()
              with contextlib.ExitStack() as st:
                gn = B.sb(st, "gn", [128, 256]); DMA(gn.ap, mln_d[l], [], [gn])
                yf = B.sb(st, "myf", [128, 256]); yb = B.sb(st, "myb", [128, 256]); zz = B.sb(st, "mzz", [128, 256])
                q4 = B.sb(st, "mq4", [128, 4]); jk = B.sb(st, "mjk", [128, 256])
                for ti in range(NTILE):
                    r0 = ti * 128
                    DMA(yf.ap, YF[r0:r0 + 128, 256:512], [YF], [yf]); DMA(yb.ap, YB[r0:r0 + 128, 256:512], [YB], [yb])
                    DMA(zz.ap, U[r0:r0 + 128, C_MLO:C_MLO + 256], [U], [zz])
                    TT(yf.ap, yf.ap, yb.ap, ALU.add, [yf, yb], [yf])
                    ACT(jk.ap, yf.ap, AF.Square, [yf], [jk])
                    k.op("dve", lambda e: e.tensor_reduce(out=q4.ap, in_=jk.ap.rearrange("p (a b) -> p a b", a=4), op=ALU.add, axis=AX.X), reads=_r([jk]), writes=_r([q4]))
                    TS(q4.ap, q4.ap, 1.0 / 64, EPS, ALU.mult, ALU.add, [q4], [q4])
                    ACT(q4.ap, q4.ap, AF.Sqrt, [q4], [q4])
                    k.op("dve", lambda e: e.reciprocal(out=q4.ap, in_=q4.ap), reads=_r([q4]), writes=_r([q4]))
                    TT(yf.ap.rearrange("p (a b) -> p a b", a=4), yf.ap.rearrange("p (a b) -> p a b", a=4), q4.ap.unsqueeze(2).to_broadcast([128, 4, 64]), ALU.mult, [yf, q4], [yf])
                    TT(yf.ap, yf.ap, gn.ap, ALU.mult, [yf, gn], [yf])
                    ACT(zz.ap, zz.ap, AF.Sigmoid, [zz], [zz])
                    TT(yf.ap, yf.ap, zz.ap, ALU.mult, [yf, zz], [yf])
                    DMA(MIX[r0:r0 + 128, 512:768], yf.ap, [yf], [MIX])
              k.barrier()

            if "rwkv" in mixers:
              with contextlib.ExitStack() as st:
                rp = B.sb(st, "rwp", [128, 4, 256]); DMA(rp.ap, rwp_d[l], [], [rp])
                ut = B.sb(st, "rut", [128, 768]); da = B.sb(st, "rda", [128, 1024]); rq = B.sb(st, "rrq", [128, 1536])
                q4 = B.sb(st, "rq4", [128, 4]); jk = B.sb(st, "rjk", [128, 256]); tm = B.sb(st, "rtm", [128, 256])
                def v4(ap): return ap.rearrange("p (a b) -> p a b", a=4)
                for ti in range(NTILE):
                    r0 = ti * 128
                    DMA(ut.ap, U[r0:r0 + 128, 0:768], [U], [ut]); DMA(da.ap, U[r0:r0 + 128, C_DEC:C_DEC + 1024], [U], [da])
                    TT(rq[:, 0:256], ut[:, 256:512], rp[:, 0, :], ALU.mult, [ut, rp], [rq])
                    ACT(jk.ap, rq[:, 0:256], AF.Square, [rq], [jk])
                    k.op("dve", lambda e: e.tensor_reduce(out=q4.ap, in_=v4(jk.ap), op=ALU.add, axis=AX.X), reads=_r([jk]), writes=_r([q4]))
                    TS(q4.ap, q4.ap, EPS, None, ALU.add, ALU.bypass, [q4], [q4])
                    ACT(q4.ap, q4.ap, AF.Sqrt, [q4], [q4])
                    k.op("dve", lambda e: e.reciprocal(out=q4.ap, in_=q4.ap), reads=_r([q4]), writes=_r([q4]))
                    TT(v4(rq[:, 0:256]), v4(rq[:, 0:256]), q4.ap.unsqueeze(2).to_broadcast([128, 4, 64]), ALU.mult, [rq, q4], [rq])
                    for d in range(2):
                        a_ = da[:, 512 + 256 * d:512 + 256 * (d + 1)]
                        STT(rq[:, 256 + 256 * d:512 + 256 * d], a_, -1.0, rq[:, 0:256], ALU.mult, ALU.mult, [da, rq], [rq])
                        STT(tm.ap, a_, -1.0, rp[:, 1, :], ALU.add, ALU.mult, [da, rp], [tm])
                        STT(rq[:, 768 + 256 * d:1024 + 256 * d], tm.ap, 1.0, ut[:, 256:512], ALU.add, ALU.mult, [tm, ut], [rq])
                    TT(tm.ap, ut[:, 0:256], ut[:, 256:512], ALU.mult, [ut], [tm])
                    TT(tm.ap, tm.ap, rp[:, 2, :], ALU.mult, [tm, rp], [tm])
                    k.op("dve", lambda e: e.tensor_reduce(out=q4.ap, in_=v4(tm.ap), op=ALU.add, axis=AX.X), reads=_r([tm]), writes=_r([q4]))
                    TT(v4(rq[:, 1280:1536]), v4(ut[:, 512:768]), q4.ap.unsqueeze(2).to_broadcast([128, 4, 64]), ALU.mult, [ut, q4], [rq])
                    DMA(RQ[r0:r0 + 128, :], rq.ap, [rq], [RQ])
              k.barrier()
              with contextlib.ExitStack() as st:
                ones = B.sb(st, "rones", [64, 1]); k.op("dve", lambda e: e.memset(ones.ap, 1.0), writes=_r([ones]))
                MK = B.sb(st, "rMK", [128, 2, 256]); SM = B.sb(st, "rSM", [128, 2, 64]); TRI2 = B.sb(st, "rTRI2", [128, 2, 128])
                for d in range(2):
                    DMA(MK[:, d, 0:128], rwm_d[d], [], [MK]); DMA(MK[:, d, 128:256], rwm_d[d], [], [MK]); DMA(SM[:, d, :], rws_d[d], [], [SM]); DMA(TRI2[:, d, :], tri2_d[d], [], [TRI2])
                ctx = []
                for d in range(2):
                    c = dict(d=d)
                    for nm in ("ka", "na", "kt", "rr", "lw", "e1", "e2", "V2"): c[nm] = B.sb(st, f"r{nm}{d}", [128, 256])
                    c["gL"] = B.sb(st, f"rgL{d}", [128, 2]); c["pc"] = ps[d * 4]; c["units"] = []
                    for q in range(2):
                        u = dict(d=d, q=q, c=c, pa=ps[d * 4 + 1 + q], pb=ps[d * 4 + 3]); cid = f"{d}{q}"
                        u["LR"] = B.sb(st, f"rLR{cid}", [128, 256]); u["G"] = B.sb(st, f"rG{cid}", [128, 256])
                        u["N0"] = B.sb(st, f"rN0{cid}", [128, 256]); u["Xn"] = [B.sb(st, f"rXn{cid}{i}", [128, 256]) for i in range(2)]
                        u["TT"] = B.sb(st, f"rTT{cid}", [128, 256]); u["Qs"] = B.sb(st, f"rQs{cid}", [128, 64]); u["PT"] = B.sb(st, f"rPT{cid}", [128, 64])
                        u["Ys"] = B.sb(st, f"rYs{cid}", [128, 64]); u["ST"] = B.sb(st, f"rSTp{cid}", [128, 64])
                        k.op("dve", lambda e, u=u: e.memset(u["N0"].ap, 0.0), writes=_r([u["N0"]]))
                        c["units"].append(u)
                    ctx.append(c)
                I2 = B.sb(st, "rI2", [128, 256])
                CP(I2[:, 0:128], ident.ap, [ident], [I2]); CP(I2[:, 128:256], ident.ap, [ident], [I2])
                sin3 = B.sb(st, "rsin3", [64, 128])
                import os as _os
                RWS = int(_os.environ.get("RW_STAGE", "9"))
                for (base, Tn, samp, sidx) in (SEQS if RWS > -3 else []):
                    cm = samp and (l % 2 == 1); nch = Tn // 64
                    if RWS == -2: nch = 0
                    for c in ctx:
                        d = c["d"]
                        for u in c["units"]:
                            q = u["q"]
                            if samp:
                                DMA(sin3.ap.rearrange("p (a b) -> p a b", a=2), s_rw[l, d, 2 * q:2 * q + 2].rearrange("h v k -> v h k"), [], [sin3])
                                TR(u["pa"][:, 0:64], sin3.ap, 64, [sin3], [u["pa"]])
                                CP(u["ST"].ap, u["pa"][:, 0:64], [u["pa"]], [u["ST"]])
                            else:
                                k.op("dve", lambda e, u=u: e.memset(u["ST"].ap, 0.0), writes=_r([u["ST"]]))
                    for j in range(nch):
                        for c in ctx:
                            d = c["d"]; jc = j if d == 0 else nch - 1 - j; t0 = jc * 64; pc = c["pc"]
                            srcs = [("ka", RQ, 0), ("na", RQ, 256 + 256 * d), ("kt", RQ, 768 + 256 * d), ("rr", U, C_RWR), ("lw", U, C_DEC + 256 * d), ("V2", U, C_RWV)]
                            for nm, A_, c0 in srcs:
                                t_ = c[nm]
                                for hf in range(2):
                                    LOADR(t_, lambda p0, pn, t_=t_, hf=hf: t_[64 * hf + p0:64 * hf + p0 + pn, :], A_, base, t0, 64, cm, c0, c0 + 256)
                            if RWS == -1: continue
                            SUB = int(_os.environ.get("RW_SUB", "99"))
                            ops = []
                            ops.append(lambda: MM(pc[:, 0:256], TRI2[:, d, :], c["lw"].ap, [TRI2, c["lw"]], [pc]))
                            ops.append(lambda: [MM(pc[:, 256 + q:257 + q], c["lw"][0:64, 128 * q:128 * (q + 1)], ones.ap, [c["lw"], ones], [pc]) for q in range(2)])
                            ops.append(lambda: ACT(c["e1"].ap, pc[:, 0:256], AF.Exp, [pc], [c["e1"]], scale=-1.0))
                            ops.append(lambda: TT(c["e2"].ap, pc[:, 0:256], c["lw"].ap, ALU.subtract, [pc, c["lw"]], [c["e2"]]))
                            ops.append(lambda: ACT(c["e2"].ap, c["e2"].ap, AF.Exp, [c["e2"]], [c["e2"]]))
                            ops.append(lambda: TT(c["na"].ap, c["na"].ap, c["e1"].ap, ALU.mult, [c["na"], c["e1"]], [c["na"]]))
                            ops.append(lambda: TT(c["kt"].ap, c["kt"].ap, c["e1"].ap, ALU.mult, [c["kt"], c["e1"]], [c["kt"]]))
                            ops.append(lambda: TT(c["ka"].ap, c["ka"].ap, c["e2"].ap, ALU.mult, [c["ka"], c["e2"]], [c["ka"]]))
                            ops.append(lambda: ACT(c["e1"].ap, pc[:, 0:256], AF.Exp, [pc], [c["e1"]]))
                            ops.append(lambda: TT(c["rr"].ap, c["rr"].ap, c["e1"].ap, ALU.mult, [c["rr"], c["e1"]], [c["rr"]]))
                            ops.append(lambda: ACT(c["gL"].ap, pc[:, 256:258], AF.Exp, [pc], [c["gL"]]))
                            for f_ in ops[:SUB]: f_()
                        units = [u for c in ctx for u in c["units"]]
                        import os as _os
                        RWS = int(_os.environ.get("RW_STAGE", "9"))
                        if RWS < 1: continue
                        for u in units:
                            c = u["c"]; q = u["q"]; pa = u["pa"]; sl = slice(128 * q, 128 * (q + 1))
                            for i_, nm in enumerate(("na", "kt", "ka", "rr")):
                                TR(pa[:, 64 * i_:64 * (i_ + 1)], c[nm][0:64, sl], 64, [c[nm]], [pa])
                            CP(u["LR"].ap, pa[:, 0:256], [pa], [u["LR"]], eng="act")
                        if RWS < 2: continue
                        for u in units:
                            c = u["c"]; d = c["d"]; pa = u["pa"]; pb = u["pb"]; o = 256 * u["q"]; LR = u["LR"]
                            for hh in range(2):
                                b0 = 64 * hh
                                MM(pa[b0:b0 + 64, 256:384], LR[b0:b0 + 64, 0:64], LR[b0:b0 + 64, 128:256], [LR], [pa])
                                MM(pa[b0:b0 + 64, 384:512], LR[b0:b0 + 64, 64:128], LR[b0:b0 + 64, 128:256], [LR], [pa])
                                MM(pb[b0:b0 + 64, o + b0:o + b0 + 64], LR[b0:b0 + 64, 128:192], LR[b0:b0 + 64, 0:64], [LR], [pb])
                            TT(u["G"].ap, pa[:, 256:512], MK[:, d, :], ALU.mult, [pa, MK], [u["G"]])
                            for hh in range(2):
                                b0 = 64 * hh
                                CP(u["N0"][b0:b0 + 64, b0:b0 + 64], u["G"][b0:b0 + 64, 0:64], [u["G"]], [u["N0"]])
                                TT(u["N0"][b0:b0 + 64, 128 + b0:128 + b0 + 64], pb[b0:b0 + 64, o + b0:o + b0 + 64], SM[b0:b0 + 64, 1 - d, :], ALU.mult, [pb, SM], [u["N0"]])
                            TT(u["TT"].ap, u["N0"].ap, I2.ap, ALU.add, [u["N0"], I2], [u["TT"]])
                        if RWS < 3: continue
                        for jj in range(1, 6):
                            for u in units:
                                pb = u["pb"]; o = 256 * u["q"]
                                Xp = u["N0"] if jj == 1 else u["Xn"][(jj - 1) % 2]; Xn = u["Xn"][jj % 2]
                                MM(pb[:, o:o + 128], Xp[:, 128:256], Xp[:, 0:128], [Xp], [pb])
                                if jj < 5: MM(pb[:, o + 128:o + 256], Xp[:, 0:128], Xp[:, 128:256], [Xp], [pb])
                                w_ = 256 if jj < 5 else 128
                                CP(Xn[:, 0:w_], pb[:, o:o + w_], [pb], [Xn], eng="act")
                            for u in units:
                                pb = u["pb"]; o = 256 * u["q"]; Xn = u["Xn"][jj % 2]; T_ = u["TT"]
                                MM(pb[:, o:o + 128], T_[:, 128:256], Xn[:, 0:128], [T_, Xn], [pb])
                                if jj < 5: MM(pb[:, o + 128:o + 256], Xn[:, 0:128], T_[:, 128:256], [T_, Xn], [pb])
                                w_ = 256 if jj < 5 else 128
                                TT(T_[:, 0:w_], T_[:, 0:w_], pb[:, o:o + w_], ALU.add, [T_, pb], [T_])
                        if RWS < 4: continue
                        for u in units:
                            c = u["c"]; d = c["d"]; q = u["q"]; pa = u["pa"]; LR = u["LR"]; V2 = c["V2"]; ST_ = u["ST"]; G = u["G"]
                            for hh in range(2):
                                b0 = 64 * hh; h = 2 * q + hh
                                MM(pa[b0:b0 + 64, 0:64], LR[b0:b0 + 64, 128:192], ST_[b0:b0 + 64, :], [LR, ST_], [pa], start=True, stop=False)
                                MM(pa[b0:b0 + 64, 0:64], G[b0:b0 + 64, 128:192], V2[b0:b0 + 64, 64 * h:64 * (h + 1)], [G, V2], [pa], start=False, stop=True)
                            CP(u["Qs"].ap, pa[:, 0:64], [pa], [u["Qs"]], eng="act")
                            MM(pa[:, 64:128], u["TT"][:, 0:128], u["Qs"].ap, [u["TT"], u["Qs"]], [pa])
                            CP(u["PT"].ap, pa[:, 64:128], [pa], [u["PT"]], eng="act")
                        if RWS < 5: continue
                        for u in units:
                            c = u["c"]; d = c["d"]; q = u["q"]; pa = u["pa"]; pb = u["pb"]; o = 256 * q; LR = u["LR"]; V2 = c["V2"]; ST_ = u["ST"]; G = u["G"]; PT = u["PT"]
                            jc = j if d == 0 else nch - 1 - j; t0 = jc * 64
                            for hh in range(2):
                                b0 = 64 * hh; h = 2 * q + hh
                                MM(pa[b0:b0 + 64, 128:192], LR[b0:b0 + 64, 192:256], ST_[b0:b0 + 64, :], [LR, ST_], [pa], start=True, stop=False)
                                MM(pa[b0:b0 + 64, 128:192], G[b0:b0 + 64, 192:256], V2[b0:b0 + 64, 64 * h:64 * (h + 1)], [G, V2], [pa], start=False, stop=False)
                                MM(pa[b0:b0 + 64, 128:192], G[b0:b0 + 64, 64:128], PT[b0:b0 + 64, :], [G, PT], [pa], start=False, stop=True)
                            Ys = u["Ys"]
                            CP(Ys.ap, pa[:, 128:192], [pa], [Ys], eng="act")
                            for hh in range(2):
                                h = 2 * q + hh
                                STORER(YR[d], base, t0, 64, cm, 64 * h, 64 * (h + 1), Ys, lambda p0, pn, Ys=Ys, hh=hh: Ys[64 * hh + p0:64 * hh + p0 + pn, :])
                            for hh in range(2):
                                b0 = 64 * hh; h = 2 * q + hh
                                MM(pb[b0:b0 + 64, o:o + 64], c["na"][b0:b0 + 64, 64 * h:64 * (h + 1)], PT[b0:b0 + 64, :], [c["na"], PT], [pb], start=True, stop=False)
                                MM(pb[b0:b0 + 64, o:o + 64], c["kt"][b0:b0 + 64, 64 * h:64 * (h + 1)], V2[b0:b0 + 64, 64 * h:64 * (h + 1)], [c["kt"], V2], [pb], start=False, stop=True)
                            TT(ST_.ap, ST_.ap, pb[:, o:o + 64], ALU.add, [ST_, pb], [ST_])
                            TS(ST_.ap, ST_.ap, c["gL"][:, q:q + 1], None, ALU.mult, ALU.bypass, [ST_, c["gL"]], [ST_])
                    if not samp:
                        for c in ctx:
                            d = c["d"]
                            for u in c["units"]:
                                q = u["q"]
                                TR(u["pa"][0:64, 0:128], u["ST"].ap, 128, [u["ST"]], [u["pa"]])
                                CP(sin3.ap, u["pa"][0:64, 0:128], [u["pa"]], [sin3])
                                DMA(o_rw[sidx, l, d, 2 * q:2 * q + 2].rearrange("h v k -> v h k"), sin3.ap.rearrange("p (a b) -> p a b", a=2), [sin3], [o_rw])
              k.barrier()
              with contextlib.ExitStack() as st:
                rp = B.sb(st, "rwp2", [128, 4, 256]); DMA(rp.ap, rwp_d[l], [], [rp])
                yf = B.sb(st, "ryf", [128, 256]); yb = B.sb(st, "ryb", [128, 256]); zz = B.sb(st, "rzz", [128, 256]); bo = B.sb(st, "rbo", [128, 256])
                q4 = B.sb(st, "rq4b", [128, 4]); jk = B.sb(st, "rjkb", [128, 256])
                def v4(ap): return ap.rearrange("p (a b) -> p a b", a=4)
                for ti in range(NTILE):
                    r0 = ti * 128
                    DMA(yf.ap, YR[0][r0:r0 + 128, :], [YR[0]], [yf]); DMA(yb.ap, YR[1][r0:r0 + 128, :], [YR[1]], [yb])
                    DMA(zz.ap, U[r0:r0 + 128, C_RWG:C_RWG + 256], [U], [zz]); DMA(bo.ap, RQ[r0:r0 + 128, 1280:1536], [RQ], [bo])
                    TT(yf.ap, yf.ap, yb.ap, ALU.add, [yf, yb], [yf])
                    ACT(jk.ap, yf.ap, AF.Square, [yf], [jk])
                    k.op("dve", lambda e: e.tensor_reduce(out=q4.ap, in_=v4(jk.ap), op=ALU.add, axis=AX.X), reads=_r([jk]), writes=_r([q4]))
                    TS(q4.ap, q4.ap, 1.0 / 64, EPS, ALU.mult, ALU.add, [q4], [q4])
                    ACT(q4.ap, q4.ap, AF.Sqrt, [q4], [q4])
                    k.op("dve", lambda e: e.reciprocal(out=q4.ap, in_=q4.ap), reads=_r([q4]), writes=_r([q4]))
                    TT(v4(yf.ap), v4(yf.ap), q4.ap.unsqueeze(2).to_broadcast([128, 4, 64]), ALU.mult, [yf, q4], [yf])
                    TT(yf.ap, yf.ap, rp[:, 3, :], ALU.mult, [yf, rp], [yf])
                    TT(yf.ap, yf.ap, bo.ap, ALU.add, [yf, bo], [yf])
                    ACT(zz.ap, zz.ap, AF.Sigmoid, [zz], [zz])
                    TT(yf.ap, yf.ap, zz.ap, ALU.mult, [yf, zz], [yf])
                    DMA(MIX[r0:r0 + 128, 0:256], yf.ap, [yf], [MIX])
              k.barrier()

            if stop_after != ("C", l):
              with contextlib.ExitStack() as st:
                wo = B.sb(st, "wo", [128, 8, D]); wq = wo
                pk = B.sb(st, "pk", [64, 2, 128])
                for sd in range(2): DMA(pk[:, sd, :], pk_d[l, sd], [], [pk])
                iot = B.sb(st, "iot", [128, 128]); DMA(iot.ap, iota_d.ap, [], [iot])
                mt = B.sb(st, "pmt", [128, D]); mT = B.sb(st, "pmT", [128, 8, 128]); x1 = B.sb(st, "px1", [128, D]); h2 = B.sb(st, "ph2", [128, D])
                sq = B.sb(st, "psq", [128, 2]); junk = B.sb(st, "pjunk", [128, D]); qT = B.sb(st, "pqT", [64, 16, 128])
                sc = B.sb(st, "psc", [128, 16, 128]); scw = B.sb(st, "pscw", [128, 128]); v16 = B.sb(st, "pv16", [128, 16, 16]); i16 = B.sb(st, "pi16", [128, 16, 16], U32)
                i16f = B.sb(st, "pi16f", [128, 16, 16]); cand = B.sb(st, "pcand", [128, 8, 256]); cidx = B.sb(st, "pcidx", [128, 8, 256]); cw = B.sb(st, "pcw", [128, 256])
                s16 = B.sb(st, "ps16", [128, 8, 16]); eqm = B.sb(st, "peqm", [128, 16, 256]); ef = B.sb(st, "pef", [128, 128]); gt = B.sb(st, "pgt", [128, 128])
                p16 = B.sb(st, "pp16", [128, 8, 16], U32); p16f = B.sb(st, "pp16f", [128, 8, 16]); io256 = B.sb(st, "pio256", [128, 256]); DMA(io256.ap, iota256_d.ap, [], [io256])
                g8 = B.sb(st, "pg8", [128, 8]); eT = B.sb(st, "peT", [128, 128], I32)
                Ug = [B.sb(st, f"pUg{i}", [128, 2 * D]) for i in range(6)]
                eqf = eqm.ap.rearrange("p a b -> p (a b)")
                Ug += [T(eqf[:, 0:2 * D]), T(eqf[:, 2 * D:4 * D]), T(cand.ap.rearrange("p a b -> p (a b)")), T(cidx.ap.rearrange("p a b -> p (a b)"))]
                NGB = len(Ug)
                dots = B.sb(st, "pdots", [128, 128]); dots2 = T(cw.ap.rearrange("p (a b) -> p a b", a=2))
                Xs = xin if l == 0 else X
                for ti in range(NTILE):
                    s_ = 0 if ti < 4 else 1; r0 = ti * 128
                    DMA(mt.ap, MIX[r0:r0 + 128, :], [MIX], [mt]); DMA(x1.ap, Xs[r0:r0 + 128, :], [Xs], [x1])
                    for j in range(8): TR(ps[j // 4][:, (j % 4) * 128:(j % 4 + 1) * 128], mt[:, j * 128:(j + 1) * 128], 128, [mt], [ps[j // 4]])
                    for hh in range(2): CP(mT[:, hh * 4:(hh + 1) * 4, :], ps[hh].ap.rearrange("p (a b) -> p a b", a=4), [ps[hh]], [mT], eng="act")
                    DMA(wo.ap, wout_d[l].rearrange("(j p) n -> p j n", p=128), [], [wo])
                    for nb in range(2):
                        for j in range(8): MM(ps[2 + nb].ap, mT[:, j, :], wo[:, j, nb * 512:(nb + 1) * 512], [mT, wo], [ps[2 + nb]], start=(j == 0), stop=(j == 7))
                        TT(junk[:, nb * 512:(nb + 1) * 512], ps[2 + nb].ap, MOD[s_][:, 2, nb * 512:(nb + 1) * 512], ALU.mult, [ps[2 + nb], MOD[s_]], [junk])
                    TT(x1.ap, x1.ap, junk.ap, ALU.add, [x1, junk], [x1])
                    ACT(junk.ap, x1.ap, AF.Square, [x1], [junk, sq], accum_out=sq[:, 0:1])
                    TS(sq[:, 1:2], sq[:, 0:1], 1.0 / D, EPS, ALU.mult, ALU.add, [sq], [sq])
                    ACT(sq[:, 1:2], sq[:, 1:2], AF.Sqrt, [sq], [sq])
                    k.op("dve", lambda e: e.reciprocal(out=sq[:, 1:2], in_=sq[:, 1:2]), reads=_r([sq]), writes=_r([sq]))
                    STT(h2.ap, x1.ap, sq[:, 1:2], MOD[s_][:, 4, :], ALU.mult, ALU.mult, [x1, sq, MOD[s_]], [h2])
                    TT(h2.ap, h2.ap, MOD[s_][:, 3, :], ALU.add, [h2, MOD[s_]], [h2])
                    if "H2" in debug and l == 0:
                        if ti == 0: H2d = B.out("H2", [NT, D])
                        DMA(H2d[r0:r0 + 128, :], h2.ap, [h2], [H2d])
                    for j in range(8): TR(ps[j // 4][:, (j % 4) * 128:(j % 4 + 1) * 128], h2[:, j * 128:(j + 1) * 128], 128, [h2], [ps[j // 4]])
                    for hh in range(2): CP(mT[:, hh * 4:(hh + 1) * 4, :], ps[hh].ap.rearrange("p (a b) -> p a b", a=4), [ps[hh]], [mT], eng="act")
                    DMA(wq.ap, wq_d[l].rearrange("(j p) n -> p j n", p=128), [], [wq])
                    for g_ in range(16):
                        p = ps[g_ // 4]
                        for j in range(8): MM(p[0:64, (g_ % 4) * 128:(g_ % 4 + 1) * 128], wq[:, j, g_ * 64:(g_ + 1) * 64], mT[:, j, :], [wq, mT], [p], start=(j == 0), stop=(j == 7))
                    for b4 in range(4): CP(qT[:, b4 * 4:(b4 + 1) * 4, :], ps[b4][0:64, :].rearrange("p (a b) -> p a b", a=4), [ps[b4]], [qT], eng=("act" if b4 % 2 else "dve"))
                    for g_ in range(16):
                        p = ps[g_ // 4]
                        MM(p[:, (g_ % 4) * 128:(g_ % 4 + 1) * 128], qT[:, g_, :], pk[:, g_ % 2, :], [qT, pk], [p])
                    for b4 in range(4): CP(sc[:, b4 * 4:(b4 + 1) * 4, :], ps[b4].ap.rearrange("p (a b) -> p a b", a=4), [ps[b4]], [sc], eng=("act" if b4 % 2 else "dve"))
                    for g_ in range(16):
                        k.op("dve", lambda e, g_=g_: e.max(out=v16[:, g_, 0:8], in_=sc[:, g_, :]), reads=_r([sc]), writes=_r([v16]))
                        k.op("dve", lambda e, g_=g_: e.max_index(out=i16[:, g_, 0:8], in_max=v16[:, g_, 0:8], in_values=sc[:, g_, :]), reads=_r([sc, v16]), writes=_r([i16]))
                        k.op("dve", lambda e, g_=g_: e.match_replace(out=scw.ap, in_to_replace=v16[:, g_, 0:8], in_values=sc[:, g_, :], imm_value=-1e30), reads=_r([sc, v16]), writes=_r([scw]))
                        k.op("dve", lambda e, g_=g_: e.max(out=v16[:, g_, 8:16], in_=scw.ap), reads=_r([scw]), writes=_r([v16]))
                        k.op("dve", lambda e, g_=g_: e.max_index(out=i16[:, g_, 8:16], in_max=v16[:, g_, 8:16], in_values=scw.ap), reads=_r([scw, v16]), writes=_r([i16]))
                    CP(i16f.ap, i16.ap, [i16], [i16f])
                    for hp in range(8):
                        a_v = v16[:, 2 * hp, :].unsqueeze(2).to_broadcast([128, 16, 16]); b_v = v16[:, 2 * hp + 1, :].unsqueeze(1).to_broadcast([128, 16, 16])
                        TT(cand[:, hp, :].rearrange("p (a b) -> p a b", a=16), a_v, b_v, ALU.add, [v16], [cand])
                        a_i = i16f[:, 2 * hp, :].unsqueeze(2).to_broadcast([128, 16, 16]); b_i = i16f[:, 2 * hp + 1, :].unsqueeze(1).to_broadcast([128, 16, 16])
                        STT(cidx[:, hp, :].rearrange("p (a b) -> p a b", a=16), a_i, 128.0, b_i, ALU.mult, ALU.add, [i16f], [cidx])
                    for hp in range(8):
                        k.op("dve", lambda e, hp=hp: e.max(out=s16[:, hp, 0:8], in_=cand[:, hp, :]), reads=_r([cand]), writes=_r([s16]))
                        k.op("dve", lambda e, hp=hp: e.match_replace(out=cw.ap, in_to_replace=s16[:, hp, 0:8], in_values=cand[:, hp, :], imm_value=-1e30), reads=_r([cand, s16]), writes=_r([cw]))
                        k.op("dve", lambda e, hp=hp: e.max(out=s16[:, hp, 8:16], in_=cw.ap), reads=_r([cw]), writes=_r([s16]))
                        k.op("dve", lambda e, hp=hp: e.max_index(out=p16[:, hp, 0:8], in_max=s16[:, hp, 0:8], in_values=cand[:, hp, :]), reads=_r([cand, s16]), writes=_r([p16]))
                        k.op("dve", lambda e, hp=hp: e.max_index(out=p16[:, hp, 8:16], in_max=s16[:, hp, 8:16], in_values=cw.ap), reads=_r([cw, s16]), writes=_r([p16]))
                        CP(p16f[:, hp, :], p16[:, hp, :], [p16], [p16f])
                        TT(eqm.ap, io256.ap.unsqueeze(1).to_broadcast([128, 16, 256]), p16f[:, hp, :].unsqueeze(2).to_broadcast([128, 16, 256]), ALU.is_equal, [io256, p16f], [eqm])
                        TT(eqm.ap, eqm.ap, cidx[:, hp, :].unsqueeze(1).to_broadcast([128, 16, 256]), ALU.mult, [eqm, cidx], [eqm])
                        k.op("dve", lambda e, hp=hp: e.tensor_reduce(out=ef[:, hp * 16:(hp + 1) * 16], in_=eqm.ap, op=ALU.add, axis=AX.X), reads=_r([eqm]), writes=_r([ef]))
                    gt3 = gt.ap.rearrange("p (a b) -> p a b", a=8)
                    TT(gt3, s16.ap, s16[:, :, 0:1].to_broadcast([128, 8, 16]), ALU.subtract, [s16], [gt])
                    ACT(gt.ap, gt.ap, AF.Exp, [gt], [gt])
                    k.op("dve", lambda e: e.tensor_reduce(out=g8.ap, in_=gt3, op=ALU.add, axis=AX.X), reads=_r([gt]), writes=_r([g8]))
                    k.op("dve", lambda e: e.reciprocal(out=g8.ap, in_=g8.ap), reads=_r([g8]), writes=_r([g8]))
                    TT(gt3, gt3, g8.ap.unsqueeze(2).to_broadcast([128, 8, 16]), ALU.mult, [gt, g8], [gt])
                    CP(eT.ap, ef.ap, [ef], [eT])
                    GS = 4
                    for nb in range(2): CP(ps[4 + nb].ap, h2[:, nb * 512:(nb + 1) * 512], [h2], [ps[4 + nb]], eng="act")
                    ps[4].res.excl = False; ps[5].res.excl = False
                    dres = [T(None) for _ in range(128)]
                    ACC = [(ps[6], ps[7]), (ps[0], ps[1]), (ps[2], ps[3])]
                    for g0 in range(0, 128, GS):
                        for j_ in range(g0, g0 + GS):
                            u_ = Ug[j_ % NGB]
                            k.dma(None, None, reads=_r([eT]), writes=_r([u_]), q="pool", fn=lambda e, u_=u_, j_=j_: e.indirect_dma_start(out=u_.ap, out_offset=None, in_=puv_d[l],
                                  in_offset=bass.IndirectOffsetOnAxis(ap=eT[:, j_:j_ + 1], axis=0)))
                            for nb in range(2):
                                k.op("dve", lambda e, u_=u_, j_=j_, nb=nb: e.scalar_tensor_tensor(out=junk[:, nb * 512:(nb + 1) * 512], in0=u_[:, nb * 512:(nb + 1) * 512], scalar=1.0, in1=ps[4 + nb].ap,
                                     op0=ALU.mult, op1=ALU.mult, accum_out=dots2[:, nb, j_:j_ + 1]), reads=_r([u_, ps[4 + nb]]), writes=_r([dres[j_]]))
                        TT(dots[:, g0:g0 + GS], dots2[:, 0, g0:g0 + GS], dots2[:, 1, g0:g0 + GS], ALU.add, dres[g0:g0 + GS], [dots])
                        ACT(dots[:, g0:g0 + GS], dots[:, g0:g0 + GS], AF.Gelu, [dots], [dots])
                        TT(dots[:, g0:g0 + GS], dots[:, g0:g0 + GS], gt[:, g0:g0 + GS], ALU.mult, [dots, gt], [dots])
                        for j_ in range(g0, g0 + GS):
                            u_ = Ug[j_ % NGB]; A_ = ACC[j_ % 3]
                            for nb in range(2):
                                vv = u_[:, D + nb * 512:D + (nb + 1) * 512]
                                if j_ < 3:
                                    TS(A_[nb].ap, vv, dots[:, j_:j_ + 1], None, ALU.mult, ALU.bypass, [u_, dots], [A_[nb]])
                                else:
                                    STT(A_[nb].ap, vv, dots[:, j_:j_ + 1], A_[nb].ap, ALU.mult, ALU.add, [u_, dots, A_[nb]], [A_[nb]])
                    ps[4].res.excl = True; ps[5].res.excl = True
                    for nb in range(2):
                        CP(junk[:, nb * 512:(nb + 1) * 512], ACC[1][nb].ap, [ACC[1][nb]], [junk])
                        TT(junk[:, nb * 512:(nb + 1) * 512], junk[:, nb * 512:(nb + 1) * 512], ACC[2][nb].ap, ALU.add, [junk, ACC[2][nb]], [junk])
                        TT(junk[:, nb * 512:(nb + 1) * 512], junk[:, nb * 512:(nb + 1) * 512], ACC[0][nb].ap, ALU.add, [junk, ACC[0][nb]], [junk])
                        TT(junk[:, nb * 512:(nb + 1) * 512], junk[:, nb * 512:(nb + 1) * 512], MOD[s_][:, 5, nb * 512:(nb + 1) * 512], ALU.mult, [junk, MOD[s_]], [junk])
                    TT(x1.ap, x1.ap, junk.ap, ALU.add, [x1, junk], [x1])
                    DMA(X[r0:r0 + 128, :], x1.ap, [x1], [X])
              k.barrier()

            if stop_after in (("C", l), ("E", l)):
                break
        with contextlib.ExitStack() as st:
            fn = B.sb(st, "fn", [128, D]); DMA(fn.ap, fin_d.ap, [], [fn])
            xt = B.sb(st, "fxt", [128, D]); junk = B.sb(st, "fjunk", [128, D]); sq = B.sb(st, "fsq", [128, 2])
            for ti in range(NTILE):
                r0 = ti * 128
                DMA(xt.ap, X[r0:r0 + 128, :], [X], [xt])
                ACT(junk.ap, xt.ap, AF.Square, [xt], [junk, sq], accum_out=sq[:, 0:1])
                TS(sq[:, 1:2], sq[:, 0:1], 1.0 / D, EPS, ALU.mult, ALU.add, [sq], [sq])
                ACT(sq[:, 1:2], sq[:, 1:2], AF.Sqrt, [sq], [sq])
                k.op("dve", lambda e: e.reciprocal(out=sq[:, 1:2], in_=sq[:, 1:2]), reads=_r([sq]), writes=_r([sq]))
                STT(junk.ap, xt.ap, sq[:, 1:2], fn.ap, ALU.mult, ALU.mult, [xt, sq, fn], [junk])
                DMA(yp[r0:r0 + 128, :], junk.ap, [junk], [yp])
    k.finish()
    return B


def host_inputs(inputs, core):
    f = np.float32
    I = {k_: np.asarray(v) for k_, v in inputs.items()}
    b = core if core < 2 else core % 2
    xin = np.concatenate([I["x_prompt"][2 * core].astype(f), I["x_prompt"][2 * core + 1].astype(f), I["x_sample"][b].astype(f)], 0)
    cv = np.stack([I["c_ctx"], I["c"][b]], 0).astype(f)
    cvecs = cv.reshape(2, 8, 128).transpose(2, 0, 1).copy()
    m = {"xin": xin, "cvecs": cvecs}
    for nm, key in (("s_rw", "state_rwkv"), ("s_ss", "state_ssm"), ("s_mc", "state_mlstm_C"), ("s_mn", "state_mlstm_n"), ("s_mm", "state_mlstm_m"), ("s_lr", "state_lru")):
        m[nm] = np.ascontiguousarray(I[key][b].astype(f))
    m["s_mm_rep"] = np.ascontiguousarray(np.broadcast_to(m["s_mm"][:, :, None, :], (L_, 2, 64, 4)))
    return m


def host_shared(inputs):
    f = np.float32
    I = {k_: np.asarray(v).astype(f) if np.asarray(v).dtype != np.int32 else np.asarray(v) for k_, v in inputs.items()}
    rep = lambda a: np.ascontiguousarray(np.broadcast_to(a[:, None, ...], (a.shape[0], 128) + a.shape[1:]))
    m = {}
    m["ident"] = np.eye(128, dtype=f)
    m["mod_w"] = I["mod_w"]; m["mod_b_rep"] = rep(I["mod_b"])
    m["norm_rep"] = rep(np.stack([I["norm1"], I["norm2"]], 1))
    m["w_ext"] = np.ascontiguousarray(np.concatenate([I["w_in"], I["rw_wA"][:, 0], I["rw_wA"][:, 1], I["rw_aA"][:, 0], I["rw_aA"][:, 1]], axis=2))
    wB = np.zeros((L_, 128, 2, 256), f)
    wB[:, 0:64, 0] = I["rw_wB"][:, 0]; wB[:, 64:128, 0] = I["rw_wB"][:, 1]; wB[:, 0:64, 1] = I["rw_aB"][:, 0]; wB[:, 64:128, 1] = I["rw_aB"][:, 1]
    m["wB_sb"] = wB
    m["w0a0_rep"] = rep(np.concatenate([I["rw_w0"][:, 0], I["rw_w0"][:, 1], I["rw_a0"][:, 0], I["rw_a0"][:, 1]], 1))
    cvw = np.concatenate([I["ssm_conv_w"], I["lru_conv_w"]], axis=3).reshape(L_, 9, 768)
    m["cvw_rep"] = rep(cvw); m["cvb_rep"] = rep(np.concatenate([I["ssm_conv_b"], I["lru_conv_b"]], 1))
    tri = np.triu(np.ones((64, 64), f))
    m["tri"] = np.stack([tri, tri.T.copy()], 0)
    m["jmat"] = np.eye(128, dtype=f)[::-1].copy()
    incl = m["tri"]; strict = incl - np.eye(64, dtype=f)[None]
    m["rwm"] = np.ascontiguousarray(np.concatenate([np.concatenate([strict, incl], 2)] * 2, 1))
    t2_ = np.zeros((2, 128, 128), f); t2_[:, 0:64, 0:64] = incl; t2_[:, 64:128, 64:128] = incl; m["tri2"] = t2_
    m["rws"] = np.ascontiguousarray(np.concatenate([strict] * 2, 1))
    lw = np.zeros((L_, 2, 2, 2, 128, 128), f)
    for cb in range(2):
        for hh in range(2):
            h = cb * 2 + hh
            lw[:, :, 0, cb, hh * 64:(hh + 1) * 64, hh * 64:(hh + 1) * 64] = I["lru_wr"][:, :, h]
            lw[:, :, 1, cb, hh * 64:(hh + 1) * 64, hh * 64:(hh + 1) * 64] = I["lru_wi"][:, :, h]
    m["lruw"] = lw
    br = I["lru_br"].reshape(L_, 2, 256); bi = I["lru_bi"].reshape(L_, 2, 256); lam = I["lru_lam"]
    pb = np.stack([br, bi, lam], -1)
    m["lrub"] = np.ascontiguousarray(pb.reshape(L_, 2, 2, 128, 3).transpose(0, 3, 2, 1, 4))
    m["ssp_rep"] = rep(np.concatenate([I["ssm_dt_bias"].reshape(L_, 8), I["ssm_A_log"].reshape(L_, 8)], 1))
    m["w_out"] = I["w_out"]; m["peer_wq"] = I["peer_wq"];
    for i in range(L_):
        m[f"peer_uv{i}"] = np.ascontiguousarray(np.concatenate([I["peer_u"][i], I["peer_v"][i]], axis=1))
    m["peer_kT"] = np.ascontiguousarray(np.stack([I["peer_k1"], I["peer_k2"]], 1).transpose(0, 1, 3, 2))
    m["fin_rep"] = np.ascontiguousarray(np.broadcast_to(I["final_norm"][None, :], (128, D)))
    m["iota256"] = np.ascontiguousarray(np.broadcast_to(np.arange(256, dtype=f)[None, :], (128, 256)))
    m["iota128"] = np.ascontiguousarray(np.broadcast_to(np.arange(128, dtype=f)[None, :], (128, 128)))
    m["rwp_rep"] = rep(np.stack([I["rw_kk"], I["rw_ka"], I["rw_rk"].reshape(L_, 256), I["rw_ln"]], 1))
    m["mlp_rep"] = rep(np.concatenate([I["ml_ib"].reshape(L_, 8), I["ml_fb"].reshape(L_, 8)], 1)); m["mln_rep"] = rep(I["ml_norm"])
    m["ssD_rep"] = rep(I["ssm_D"]); m["ssn_rep"] = rep(I["ssm_norm"])
    p = np.arange(128)
    m["cmask"] = np.stack([(p % 64 != 0), (p % 64 != 63)], 1).astype(f)
    return m


_CACHE = {}

def kernel(**inputs):
    if "B" not in _CACHE:
        _CACHE["B"] = build()
    B = _CACHE["B"]
    shared = host_shared(inputs)
    in_maps = []
    for c in range(8):
        m = dict(shared); m.update(host_inputs(inputs, c)); in_maps.append(m)
    res = run_bass_kernel_spmd(B.nc, in_maps, core_ids=list(range(8)))
    R = res.results
    f = np.float32
    y_prompt = np.stack([R[c]["y_tok"][i * 256:(i + 1) * 256] for c in range(8) for i in range(2)], 0).astype(f)
    y_sample = np.stack([R[b]["y_tok"][NP:] for b in range(2)], 0).astype(f)
    cat = lambda nm: np.concatenate([R[c][nm] for c in range(8)], 0).astype(f)
    return (y_prompt, y_sample, cat("st_rwkv"), cat("st_ssm"), cat("st_mlC"), cat("st_mln"), cat("st_mlm"), cat("st_lru"))
```

```python
import contextlib
import math
import numpy as np
import concourse.bass as bass
import concourse.mybir as mybir
from concourse.bass_utils import run_bass_kernel_spmd

F32 = mybir.dt.float32; I32 = mybir.dt.int32; U32 = mybir.dt.uint32
AF = mybir.ActivationFunctionType; ALU = mybir.AluOpType; AX = mybir.AxisListType

D = 1024; L_ = 2; NP = 512; NS = 4096; NT = NP + NS; NTILE = NT // 128
DIN = 3352; DEXT = 3608; UW = 4376
C_RWR, C_RWK, C_RWV, C_RWG = 0, 256, 512, 768
C_SSX, C_SSBC, C_SSZ, C_SSDT = 1024, 1280, 1536, 1792
C_MLQ, C_MLK, C_MLV, C_MLO, C_MLI, C_MLF = 1800, 2056, 2312, 2568, 2824, 2832
C_LRX, C_LRG = 2840, 3096
C_DEC, C_A = 3352, 3864
EPS = 1e-6
CV_P0, CV_P1, CV_S = 1, 258, 640
CV_ROWS = 4864


class Res:
    __slots__ = ("w", "r", "excl")
    def __init__(self):
        self.w = None; self.r = {}; self.excl = False


class K:
    def __init__(self, nc, ndma=24):
        self.nc = nc
        self.eng = {"pe": nc.tensor, "act": nc.scalar, "dve": nc.vector, "pool": nc.gpsimd, "sp": nc.sync}
        self.sem = {e: nc.alloc_semaphore(name="c_" + e) for e in self.eng}
        self.cnt = {e: 0 for e in self.eng}
        self.seen = {e: {} for e in self.eng}
        self.rings = {}
        for q in ("sp", "pool"):
            n = ndma if q == "sp" else 8
            self.rings[q] = dict(sems=[nc.alloc_semaphore(name=f"d_{q}{i}") for i in range(n)], uses=[0] * n, nxt=0)
        self.ninst = 0
    def _wait(self, e, ev):
        if ev is None: return
        s, v = ev
        if self.seen[e].get(s.num, 0) >= v: return
        self.seen[e][s.num] = v
        self.eng[e].wait_ge(s, v); self.ninst += 1
    def _deps(self, e, reads, writes):
        for r in reads: self._wait(e, r.w)
        for w in writes:
            self._wait(e, w.w)
            for ev in w.r.values(): self._wait(e, ev)
    def _commit(self, ev, reads, writes):
        for r in reads: r.r[ev[0].num] = ev
        for w in writes:
            w.w = ev; w.r = {}
    def op(self, e, fn, reads=(), writes=()):
        ex = [r for r in reads if r.excl]
        if ex:
            reads = [r for r in reads if not r.excl]; writes = list(writes) + ex
        self._deps(e, reads, writes)
        inst = fn(self.eng[e])
        self.cnt[e] += 1; self.ninst += 1
        inst.then_inc(self.sem[e], 1)
        ev = (self.sem[e], self.cnt[e])
        self._commit(ev, reads, writes)
        return ev
    def dma(self, out, in_, reads=(), writes=(), q="sp", fn=None):
        self._deps(q, reads, writes)
        ring = self.rings[q]; i = ring["nxt"]; ring["nxt"] = (i + 1) % len(ring["sems"])
        s = ring["sems"][i]
        if ring["uses"][i] > 0: self._wait(q, (s, 16 * ring["uses"][i]))
        inst = self.eng[q].dma_start(out=out, in_=in_) if fn is None else fn(self.eng[q])
        inst.then_inc(s, 16); ring["uses"][i] += 1; self.ninst += 1
        ev = (s, 16 * ring["uses"][i])
        self._commit(ev, reads, writes)
        return ev
    def barrier(self):
        evs = [(self.sem[f], self.cnt[f]) for f in self.eng if self.cnt[f]]
        for q, ring in self.rings.items():
            evs += [(s_, 16 * u) for s_, u in zip(ring["sems"], ring["uses"]) if u]
        for e in self.eng:
            for ev in evs:
                if ev[0] is not self.sem[e]: self._wait(e, ev)
    def finish(self):
        for q, ring in self.rings.items():
            for s, u in zip(ring["sems"], ring["uses"]):
                if u: self._wait("sp", (s, 16 * u))
        for e in self.eng:
            if self.cnt[e]: self._wait("sp", (self.sem[e], self.cnt[e]))


class T:
    def __init__(self, ap):
        self.ap = ap; self.res = Res()
    def __getitem__(self, idx):
        return self.ap[idx]


class Builder:
    def __init__(self, debug=()):
        self.nc = nc = bass.Bass("TRN2", target_bir_lowering=False)
        self.k = K(nc)
        self.debug = debug
        self.ins = {}; self.outs = {}
        self.ps = [T(nc.alloc_psum_tensor(f"ps{i}", [128, 512], F32).ap()) for i in range(8)]
        for p in self.ps: p.res.excl = True
        self.stack = contextlib.ExitStack()
    def inp(self, name, shape, dt=F32):
        t = T(self.nc.dram_tensor(name, list(shape), dt, kind="ExternalInput").ap()); self.ins[name] = t; return t
    def out(self, name, shape, dt=F32):
        t = T(self.nc.dram_tensor(name, list(shape), dt, kind="ExternalOutput").ap()); self.outs[name] = t; return t
    def scratch(self, name, shape, dt=F32):
        if name in self.debug: return self.out(name, shape, dt)
        return T(self.nc.dram_tensor(name, list(shape), dt, kind="Internal").ap())
    def sb(self, st, name, shape, dt=F32):
        self._n = getattr(self, "_n", 0) + 1
        name = f"{name}_{self._n}"
        return T(st.enter_context(self.nc.sbuf_tensor(name, list(shape), dt)).ap())


def build(debug=(), stop_after=None, mixers=("lru", "ssd", "mlstm", "rwkv")):
    B = Builder(debug); nc = B.nc; k = B.k; ps = B.ps
    xin = B.inp("xin", [NT, D])
    cvecs = B.inp("cvecs", [128, 2, 8])
    ident_d = B.inp("ident", [128, 128])
    mod_w = B.inp("mod_w", [L_, D, 6 * D]); mod_b = B.inp("mod_b_rep", [L_, 128, 6 * D])
    norm_rep = B.inp("norm_rep", [L_, 128, 2, D])
    w_ext = B.inp("w_ext", [L_, D, DEXT])
    wB_d = B.inp("wB_sb", [L_, 128, 2, 256]); w0a0_d = B.inp("w0a0_rep", [L_, 128, 1024])
    cvw_d = B.inp("cvw_rep", [L_, 128, 9, 768]); cvb_d = B.inp("cvb_rep", [L_, 128, 768])
    cmask_d = B.inp("cmask", [128, 2])
    X = B.scratch("X", [NT, D]); U = B.scratch("U", [NT, UW]); CVI = B.scratch("CVI", [CV_ROWS, 768])
    CVO = B.scratch("CVO", [NT, 768])
    yp = B.out("y_tok", [NT, D])
    MIX = B.scratch("MIX", [NT, D]); YF = B.scratch("YF", [NT, 512]); YB = B.scratch("YB", [NT, 512])
    o_rw = B.out("st_rwkv", [2, L_, 2, 4, 64, 64]); o_ss = B.out("st_ssm", [2, L_, 2, 4, 64, 64]); o_mc = B.out("st_mlC", [2, L_, 2, 4, 64, 64])
    o_mn = B.out("st_mln", [2, L_, 2, 4, 64]); o_mm = B.out("st_mlm", [2, L_, 2, 4]); o_lr = B.out("st_lru", [2, L_, 2, 256])
    s_rw = B.inp("s_rw", [L_, 2, 4, 64, 64]); s_ss = B.inp("s_ss", [L_, 2, 4, 64, 64]); s_mc = B.inp("s_mc", [L_, 2, 4, 64, 64])
    s_mn = B.inp("s_mn", [L_, 2, 4, 64]); s_mm = B.inp("s_mm", [L_, 2, 4]); s_lr = B.inp("s_lr", [L_, 2, 256])
    tri_d = B.inp("tri", [2, 64, 64]); jmat_d = B.inp("jmat", [128, 128])
    lruw_d = B.inp("lruw", [L_, 2, 2, 2, 128, 128]); lrub_d = B.inp("lrub", [L_, 128, 2, 2, 3])
    rwp_d = B.inp("rwp_rep", [L_, 128, 4, 256]); jm_d = jmat_d
    rwm_d = B.inp("rwm", [2, 128, 128]); rws_d = B.inp("rws", [2, 128, 64]); tri2_d = B.inp("tri2", [2, 128, 128])
    wout_d = B.inp("w_out", [L_, D, D]); wq_d = B.inp("peer_wq", [L_, D, D]); pk_d = B.inp("peer_kT", [L_, 2, 64, 128])
    puv_d = [B.inp(f"peer_uv{i}", [16384, 2 * D]).ap for i in range(L_)]; fin_d = B.inp("fin_rep", [128, D]); iota_d = B.inp("iota128", [128, 128]); iota256_d = B.inp("iota256", [128, 256])
    RQ = B.scratch("RQ", [NT, 1536]); YR = [B.scratch(f"YR{d}", [NT, 256]) for d in range(2)]
    mlp_d = B.inp("mlp_rep", [L_, 128, 16]); mln_d = B.inp("mln_rep", [L_, 128, 256]); smm_d = B.inp("s_mm_rep", [L_, 2, 64, 4])
    ssp_d = B.inp("ssp_rep", [L_, 128, 16]); ssD_d = B.inp("ssD_rep", [L_, 128, 4]); ssn_d = B.inp("ssn_rep", [L_, 128, 256])


    def _r(ts): return [t.res for t in ts]
    def TT(o, a, b, op, rd, wr, eng="dve"): k.op(eng, lambda e: e.tensor_tensor(out=o, in0=a, in1=b, op=op), reads=_r(rd), writes=_r(wr))
    def TS(o, a, s1, s2, op0, op1, rd, wr): k.op("dve", lambda e: e.tensor_scalar(out=o, in0=a, scalar1=s1, scalar2=s2, op0=op0, op1=op1), reads=_r(rd), writes=_r(wr))
    def STT(o, a, sc, b, op0, op1, rd, wr): k.op("dve", lambda e: e.scalar_tensor_tensor(out=o, in0=a, scalar=sc, in1=b, op0=op0, op1=op1), reads=_r(rd), writes=_r(wr))
    def ACT(o, a, func, rd, wr, **kw): k.op("act", lambda e: e.activation(out=o, in_=a, func=func, **kw), reads=_r(rd), writes=_r(wr))
    def MM(o, lhsT, rhs, rd, wr, start=True, stop=True): k.op("pe", lambda e: e.matmul(o, lhsT=lhsT, rhs=rhs, start=start, stop=stop), reads=_r(rd), writes=_r(wr))
    def CP(o, a, rd, wr, eng="dve"):
        if eng == "act": k.op("act", lambda e: e.copy(out=o, in_=a), reads=_r(rd), writes=_r(wr))
        else: k.op(eng, lambda e: e.tensor_copy(out=o, in_=a), reads=_r(rd), writes=_r(wr))
    def DMA(o, a, rd, wr): k.dma(o, a, reads=_r(rd), writes=_r(wr))
    def rows(A, base, t0, n, cm, c0, c1):
        if not cm: return [(0, n, A[base + t0:base + t0 + n, c0:c1])]
        out = []
        for i in range(n // 64):
            c = t0 // 64 + i
            out.append((64 * i, 64, A[base + c:base + c + 63 * 64 + 1:64, c0:c1]))
        return out
    def LOADR(t, tap, A, base, t0, n, cm, c0, c1):
        for (p0, pn, ap) in rows(A, base, t0, n, cm, c0, c1): DMA(tap(p0, pn), ap, [A], [t])
    def STORER(A, base, t0, n, cm, c0, c1, t, tap):
        for (p0, pn, ap) in rows(A, base, t0, n, cm, c0, c1): DMA(ap, tap(p0, pn), [t], [A])
    SEQS = [(0, 256, False, 0), (256, 256, False, 1), (512, 4096, True, None)]

    with contextlib.ExitStack() as g:
        ident = B.sb(g, "identsb", [128, 128]); k.dma(ident.ap, ident_d.ap, writes=[ident.res])
        def TR(o, a, n, rd, wr): k.op("pe", lambda e: e.transpose(o, a, ident[0:n, 0:n]), reads=_r(rd) + [ident.res], writes=_r(wr))
        TRI = B.sb(g, "TRI", [64, 2, 64])
        for d in range(2): k.dma(TRI[:, d, :], tri_d[d], writes=[TRI.res])
        MOD = [B.sb(g, f"MOD{s}", [128, 6, D]) for s in range(2)]
        with contextlib.ExitStack() as zs:
            zt = B.sb(zs, "zt", [128, 768]); k.op("dve", lambda e: e.memset(zt.ap, 0.0), writes=[zt.res])
            for r0 in range(0, CV_ROWS, 128):
                k.dma(CVI[r0:r0 + 128, :], zt.ap, reads=[zt.res], writes=[CVI.res])
        k.barrier()

        for l in range(L_):
            Xsrc = xin if l == 0 else X
            with contextlib.ExitStack() as st:
                cv = B.sb(st, "cv", [128, 2, 8]); k.dma(cv.ap, cvecs.ap, writes=[cv.res])
                k.op("act", lambda e: e.activation(out=cv.ap, in_=cv.ap, func=AF.Silu), reads=[cv.res], writes=[cv.res])
                wb = [B.sb(st, f"modwb{i}", [128, 8, 512]) for i in range(2)]
                mb = B.sb(st, "modb", [128, 6 * D]); k.dma(mb.ap, mod_b[l], writes=[mb.res])
                nr = B.sb(st, "nr", [128, 2, D]); k.dma(nr.ap, norm_rep[l], writes=[nr.res])
                mw = mod_w[l].rearrange("(j p) n -> p j n", p=128)
                for nb in range(12):
                    w = wb[nb % 2]
                    k.dma(w.ap, mw[:, :, nb * 512:(nb + 1) * 512], writes=[w.res])
                    for s in range(2):
                        p = ps[(nb * 2 + s) % 4]
                        for j in range(8):
                            k.op("pe", lambda e, j=j, s=s, p=p, w=w: e.matmul(p.ap, lhsT=cv[:, s, j:j + 1].to_broadcast([128, 128]),
                                 rhs=w[:, j, :], start=(j == 0), stop=(j == 7)), reads=[cv.res, w.res], writes=[p.res])
                        sec, off = divmod(nb * 512, D)
                        k.op("dve", lambda e, s=s, p=p, sec=sec, off=off, nb=nb: e.tensor_tensor(out=MOD[s][:, sec, off:off + 512], in0=p.ap,
                             in1=mb[:, nb * 512:(nb + 1) * 512], op=ALU.add), reads=[p.res, mb.res], writes=[MOD[s].res])
                for s in range(2):
                    for sec, ni in ((1, 0), (4, 1)):
                        k.op("dve", lambda e, s=s, sec=sec, ni=ni: e.scalar_tensor_tensor(out=MOD[s][:, sec, :], in0=MOD[s][:, sec, :], scalar=1.0,
                             in1=nr[:, ni, :], op0=ALU.add, op1=ALU.mult), reads=[MOD[s].res, nr.res], writes=[MOD[s].res])
            k.barrier()
            with contextlib.ExitStack() as st:
                xt = B.sb(st, "xt", [128, D]); ht = B.sb(st, "ht", [128, D]); hT = B.sb(st, "hT", [128, 8, 128])
                ut = B.sb(st, "ut", [128, UW]); sq = B.sb(st, "sq", [128, 2]); junk = B.sb(st, "junk", [128, D])
                wb = [B.sb(st, f"winb{i}", [128, 8, 512]) for i in range(2)]
                LT = B.sb(st, "LT", [128, 2, 128]); wBs = B.sb(st, "wBs", [128, 2, 256]); w0a0 = B.sb(st, "w0a0", [128, 1024])
                k.dma(wBs.ap, wB_d[l], writes=[wBs.res]); k.dma(w0a0.ap, w0a0_d[l], writes=[w0a0.res])
                wx = w_ext[l].rearrange("(j p) n -> p j n", p=128)
                nwb = 0
                for ti in range(NTILE):
                    s = 0 if ti < NP // 128 else 1
                    k.dma(xt.ap, Xsrc[ti * 128:(ti + 1) * 128, :], reads=[Xsrc.res], writes=[xt.res])
                    k.op("act", lambda e: e.activation(out=junk.ap, in_=xt.ap, func=AF.Square, accum_out=sq[:, 0:1]), reads=[xt.res], writes=[junk.res, sq.res])
                    k.op("dve", lambda e: e.tensor_scalar(out=sq[:, 1:2], in0=sq[:, 0:1], scalar1=1.0 / D, scalar2=EPS, op0=ALU.mult, op1=ALU.add), reads=[sq.res], writes=[sq.res])
                    k.op("act", lambda e: e.activation(out=sq[:, 1:2], in_=sq[:, 1:2], func=AF.Sqrt), reads=[sq.res], writes=[sq.res])
                    k.op("dve", lambda e: e.reciprocal(out=sq[:, 1:2], in_=sq[:, 1:2]), reads=[sq.res], writes=[sq.res])
                    k.op("dve", lambda e, s=s: e.scalar_tensor_tensor(out=ht.ap, in0=xt.ap, scalar=sq[:, 1:2], in1=MOD[s][:, 1, :], op0=ALU.mult, op1=ALU.mult),
                         reads=[xt.res, sq.res, MOD[s].res], writes=[ht.res])
                    k.op("dve", lambda e, s=s: e.tensor_tensor(out=ht.ap, in0=ht.ap, in1=MOD[s][:, 0, :], op=ALU.add), reads=[ht.res, MOD[s].res], writes=[ht.res])
                    if "HDBG" in debug and ti == 0 and l == 0:
                        hd = B.out("HDBG", [128, D + 2])
                        k.dma(hd[:, 0:D], ht.ap, reads=[ht.res], writes=[hd.res]); k.dma(hd[:, D:D + 2], sq.ap, reads=[sq.res], writes=[hd.res])
                        md = B.out("MDBG", [128, 6, D]); k.dma(md.ap, MOD[0].ap, reads=[MOD[0].res], writes=[md.res])
                    for j in range(8):
                        p = ps[j // 4]
                        k.op("pe", lambda e, j=j, p=p: e.transpose(p[:, (j % 4) * 128:(j % 4 + 1) * 128], ht[:, j * 128:(j + 1) * 128], ident.ap),
                             reads=[ht.res, ident.res], writes=[p.res])
                    for hh in range(2):
                        k.op("act", lambda e, hh=hh: e.copy(out=hT[:, hh * 4:(hh + 1) * 4, :], in_=ps[hh].ap.rearrange("p (a b) -> p a b", a=4)),
                             reads=[ps[hh].res], writes=[hT.res])
                    for nb in range(8):
                        c0 = nb * 512; cw = min(512, DEXT - c0)
                        w = wb[nwb % 2]; p = ps[2 + nwb % 2]; nwb += 1
                        k.dma(w[:, :, 0:cw], wx[:, :, c0:c0 + cw], writes=[w.res])
                        for j in range(8):
                            k.op("pe", lambda e, j=j, p=p, w=w, cw=cw: e.matmul(p[:, 0:cw], lhsT=hT[:, j, :], rhs=w[:, j, 0:cw], start=(j == 0), stop=(j == 7)),
                                 reads=[hT.res, w.res], writes=[p.res])
                        if c0 + cw <= DIN or c0 >= DIN:
                            segs = [(c0, cw, 0)]
                        else:
                            segs = [(c0, DIN - c0, 0), (DIN, c0 + cw - DIN, DIN - c0)]
                        for (a0, aw, po) in segs:
                            eng = "dve" if nb % 2 == 0 else "act"
                            if eng == "dve":
                                k.op("dve", lambda e, a0=a0, aw=aw, po=po, p=p: e.tensor_copy(out=ut[:, a0:a0 + aw], in_=p[:, po:po + aw]), reads=[p.res], writes=[ut.res])
                            else:
                                k.op("act", lambda e, a0=a0, aw=aw, po=po, p=p: e.copy(out=ut[:, a0:a0 + aw], in_=p[:, po:po + aw]), reads=[p.res], writes=[ut.res])
                    k.op("act", lambda e: e.activation(out=ut[:, DIN:DIN + 128], in_=ut[:, DIN:DIN + 128], func=AF.Tanh), reads=[ut.res], writes=[ut.res])
                    for q in range(2):
                        k.op("pe", lambda e, q=q: e.transpose(ps[4][:, q * 128:(q + 1) * 128], ut[:, DIN + q * 128:DIN + (q + 1) * 128], ident.ap),
                             reads=[ut.res, ident.res], writes=[ps[4].res])
                    k.op("dve", lambda e: e.tensor_copy(out=LT.ap, in_=ps[4][:, 0:256].rearrange("p (a b) -> p a b", a=2)), reads=[ps[4].res], writes=[LT.res])
                    for q in range(2):
                        for d in range(2):
                            p = ps[5 + q]
                            k.op("pe", lambda e, q=q, d=d, p=p: e.matmul(p[:, d * 256:(d + 1) * 256], lhsT=LT[64 * d:64 * d + 64, q, :], rhs=wBs[64 * d:64 * d + 64, q, :],
                                 start=True, stop=True), reads=[LT.res, wBs.res], writes=[p.res])
                    for q in range(2):
                        k.op("dve", lambda e, q=q: e.tensor_tensor(out=ut[:, C_DEC + q * 512:C_DEC + (q + 1) * 512], in0=ps[5 + q].ap, in1=w0a0[:, q * 512:(q + 1) * 512], op=ALU.add),
                             reads=[ps[5 + q].res, w0a0.res], writes=[ut.res])
                    k.op("act", lambda e: e.activation(out=ut[:, C_DEC:C_DEC + 1024], in_=ut[:, C_DEC:C_DEC + 1024], func=AF.Sigmoid), reads=[ut.res], writes=[ut.res])
                    k.op("dve", lambda e: e.tensor_scalar(out=ut[:, C_DEC:C_DEC + 512], in0=ut[:, C_DEC:C_DEC + 512], scalar1=-math.exp(-0.5), scalar2=None, op0=ALU.mult), reads=[ut.res], writes=[ut.res])
                    k.dma(U[ti * 128:(ti + 1) * 128, :], ut.ap, reads=[ut.res], writes=[U.res])
                    if s == 0:
                        cr = (CV_P0 if ti < 2 else CV_P1) + (ti % 2) * 128
                    else:
                        cr = CV_S + (ti - 4) * 128
                    k.dma(CVI[cr:cr + 128, 0:512], ut[:, C_SSX:C_SSX + 512], reads=[ut.res], writes=[CVI.res])
                    k.dma(CVI[cr:cr + 128, 512:768], ut[:, C_LRX:C_LRX + 256], reads=[ut.res], writes=[CVI.res])
            k.barrier()
            with contextlib.ExitStack() as st:
                cw9 = B.sb(st, "cw9", [128, 9, 768]); cb = B.sb(st, "cb", [128, 768]); cm = B.sb(st, "cm", [128, 2])
                k.dma(cw9.ap, cvw_d[l], writes=[cw9.res]); k.dma(cb.ap, cvb_d[l], writes=[cb.res]); k.dma(cm.ap, cmask_d.ap, writes=[cm.res])
                sh = [B.sb(st, f"sh{i}", [128, 768]) for i in range(3)]
                acc = B.sb(st, "cacc", [128, 768]); tmp = B.sb(st, "ctmp", [128, 768])
                nsh = 0
                for ti in range(NTILE):
                    s = 0 if ti < 4 else 1
                    if s == 0:
                        cr = (CV_P0 if ti < 2 else CV_P1) + (ti % 2) * 128
                        taps = [(1, 0), (1, 1), (1, 2)]
                    else:
                        cr = CV_S + (ti - 4) * 128
                        taps = [(i, j) for i in range(3) for j in range(3)]
                    first = True
                    for (i, j) in taps:
                        off = (i - 1) * 64 + (j - 1) if s == 1 else (j - 1)
                        t_ = sh[nsh % 3]; nsh += 1
                        k.dma(t_.ap, CVI[cr + off:cr + off + 128, :], reads=[CVI.res], writes=[t_.res])
                        if first:
                            k.op("dve", lambda e, t_=t_, i=i, j=j: e.tensor_tensor(out=acc.ap, in0=t_.ap, in1=cw9[:, i * 3 + j, :], op=ALU.mult), reads=[t_.res, cw9.res], writes=[acc.res])
                            if s == 1 and j != 1:
                                k.op("dve", lambda e, j=j: e.tensor_scalar(out=acc.ap, in0=acc.ap, scalar1=cm[:, j // 2:j // 2 + 1], scalar2=None, op0=ALU.mult), reads=[acc.res, cm.res], writes=[acc.res])
                            first = False
                        else:
                            k.op("dve", lambda e, t_=t_, i=i, j=j: e.tensor_tensor(out=tmp.ap, in0=t_.ap, in1=cw9[:, i * 3 + j, :], op=ALU.mult), reads=[t_.res, cw9.res], writes=[tmp.res])
                            if s == 1 and j != 1:
                                k.op("dve", lambda e, j=j: e.scalar_tensor_tensor(out=acc.ap, in0=tmp.ap, scalar=cm[:, j // 2:j // 2 + 1], in1=acc.ap, op0=ALU.mult, op1=ALU.add),
                                     reads=[tmp.res, cm.res, acc.res], writes=[acc.res])
                            else:
                                k.op("dve", lambda e: e.tensor_tensor(out=acc.ap, in0=tmp.ap, in1=acc.ap, op=ALU.add), reads=[tmp.res, acc.res], writes=[acc.res])
                    k.op("dve", lambda e: e.tensor_tensor(out=acc.ap, in0=acc.ap, in1=cb.ap, op=ALU.add), reads=[acc.res, cb.res], writes=[acc.res])
                    k.op("act", lambda e: e.activation(out=acc[:, 0:512], in_=acc[:, 0:512], func=AF.Silu), reads=[acc.res], writes=[acc.res])
                    k.dma(CVO[ti * 128:(ti + 1) * 128, :], acc.ap, reads=[acc.res], writes=[CVO.res])
            k.barrier()
            if stop_after == ("B", l):
                break

            if "lru" in mixers:
              with contextlib.ExitStack() as st:
                lw = B.sb(st, "lw", [128, 2, 2, 2, 128]); lb = B.sb(st, "lb", [128, 2, 2, 3]); nsp = B.sb(st, "nsp", [128, 2, 2])
                for d in range(2):
                    for q in range(2):
                        for cb in range(2): DMA(lw[:, d, q, cb, :], lruw_d[l, d, q, cb], [], [lw])
                DMA(lb.ap, lrub_d[l], [], [lb])
                ACT(nsp.ap, lb[:, :, :, 2], AF.Exp, [lb], [nsp], scale=-1.0)
                ACT(nsp.ap, nsp.ap, AF.Ln, [nsp], [nsp], bias=1.0)
                TS(nsp.ap, nsp.ap, -8.0, None, ALU.mult, ALU.bypass, [nsp], [nsp])
                xT = B.sb(st, "xT", [128, 4096]); aT = B.sb(st, "aT", [128, 4096]); xiT = B.sb(st, "xiT", [128, 4096])
                hT = [B.sb(st, f"hT{d}", [128, 4096]) for d in range(2)]
                rT = B.sb(st, "rT", [128, 512]); iT = B.sb(st, "iT", [128, 512]); tmq = B.sb(st, "tmq", [128, 512])
                lt = [B.sb(st, f"lt{i}", [128, 128]) for i in range(2)]; gt = [B.sb(st, f"gt{i}", [128, 128]) for i in range(2)]
                ot = [B.sb(st, f"ot{i}", [128, 128]) for i in range(2)]; h0 = B.sb(st, "h0", [128, 2])
                for (base, Tn, samp, sidx) in SEQS:
                    cm = samp and (l % 2 == 1)
                    for cb in range(2):
                        for blk in range(Tn // 128):
                            t_ = lt[blk % 2]; p = ps[blk % 2]
                            LOADR(t_, lambda p0, pn: t_[p0:p0 + pn, :], CVO, base, blk * 128, 128, cm, 512 + cb * 128, 512 + (cb + 1) * 128)
                            TR(p[:, 0:128], t_.ap, 128, [t_], [p])
                            CP(xT[:, blk * 128:(blk + 1) * 128], p[:, 0:128], [p], [xT], eng="act")
                        for d in range(2):
                            if samp:
                                DMA(h0[:, d:d + 1], s_lr[l, d, cb * 128:(cb + 1) * 128].rearrange("(p o) -> p o", o=1), [], [h0])
                            for c0 in range(0, Tn, 512):
                                n = min(512, Tn)
                                MM(ps[2][:, 0:n], lw[:, d, 0, cb, :], xT[:, c0:c0 + n], [lw, xT], [ps[2]])
                                ACT(rT[:, 0:n], ps[2][:, 0:n], AF.Sigmoid, [ps[2], lb], [rT], bias=lb[:, cb, d, 0:1])
                                MM(ps[3][:, 0:n], lw[:, d, 1, cb, :], xT[:, c0:c0 + n], [lw, xT], [ps[3]])
                                ACT(iT[:, 0:n], ps[3][:, 0:n], AF.Sigmoid, [ps[3], lb], [iT], bias=lb[:, cb, d, 1:2])
                                ACT(aT[:, c0:c0 + n], rT[:, 0:n], AF.Exp, [rT, nsp], [aT], scale=nsp[:, cb, d:d + 1])
                                ACT(tmq[:, 0:n], aT[:, c0:c0 + n], AF.Square, [aT], [tmq])
                                ACT(tmq[:, 0:n], tmq[:, 0:n], AF.Sqrt, [tmq], [tmq], scale=-1.0, bias=1.0)
                                TT(tmq[:, 0:n], tmq[:, 0:n], iT[:, 0:n], ALU.mult, [tmq, iT], [tmq])
                                TT(xiT[:, c0:c0 + n], tmq[:, 0:n], xT[:, c0:c0 + n], ALU.mult, [tmq, xT], [xiT])
                            init = h0[:, d:d + 1] if samp else 0.0
                            if d == 0:
                                k.op("dve", lambda e, init=init: e.tensor_tensor_scan(out=hT[0][:, 0:Tn], data0=aT[:, 0:Tn], data1=xiT[:, 0:Tn], initial=init, op0=ALU.mult, op1=ALU.add),
                                     reads=_r([aT, xiT, h0]), writes=_r([hT[0]]))
                            else:
                                k.op("dve", lambda e, init=init: e.tensor_tensor_scan(out=hT[1][:, Tn - 1::-1] if False else hT[1][:, 0:Tn][:, ::-1], data0=aT[:, 0:Tn][:, ::-1], data1=xiT[:, 0:Tn][:, ::-1], initial=init, op0=ALU.mult, op1=ALU.add),
                                     reads=_r([aT, xiT, h0]), writes=_r([hT[1]]))
                        if not samp:
                            DMA(o_lr[sidx, l, 0, cb * 128:(cb + 1) * 128].rearrange("(p o) -> p o", o=1), hT[0][:, Tn - 1:Tn], [hT[0]], [o_lr])
                            DMA(o_lr[sidx, l, 1, cb * 128:(cb + 1) * 128].rearrange("(p o) -> p o", o=1), hT[1][:, 0:1], [hT[1]], [o_lr])
                        TT(hT[0][:, 0:Tn], hT[0][:, 0:Tn], hT[1][:, 0:Tn], ALU.add, [hT[0], hT[1]], [hT[0]])
                        for blk in range(Tn // 128):
                            g_ = gt[blk % 2]; o_ = ot[blk % 2]; p = ps[4 + blk % 2]
                            LOADR(g_, lambda p0, pn: g_[p0:p0 + pn, :], U, base, blk * 128, 128, cm, C_LRG + cb * 128, C_LRG + (cb + 1) * 128)
                            ACT(g_.ap, g_.ap, AF.Gelu, [g_], [g_])
                            TR(p[:, 0:128], hT[0][:, blk * 128:(blk + 1) * 128], 128, [hT[0]], [p])
                            TT(o_.ap, p[:, 0:128], g_.ap, ALU.mult, [p, g_], [o_])
                            STORER(MIX, base, blk * 128, 128, cm, 768 + cb * 128, 768 + (cb + 1) * 128, o_, lambda p0, pn: o_[p0:p0 + pn, :])
              k.barrier()
            if "ssd" in mixers:
              with contextlib.ExitStack() as st:
                ssp = B.sb(st, "ssp", [128, 16]); nA = B.sb(st, "nA", [128, 8])
                DMA(ssp.ap, ssp_d[l], [], [ssp])
                ACT(nA.ap, ssp[:, 8:16], AF.Exp, [ssp], [nA])
                TS(nA.ap, nA.ap, -1.0, None, ALU.mult, ALU.bypass, [nA], [nA])
                HST = [B.sb(st, f"HST{d}", [64, 4, 64]) for d in range(2)]
                def mk(nm, shp): return [B.sb(st, f"{nm}{d}", shp) for d in range(2)]
                xbc = mk("xbc", [64, 512]); dtr = mk("dtr", [64, 4]); dtt = mk("dtt", [64, 4]); la = mk("la", [64, 4]); xdt = mk("xdt", [64, 4, 64])
                cc = mk("cc", [64, 4]); seg = mk("seg", [64, 4, 64]); eR = mk("eR", [64, 4, 64]); BCT = mk("BCT", [64, 4, 64])
                scT = mk("scT", [64, 4, 64]); CTs = mk("CTs", [64, 4, 64]); ych = mk("ych", [64, 256]); xt2 = mk("xt2", [64, 4, 64])
                sin = B.sb(st, "sin", [64, 4, 64]); sout = B.sb(st, "sout", [64, 4, 64])
                for (base, Tn, samp, sidx) in SEQS:
                    cm = samp and (l % 2 == 1); nch = Tn // 64
                    for d in range(2):
                        if samp:
                            DMA(sin.ap, s_ss[l, d].rearrange("h p n -> p h n"), [], [sin])
                            for h in range(4): TR(ps[0][0:64, h * 64:(h + 1) * 64], sin[:, h, :], 64, [sin], [ps[0]])
                            CP(HST[d].ap, ps[0][0:64, 0:256].rearrange("p (h n) -> p h n", h=4), [ps[0]], [HST[d]])
                        else:
                            k.op("dve", lambda e, d=d: e.memset(HST[d].ap, 0.0), writes=_r([HST[d]]))
                    for j in range(nch):
                        for d in range(2):
                            jc = j if d == 0 else nch - 1 - j; t0 = jc * 64; ll = 63 if d == 0 else 0
                            pA, pB, pC, pD = ps[d * 4 + 0], ps[d * 4 + 1], ps[d * 4 + 2], ps[d * 4 + 3]
                            X_, dr = xbc[d], dtr[d]
                            LOADR(X_, lambda p0, pn: X_[p0:p0 + pn, :], CVO, base, t0, 64, cm, 0, 512)
                            LOADR(dr, lambda p0, pn: dr[p0:p0 + pn, :], U, base, t0, 64, cm, C_SSDT + 4 * d, C_SSDT + 4 * d + 4)
                            TT(dtt[d].ap, dr.ap, ssp[0:64, 4 * d:4 * d + 4], ALU.add, [dr, ssp], [dtt[d]])
                            ACT(dtt[d].ap, dtt[d].ap, AF.Exp, [dtt[d]], [dtt[d]])
                            ACT(dtt[d].ap, dtt[d].ap, AF.Ln, [dtt[d]], [dtt[d]], bias=1.0)
                            TT(la[d].ap, dtt[d].ap, nA[0:64, 4 * d:4 * d + 4], ALU.mult, [dtt[d], nA], [la[d]])
                            TT(xdt[d].ap, X_[:, 0:256].rearrange("p (h q) -> p h q", h=4), dtt[d].ap.unsqueeze(2).to_broadcast([64, 4, 64]), ALU.mult, [X_, dtt[d]], [xdt[d]])
                            MM(pA[0:64, 0:4], TRI[:, d, :], la[d].ap, [TRI, la[d]], [pA])
                            CP(cc[d].ap, pA[0:64, 0:4], [pA], [cc[d]], eng="act")
                            for h in range(4):
                                MM(pB[0:64, h * 64:(h + 1) * 64], la[d][:, h:h + 1].to_broadcast([64, 64]), TRI[:, d, :], [la[d], TRI], [pB])
                            for h in range(4):
                                TS(seg[d][:, h, :], pB[0:64, h * 64:(h + 1) * 64], cc[d][:, h:h + 1], 0.0, ALU.subtract, ALU.min, [pB, cc[d]], [seg[d]])
                            ACT(seg[d].ap, seg[d].ap, AF.Exp, [seg[d]], [seg[d]])
                            ACT(eR[d].ap, pB[0:64, 0:256].rearrange("p (h q) -> p h q", h=4), AF.Exp, [pB], [eR[d]])
                            for i in range(4): TR(pC[0:64, i * 64:(i + 1) * 64], X_[:, 256 + 64 * i:256 + 64 * (i + 1)], 64, [X_], [pC])
                            CP(BCT[d].ap, pC[0:64, 0:256].rearrange("p (h q) -> p h q", h=4), [pC], [BCT[d]], eng="act")
                            for g2 in range(2):
                                MM(pD[0:64, g2 * 64:(g2 + 1) * 64], BCT[d][:, g2, :], BCT[d][:, 2 + g2, :], [BCT[d]], [pD])
                            TT(seg[d].ap, seg[d].ap, TRI[:, d, :].unsqueeze(1).to_broadcast([64, 4, 64]), ALU.mult, [seg[d], TRI], [seg[d]])
                            for g2 in range(2):
                                TT(scT[d][:, 2 * g2:2 * g2 + 2, :], seg[d][:, 2 * g2:2 * g2 + 2, :], pD[0:64, g2 * 64:(g2 + 1) * 64].unsqueeze(1).to_broadcast([64, 2, 64]), ALU.mult, [seg[d], pD], [scT[d]])
                                TT(CTs[d][:, 2 * g2:2 * g2 + 2, :], eR[d][:, 2 * g2:2 * g2 + 2, :], BCT[d][:, 2 + g2, :].unsqueeze(1).to_broadcast([64, 2, 64]), ALU.mult, [eR[d], BCT[d]], [CTs[d]])
                            for h in range(4):
                                MM(pA[0:64, 64 + h * 64:64 + (h + 1) * 64], scT[d][:, h, :], xdt[d][:, h, :], [scT[d], xdt[d]], [pA], start=True, stop=False)
                                MM(pA[0:64, 64 + h * 64:64 + (h + 1) * 64], CTs[d][:, h, :], HST[d][:, h, :], [CTs[d], HST[d]], [pA], start=False, stop=True)
                            CP(ych[d].ap, pA[0:64, 64:320], [pA], [ych[d]])
                            Yd = YF if d == 0 else YB; yc = ych[d]
                            STORER(Yd, base, t0, 64, cm, 0, 256, yc, lambda p0, pn: yc[p0:p0 + pn, :])
                            TT(xt2[d].ap, xdt[d].ap, seg[d][:, :, ll:ll + 1].to_broadcast([64, 4, 64]), ALU.mult, [xdt[d], seg[d]], [xt2[d]])
                            for h in range(4):
                                MM(pC[0:64, 256 + h * 64:256 + (h + 1) * 64], X_[:, 256 + 64 * (h // 2):256 + 64 * (h // 2 + 1)], xt2[d][:, h, :], [X_, xt2[d]], [pC])
                            for h in range(4):
                                STT(HST[d][:, h, :], HST[d][:, h, :], eR[d][:, h, ll:ll + 1], pC[0:64, 256 + h * 64:256 + (h + 1) * 64], ALU.mult, ALU.add, [HST[d], eR[d], pC], [HST[d]])
                    if not samp:
                        for d in range(2):
                            for h in range(4): TR(ps[0][0:64, h * 64:(h + 1) * 64], HST[d][:, h, :], 64, [HST[d]], [ps[0]])
                            CP(sout.ap, ps[0][0:64, 0:256].rearrange("p (h n) -> p h n", h=4), [ps[0]], [sout])
                            DMA(o_ss[sidx, l, d].rearrange("h p n -> p h n"), sout.ap, [sout], [o_ss])
              k.barrier()
              with contextlib.ExitStack() as st:
                sD = B.sb(st, "sD", [128, 4]); sn = B.sb(st, "sn", [128, 256]); DMA(sD.ap, ssD_d[l], [], [sD]); DMA(sn.ap, ssn_d[l], [], [sn])
                yf = B.sb(st, "yf", [128, 256]); yb = B.sb(st, "yb", [128, 256]); xs_ = B.sb(st, "xs_", [128, 256]); zz = B.sb(st, "zz", [128, 256])
                q2 = B.sb(st, "q2", [128, 2]); jk = B.sb(st, "jk", [128, 256])
                for ti in range(NTILE):
                    r0 = ti * 128
                    DMA(yf.ap, YF[r0:r0 + 128, 0:256], [YF], [yf]); DMA(yb.ap, YB[r0:r0 + 128, 0:256], [YB], [yb])
                    DMA(xs_.ap, CVO[r0:r0 + 128, 0:256], [CVO], [xs_]); DMA(zz.ap, U[r0:r0 + 128, C_SSZ:C_SSZ + 256], [U], [zz])
                    TT(yf.ap, yf.ap, yb.ap, ALU.add, [yf, yb], [yf])
                    TT(xs_.ap.rearrange("p (h q) -> p h q", h=4), xs_.ap.rearrange("p (h q) -> p h q", h=4), sD.ap.unsqueeze(2).to_broadcast([128, 4, 64]), ALU.mult, [xs_, sD], [xs_])
                    TT(yf.ap, yf.ap, xs_.ap, ALU.add, [yf, xs_], [yf])
                    ACT(zz.ap, zz.ap, AF.Silu, [zz], [zz])
                    TT(yf.ap, yf.ap, zz.ap, ALU.mult, [yf, zz], [yf])
                    ACT(jk.ap, yf.ap, AF.Square, [yf], [jk, q2], accum_out=q2[:, 0:1])
                    TS(q2[:, 1:2], q2[:, 0:1], 1.0 / 256, EPS, ALU.mult, ALU.add, [q2], [q2])
                    ACT(q2[:, 1:2], q2[:, 1:2], AF.Sqrt, [q2], [q2])
                    k.op("dve", lambda e: e.reciprocal(out=q2[:, 1:2], in_=q2[:, 1:2]), reads=_r([q2]), writes=_r([q2]))
                    STT(yf.ap, yf.ap, q2[:, 1:2], sn.ap, ALU.mult, ALU.mult, [yf, q2, sn], [yf])
                    DMA(MIX[r0:r0 + 128, 256:512], yf.ap, [yf], [MIX])
              k.barrier()

            if "mlstm" in mixers:
              with contextlib.ExitStack() as st:
                mlp = B.sb(st, "mlp", [128, 16]); DMA(mlp.ap, mlp_d[l], [], [mlp])
                NEGM = B.sb(st, "NEGM", [64, 2, 64]); TS(NEGM.ap, TRI.ap, -1.0, 1e30, ALU.add, ALU.mult, [TRI], [NEGM])
                def mk(nm, shp): return [B.sb(st, f"{nm}{d}", shp) for d in range(2)]
                CT = mk("mCT", [64, 4, 65]); mS = mk("mS", [64, 4])
                qq = mk("mq", [64, 256]); kk_ = mk("mk", [64, 256]); vx = mk("mvx", [64, 4, 65]); gi = mk("mgi", [64, 4]); gf = mk("mgf", [64, 4])
                bb = mk("mb", [64, 4]); imb = mk("mimb", [64, 4]); dm = mk("mdm", [64, 4, 64]); mx = mk("mmx", [64, 4]); mt8 = mk("mt8", [64, 8])
                nm_ = mk("mnm", [64, 4]); si = mk("msi", [64, 4]); qkT_ = mk("mqkT", [64, 8, 64]); qk = mk("mqk", [64, 4, 64]); qkt = mk("mqkt", [64, 4, 64])
                intra = mk("mintra", [64, 4, 65]); tot = mk("mtot", [64, 4, 65]); dn = mk("mdn", [64, 4]); hch = mk("mhch", [64, 4, 64])
                me = mk("mme", [64, 8]); t1 = mk("mt1", [64, 4]); we = mk("mwe", [64, 4]); se = mk("mse", [64, 4]); vw = mk("mvw", [64, 4, 65])
                sin = B.sb(st, "msin", [64, 4, 64]); sout = B.sb(st, "msout", [64, 4, 64])
                for d in range(2):
                    k.op("dve", lambda e, d=d: e.memset(vx[d].ap, 1.0), writes=_r([vx[d]]))
                for (base, Tn, samp, sidx) in SEQS:
                    cm = samp and (l % 2 == 1); nch = Tn // 64
                    for d in range(2):
                        if samp:
                            DMA(sin.ap, s_mc[l, d].rearrange("h v k -> v h k"), [], [sin])
                            for h in range(4): TR(ps[0][0:64, h * 64:(h + 1) * 64], sin[:, h, :], 64, [sin], [ps[0]])
                            CP(CT[d][:, :, 0:64], ps[0][0:64, 0:256].rearrange("p (h n) -> p h n", h=4), [ps[0]], [CT[d]])
                            for h in range(4):
                                DMA(CT[d][:, h, 64:65], s_mn[l, d, h, :].rearrange("(p o) -> p o", o=1), [], [CT[d]])
                            DMA(mS[d].ap, smm_d[l, d], [], [mS[d]])
                        else:
                            k.op("dve", lambda e, d=d: e.memset(CT[d].ap, 0.0), writes=_r([CT[d]]))
                            k.op("dve", lambda e, d=d: e.memset(mS[d].ap, 0.0), writes=_r([mS[d]]))
                    for j in range(nch):
                        for d in range(2):
                            jc = j if d == 0 else nch - 1 - j; t0 = jc * 64; ll = 63 if d == 0 else 0
                            pA, pB, pC, pD = ps[d * 4 + 0], ps[d * 4 + 1], ps[d * 4 + 2], ps[d * 4 + 3]
                            Q_, K_, V_, GI, GF = qq[d], kk_[d], vx[d], gi[d], gf[d]
                            LOADR(Q_, lambda p0, pn: Q_[p0:p0 + pn, :], U, base, t0, 64, cm, C_MLQ, C_MLQ + 256)
                            LOADR(K_, lambda p0, pn: K_[p0:p0 + pn, :], U, base, t0, 64, cm, C_MLK, C_MLK + 256)
                            for h in range(4):
                                LOADR(V_, lambda p0, pn, h=h: V_[p0:p0 + pn, h, 0:64], U, base, t0, 64, cm, C_MLV + 64 * h, C_MLV + 64 * (h + 1))
                            LOADR(GI, lambda p0, pn: GI[p0:p0 + pn, :], U, base, t0, 64, cm, C_MLI + 4 * d, C_MLI + 4 * d + 4)
                            LOADR(GF, lambda p0, pn: GF[p0:p0 + pn, :], U, base, t0, 64, cm, C_MLF + 4 * d, C_MLF + 4 * d + 4)
                            TS(K_.ap, K_.ap, 0.125, None, ALU.mult, ALU.bypass, [K_], [K_])
                            TT(GI.ap, GI.ap, mlp[0:64, 4 * d:4 * d + 4], ALU.add, [GI, mlp], [GI])
                            TT(GF.ap, GF.ap, mlp[0:64, 8 + 4 * d:8 + 4 * d + 4], ALU.add, [GF, mlp], [GF])
                            ACT(GF.ap, GF.ap, AF.Exp, [GF], [GF], scale=-1.0)
                            ACT(GF.ap, GF.ap, AF.Ln, [GF], [GF], bias=1.0)
                            TS(GF.ap, GF.ap, -1.0, None, ALU.mult, ALU.bypass, [GF], [GF])
                            MM(pA[0:64, 0:4], TRI[:, d, :], GF.ap, [TRI, GF], [pA])
                            CP(bb[d].ap, pA[0:64, 0:4], [pA], [bb[d]], eng="act")
                            TT(imb[d].ap, GI.ap, bb[d].ap, ALU.subtract, [GI, bb[d]], [imb[d]])
                            for h in range(4):
                                MM(pA[0:64, 64 + h * 64:64 + (h + 1) * 64], imb[d][:, h:h + 1].to_broadcast([64, 64]), ident[0:64, 0:64], [imb[d], ident], [pA])
                            for h in range(4):
                                STT(dm[d][:, h, :], pA[0:64, 64 + h * 64:64 + (h + 1) * 64], bb[d][:, h:h + 1], NEGM[:, 1 - d, :], ALU.add, ALU.add, [pA, bb[d], NEGM], [dm[d]])
                            k.op("dve", lambda e, d=d: e.tensor_reduce(out=mx[d].ap, in_=dm[d].ap, op=ALU.max, axis=AX.X), reads=_r([dm[d]]), writes=_r([mx[d]]))
                            TT(mt8[d][:, 4:8], bb[d].ap, mS[d].ap, ALU.add, [bb[d], mS[d]], [mt8[d]])
                            TT(mt8[d][:, 0:4], mx[d].ap, mt8[d][:, 4:8], ALU.max, [mx[d], mt8[d]], [mt8[d]])
                            TS(nm_[d].ap, mt8[d][:, 0:4], -1.0, None, ALU.mult, ALU.bypass, [mt8[d]], [nm_[d]])
                            for h in range(4):
                                ACT(dm[d][:, h, :], dm[d][:, h, :], AF.Exp, [dm[d], nm_[d]], [dm[d]], bias=nm_[d][:, h:h + 1])
                            TT(si[d].ap, mt8[d][:, 4:8], mt8[d][:, 0:4], ALU.subtract, [mt8[d]], [si[d]])
                            ACT(si[d].ap, si[d].ap, AF.Exp, [si[d]], [si[d]])
                            for h in range(4):
                                TR(pB[0:64, h * 64:(h + 1) * 64], Q_[:, h * 64:(h + 1) * 64], 64, [Q_], [pB])
                                TR(pB[0:64, 256 + h * 64:256 + (h + 1) * 64], K_[:, h * 64:(h + 1) * 64], 64, [K_], [pB])
                            CP(qkT_[d].ap, pB[0:64, :].rearrange("p (a b) -> p a b", a=8), [pB], [qkT_[d]], eng="act")
                            for h in range(4):
                                MM(pC[0:64, h * 64:(h + 1) * 64], qkT_[d][:, h, :], qkT_[d][:, 4 + h, :], [qkT_[d]], [pC])
                            TT(qk[d].ap, pC[0:64, 0:256].rearrange("p (a b) -> p a b", a=4), dm[d].ap, ALU.mult, [pC, dm[d]], [qk[d]])
                            for h in range(4):
                                TR(pC[0:64, 256 + h * 64:256 + (h + 1) * 64], qk[d][:, h, :], 64, [qk[d]], [pC])
                            CP(qkt[d].ap, pC[0:64, 256:512].rearrange("p (a b) -> p a b", a=4), [pC], [qkt[d]], eng="act")
                            for h in range(4):
                                MM(pD[0:64, h * 65:(h + 1) * 65], qkt[d][:, h, :], V_[:, h, :], [qkt[d], V_], [pD])
                                MM(pB[0:64, h * 65:(h + 1) * 65], qkT_[d][:, h, :], CT[d][:, h, :], [qkT_[d], CT[d]], [pB])
                            CP(intra[d].ap, pD[0:64, 0:260].rearrange("p (a b) -> p a b", a=4), [pD], [intra[d]])
                            for h in range(4):
                                STT(tot[d][:, h, :], pB[0:64, h * 65:(h + 1) * 65], si[d][:, h:h + 1], intra[d][:, h, :], ALU.mult, ALU.add, [pB, si[d], intra[d]], [tot[d]])
                            ACT(dn[d].ap, tot[d][:, :, 64], AF.Abs, [tot[d]], [dn[d]])
                            ACT(nm_[d].ap, nm_[d].ap, AF.Exp, [nm_[d]], [nm_[d]])
                            TT(dn[d].ap, dn[d].ap, nm_[d].ap, ALU.max, [dn[d], nm_[d]], [dn[d]])
                            k.op("dve", lambda e, d=d: e.reciprocal(out=dn[d].ap, in_=dn[d].ap), reads=_r([dn[d]]), writes=_r([dn[d]]))
                            TT(hch[d].ap, tot[d][:, :, 0:64], dn[d].ap.unsqueeze(2).to_broadcast([64, 4, 64]), ALU.mult, [tot[d], dn[d]], [hch[d]])
                            Yd = YF if d == 0 else YB; hc = hch[d]
                            STORER(Yd, base, t0, 64, cm, 256, 512, hc, lambda p0, pn: hc[p0:p0 + pn, :, :].rearrange("p a b -> p (a b)"))
                            TT(mt8[d][:, 4:8], bb[d].ap, bb[d].ap, ALU.bypass, [bb[d]], [mt8[d]]) if False else CP(mt8[d][:, 4:8], bb[d].ap, [bb[d]], [mt8[d]])
                            MM(pA[0:64, 8:16], ident[0:64, ll:ll + 1].to_broadcast([64, 64]), mt8[d].ap, [ident, mt8[d]], [pA])
                            CP(me[d].ap, pA[0:64, 8:16], [pA], [me[d]])
                            TT(t1[d].ap, me[d][:, 4:8], me[d][:, 0:4], ALU.subtract, [me[d]], [t1[d]])
                            TT(we[d].ap, imb[d].ap, t1[d].ap, ALU.add, [imb[d], t1[d]], [we[d]])
                            ACT(we[d].ap, we[d].ap, AF.Exp, [we[d]], [we[d]])
                            TT(se[d].ap, t1[d].ap, mS[d].ap, ALU.add, [t1[d], mS[d]], [se[d]])
                            ACT(se[d].ap, se[d].ap, AF.Exp, [se[d]], [se[d]])
                            TT(vw[d].ap, V_.ap, we[d].ap.unsqueeze(2).to_broadcast([64, 4, 65]), ALU.mult, [V_, we[d]], [vw[d]])
                            for h in range(4):
                                MM(pD[0:64, h * 65:(h + 1) * 65], K_[:, h * 64:(h + 1) * 64], vw[d][:, h, :], [K_, vw[d]], [pD])
                            for h in range(4):
                                STT(CT[d][:, h, :], CT[d][:, h, :], se[d][:, h:h + 1], pD[0:64, h * 65:(h + 1) * 65], ALU.mult, ALU.add, [CT[d], se[d], pD], [CT[d]])
                            CP(mS[d].ap, me[d][:, 0:4], [me[d]], [mS[d]])
                    if not samp:
                        for d in range(2):
                            for h in range(4): TR(ps[0][0:64, h * 64:(h + 1) * 64], CT[d][:, h, 0:64], 64, [CT[d]], [ps[0]])
                            CP(sout.ap, ps[0][0:64, 0:256].rearrange("p (h n) -> p h n", h=4), [ps[0]], [sout])
                            DMA(o_mc[sidx, l, d].rearrange("h v k -> v h k"), sout.ap, [sout], [o_mc])
                            for h in range(4):
                                DMA(o_mn[sidx, l, d, h, :].rearrange("(p o) -> p o", o=1), CT[d][:, h, 64:65], [CT[d]], [o_mn])
                            DMA(o_mm[sidx, l, d:d + 1, :], mS[d][0:1, :], [mS[d]], [o_mm])
              k.barrier()
              with contextlib.ExitStack() as st:
                gn = B.sb(st, "gn", [128, 256]); DMA(gn.ap, mln_d[l], [], [gn])
                yf = B.sb(st, "myf", [128, 256]); yb = B.sb(st, "myb", [128, 256]); zz = B.sb(st, "mzz", [128, 256])
                q4 = B.sb(st, "mq4", [128, 4]); jk = B.sb(st, "mjk", [128, 256])
                for ti in range(NTILE):
                    r0 = ti * 128
                    DMA(yf.ap, YF[r0:r0 + 128, 256:512], [YF], [yf]); DMA(yb.ap, YB[r0:r0 + 128, 256:512], [YB], [yb])
                    DMA(zz.ap, U[r0:r0 + 128, C_MLO:C_MLO + 256], [U], [zz])
                    TT(yf.ap, yf.ap, yb.ap, ALU.add, [yf, yb], [yf])
                    ACT(jk.ap, yf.ap, AF.Square, [yf], [jk])
                    k.op("dve", lambda e: e.tensor_reduce(out=q4.ap, in_=jk.ap.rearrange("p (a b) -> p a b", a=4), op=ALU.add, axis=AX.X), reads=_r([jk]), writes=_r([q4]))
                    TS(q4.ap, q4.ap, 1.0 / 64, EPS, ALU.mult, ALU.add, [q4], [q4])
                    ACT(q4.ap, q4.ap, AF.Sqrt, [q4], [q4])
                    k.op("dve", lambda e: e.reciprocal(out=q4.ap, in_=q4.ap), reads=_r([q4]), writes=_r([q4]))
                    TT(yf.ap.rearrange("p (a b) -> p a b", a=4), yf.ap.rearrange("p (a b) -> p a b", a=4), q4.ap.unsqueeze(2).to_broadcast([128, 4, 64]), ALU.mult, [yf, q4], [yf])
                    TT(yf.ap, yf.ap, gn.ap, ALU.mult, [yf, gn], [yf])
                    ACT(zz.ap, zz.ap, AF.Sigmoid, [zz], [zz])
                    TT(yf.ap, yf.ap, zz.ap, ALU.mult, [yf, zz], [yf])
                    DMA(MIX[r0:r0 + 128, 512:768], yf.ap, [yf], [MIX])
              k.barrier()

            if "rwkv" in mixers:
              with contextlib.ExitStack() as st:
                rp = B.sb(st, "rwp", [128, 4, 256]); DMA(rp.ap, rwp_d[l], [], [rp])
                ut = B.sb(st, "rut", [128, 768]); da = B.sb(st, "rda", [128, 1024]); rq = B.sb(st, "rrq", [128, 1536])
                q4 = B.sb(st, "rq4", [128, 4]); jk = B.sb(st, "rjk", [128, 256]); tm = B.sb(st, "rtm", [128, 256])
                def v4(ap): return ap.rearrange("p (a b) -> p a b", a=4)
                for ti in range(NTILE):
                    r0 = ti * 128
                    DMA(ut.ap, U[r0:r0 + 128, 0:768], [U], [ut]); DMA(da.ap, U[r0:r0 + 128, C_DEC:C_DEC + 1024], [U], [da])
                    TT(rq[:, 0:256], ut[:, 256:512], rp[:, 0, :], ALU.mult, [ut, rp], [rq])
                    ACT(jk.ap, rq[:, 0:256], AF.Square, [rq], [jk])
                    k.op("dve", lambda e: e.tensor_reduce(out=q4.ap, in_=v4(jk.ap), op=ALU.add, axis=AX.X), reads=_r([jk]), writes=_r([q4]))
                    TS(q4.ap, q4.ap, EPS, None, ALU.add, ALU.bypass, [q4], [q4])
                    ACT(q4.ap, q4.ap, AF.Sqrt, [q4], [q4])
                    k.op("dve", lambda e: e.reciprocal(out=q4.ap, in_=q4.ap), reads=_r([q4]), writes=_r([q4]))
                    TT(v4(rq[:, 0:256]), v4(rq[:, 0:256]), q4.ap.unsqueeze(2).to_broadcast([128, 4, 64]), ALU.mult, [rq, q4], [rq])
                    for d in range(2):
                        a_ = da[:, 512 + 256 * d:512 + 256 * (d + 1)]
                        STT(rq[:, 256 + 256 * d:512 + 256 * d], a_, -1.0, rq[:, 0:256], ALU.mult, ALU.mult, [da, rq], [rq])
                        STT(tm.ap, a_, -1.0, rp[:, 1, :], ALU.add, ALU.mult, [da, rp], [tm])
                        STT(rq[:, 768 + 256 * d:1024 + 256 * d], tm.ap, 1.0, ut[:, 256:512], ALU.add, ALU.mult, [tm, ut], [rq])
                    TT(tm.ap, ut[:, 0:256], ut[:, 256:512], ALU.mult, [ut], [tm])
                    TT(tm.ap, tm.ap, rp[:, 2, :], ALU.mult, [tm, rp], [tm])
                    k.op("dve", lambda e: e.tensor_reduce(out=q4.ap, in_=v4(tm.ap), op=ALU.add, axis=AX.X), reads=_r([tm]), writes=_r([q4]))
                    TT(v4(rq[:, 1280:1536]), v4(ut[:, 512:768]), q4.ap.unsqueeze(2).to_broadcast([128, 4, 64]), ALU.mult, [ut, q4], [rq])
                    DMA(RQ[r0:r0 + 128, :], rq.ap, [rq], [RQ])
              k.barrier()
              with contextlib.ExitStack() as st:
                ones = B.sb(st, "rones", [64, 1]); k.op("dve", lambda e: e.memset(ones.ap, 1.0), writes=_r([ones]))
                MK = B.sb(st, "rMK", [128, 2, 256]); SM = B.sb(st, "rSM", [128, 2, 64]); TRI2 = B.sb(st, "rTRI2", [128, 2, 128])
                for d in range(2):
                    DMA(MK[:, d, 0:128], rwm_d[d], [], [MK]); DMA(MK[:, d, 128:256], rwm_d[d], [], [MK]); DMA(SM[:, d, :], rws_d[d], [], [SM]); DMA(TRI2[:, d, :], tri2_d[d], [], [TRI2])
                ctx = []
                for d in range(2):
                    c = dict(d=d)
                    for nm in ("ka", "na", "kt", "rr", "lw", "e1", "e2", "V2"): c[nm] = B.sb(st, f"r{nm}{d}", [128, 256])
                    c["gL"] = B.sb(st, f"rgL{d}", [128, 2]); c["pc"] = ps[d * 4]; c["units"] = []
                    for q in range(2):
                        u = dict(d=d, q=q, c=c, pa=ps[d * 4 + 1 + q], pb=ps[d * 4 + 3]); cid = f"{d}{q}"
                        u["LR"] = B.sb(st, f"rLR{cid}", [128, 256]); u["G"] = B.sb(st, f"rG{cid}", [128, 256])
                        u["N0"] = B.sb(st, f"rN0{cid}", [128, 256]); u["Xn"] = [B.sb(st, f"rXn{cid}{i}", [128, 256]) for i in range(2)]
                        u["TT"] = B.sb(st, f"rTT{cid}", [128, 256]); u["Qs"] = B.sb(st, f"rQs{cid}", [128, 64]); u["PT"] = B.sb(st, f"rPT{cid}", [128, 64])
                        u["Ys"] = B.sb(st, f"rYs{cid}", [128, 64]); u["ST"] = B.sb(st, f"rSTp{cid}", [128, 64])
                        k.op("dve", lambda e, u=u: e.memset(u["N0"].ap, 0.0), writes=_r([u["N0"]]))
                        c["units"].append(u)
                    ctx.append(c)
                I2 = B.sb(st, "rI2", [128, 256])
                CP(I2[:, 0:128], ident.ap, [ident], [I2]); CP(I2[:, 128:256], ident.ap, [ident], [I2])
                sin3 = B.sb(st, "rsin3", [64, 128])
                import os as _os
                RWS = int(_os.environ.get("RW_STAGE", "9"))
                for (base, Tn, samp, sidx) in (SEQS if RWS > -3 else []):
                    cm = samp and (l % 2 == 1); nch = Tn // 64
                    if RWS == -2: nch = 0
                    for c in ctx:
                        d = c["d"]
                        for u in c["units"]:
                            q = u["q"]
                            if samp:
                                DMA(sin3.ap.rearrange("p (a b) -> p a b", a=2), s_rw[l, d, 2 * q:2 * q + 2].rearrange("h v k -> v h k"), [], [sin3])
                                TR(u["pa"][:, 0:64], sin3.ap, 64, [sin3], [u["pa"]])
                                CP(u["ST"].ap, u["pa"][:, 0:64], [u["pa"]], [u["ST"]])
                            else:
                                k.op("dve", lambda e, u=u: e.memset(u["ST"].ap, 0.0), writes=_r([u["ST"]]))
                    for j in range(nch):
                        for c in ctx:
                            d = c["d"]; jc = j if d == 0 else nch - 1 - j; t0 = jc * 64; pc = c["pc"]
                            srcs = [("ka", RQ, 0), ("na", RQ, 256 + 256 * d), ("kt", RQ, 768 + 256 * d), ("rr", U, C_RWR), ("lw", U, C_DEC + 256 * d), ("V2", U, C_RWV)]
                            for nm, A_, c0 in srcs:
                                t_ = c[nm]
                                for hf in range(2):
                                    LOADR(t_, lambda p0, pn, t_=t_, hf=hf: t_[64 * hf + p0:64 * hf + p0 + pn, :], A_, base, t0, 64, cm, c0, c0 + 256)
                            if RWS == -1: continue
                            SUB = int(_os.environ.get("RW_SUB", "99"))
                            ops = []
                            ops.append(lambda: MM(pc[:, 0:256], TRI2[:, d, :], c["lw"].ap, [TRI2, c["lw"]], [pc]))
                            ops.append(lambda: [MM(pc[:, 256 + q:257 + q], c["lw"][0:64, 128 * q:128 * (q + 1)], ones.ap, [c["lw"], ones], [pc]) for q in range(2)])
                            ops.append(lambda: ACT(c["e1"].ap, pc[:, 0:256], AF.Exp, [pc], [c["e1"]], scale=-1.0))
                            ops.append(lambda: TT(c["e2"].ap, pc[:, 0:256], c["lw"].ap, ALU.subtract, [pc, c["lw"]], [c["e2"]]))
                            ops.append(lambda: ACT(c["e2"].ap, c["e2"].ap, AF.Exp, [c["e2"]], [c["e2"]]))
                            ops.append(lambda: TT(c["na"].ap, c["na"].ap, c["e1"].ap, ALU.mult, [c["na"], c["e1"]], [c["na"]]))
                            ops.append(lambda: TT(c["kt"].ap, c["kt"].ap, c["e1"].ap, ALU.mult, [c["kt"], c["e1"]], [c["kt"]]))
                            ops.append(lambda: TT(c["ka"].ap, c["ka"].ap, c["e2"].ap, ALU.mult, [c["ka"], c["e2"]], [c["ka"]]))
                            ops.append(lambda: ACT(c["e1"].ap, pc[:, 0:256], AF.Exp, [pc], [c["e1"]]))
                            ops.append(lambda: TT(c["rr"].ap, c["rr"].ap, c["e1"].ap, ALU.mult, [c["rr"], c["e1"]], [c["rr"]]))
                            ops.append(lambda: ACT(c["gL"].ap, pc[:, 256:258], AF.Exp, [pc], [c["gL"]]))
                            for f_ in ops[:SUB]: f_()
                        units = [u for c in ctx for u in c["units"]]
                        import os as _os
                        RWS = int(_os.environ.get("RW_STAGE", "9"))
                        if RWS < 1: continue
                        for u in units:
                            c = u["c"]; q = u["q"]; pa = u["pa"]; sl = slice(128 * q, 128 * (q + 1))
                            for i_, nm in enumerate(("na", "kt", "ka", "rr")):
                                TR(pa[:, 64 * i_:64 * (i_ + 1)], c[nm][0:64, sl], 64, [c[nm]], [pa])
                            CP(u["LR"].ap, pa[:, 0:256], [pa], [u["LR"]], eng="act")
                        if RWS < 2: continue
                        for u in units:
                            c = u["c"]; d = c["d"]; pa = u["pa"]; pb = u["pb"]; o = 256 * u["q"]; LR = u["LR"]
                            for hh in range(2):
                                b0 = 64 * hh
                                MM(pa[b0:b0 + 64, 256:384], LR[b0:b0 + 64, 0:64], LR[b0:b0 + 64, 128:256], [LR], [pa])
                                MM(pa[b0:b0 + 64, 384:512], LR[b0:b0 + 64, 64:128], LR[b0:b0 + 64, 128:256], [LR], [pa])
                                MM(pb[b0:b0 + 64, o + b0:o + b0 + 64], LR[b0:b0 + 64, 128:192], LR[b0:b0 + 64, 0:64], [LR], [pb])
                            TT(u["G"].ap, pa[:, 256:512], MK[:, d, :], ALU.mult, [pa, MK], [u["G"]])
                            for hh in range(2):
                                b0 = 64 * hh
                                CP(u["N0"][b0:b0 + 64, b0:b0 + 64], u["G"][b0:b0 + 64, 0:64], [u["G"]], [u["N0"]])
                                TT(u["N0"][b0:b0 + 64, 128 + b0:128 + b0 + 64], pb[b0:b0 + 64, o + b0:o + b0 + 64], SM[b0:b0 + 64, 1 - d, :], ALU.mult, [pb, SM], [u["N0"]])
                            TT(u["TT"].ap, u["N0"].ap, I2.ap, ALU.add, [u["N0"], I2], [u["TT"]])
                        if RWS < 3: continue
                        for jj in range(1, 6):
                            for u in units:
                                pb = u["pb"]; o = 256 * u["q"]
                                Xp = u["N0"] if jj == 1 else u["Xn"][(jj - 1) % 2]; Xn = u["Xn"][jj % 2]
                                MM(pb[:, o:o + 128], Xp[:, 128:256], Xp[:, 0:128], [Xp], [pb])
                                if jj < 5: MM(pb[:, o + 128:o + 256], Xp[:, 0:128], Xp[:, 128:256], [Xp], [pb])
                                w_ = 256 if jj < 5 else 128
                                CP(Xn[:, 0:w_], pb[:, o:o + w_], [pb], [Xn], eng="act")
                            for u in units:
                                pb = u["pb"]; o = 256 * u["q"]; Xn = u["Xn"][jj % 2]; T_ = u["TT"]
                                MM(pb[:, o:o + 128], T_[:, 128:256], Xn[:, 0:128], [T_, Xn], [pb])
                                if jj < 5: MM(pb[:, o + 128:o + 256], Xn[:, 0:128], T_[:, 128:256], [T_, Xn], [pb])
                                w_ = 256 if jj < 5 else 128
                                TT(T_[:, 0:w_], T_[:, 0:w_], pb[:, o:o + w_], ALU.add, [T_, pb], [T_])
                        if RWS < 4: continue
                        for u in units:
                            c = u["c"]; d = c["d"]; q = u["q"]; pa = u["pa"]; LR = u["LR"]; V2 = c["V2"]; ST_ = u["ST"]; G = u["G"]
                            for hh in range(2):
                                b0 = 64 * hh; h = 2 * q + hh
                                MM(pa[b0:b0 + 64, 0:64], LR[b0:b0 + 64, 128:192], ST_[b0:b0 + 64, :], [LR, ST_], [pa], start=True, stop=False)
                                MM(pa[b0:b0 + 64, 0:64], G[b0:b0 + 64, 128:192], V2[b0:b0 + 64, 64 * h:64 * (h + 1)], [G, V2], [pa], start=False, stop=True)
                            CP(u["Qs"].ap, pa[:, 0:64], [pa], [u["Qs"]], eng="act")
                            MM(pa[:, 64:128], u["TT"][:, 0:128], u["Qs"].ap, [u["TT"], u["Qs"]], [pa])
                            CP(u["PT"].ap, pa[:, 64:128], [pa], [u["PT"]], eng="act")
                        if RWS < 5: continue
                        for u in units:
                            c = u["c"]; d = c["d"]; q = u["q"]; pa = u["pa"]; pb = u["pb"]; o = 256 * q; LR = u["LR"]; V2 = c["V2"]; ST_ = u["ST"]; G = u["G"]; PT = u["PT"]
                            jc = j if d == 0 else nch - 1 - j; t0 = jc * 64
                            for hh in range(2):
                                b0 = 64 * hh; h = 2 * q + hh
                                MM(pa[b0:b0 + 64, 128:192], LR[b0:b0 + 64, 192:256], ST_[b0:b0 + 64, :], [LR, ST_], [pa], start=True, stop=False)
                                MM(pa[b0:b0 + 64, 128:192], G[b0:b0 + 64, 192:256], V2[b0:b0 + 64, 64 * h:64 * (h + 1)], [G, V2], [pa], start=False, stop=False)
                                MM(pa[b0:b0 + 64, 128:192], G[b0:b0 + 64, 64:128], PT[b0:b0 + 64, :], [G, PT], [pa], start=False, stop=True)
                            Ys = u["Ys"]
                            CP(Ys.ap, pa[:, 128:192], [pa], [Ys], eng="act")
                            for hh in range(2):
                                h = 2 * q + hh
                                STORER(YR[d], base, t0, 64, cm, 64 * h, 64 * (h + 1), Ys, lambda p0, pn, Ys=Ys, hh=hh: Ys[64 * hh + p0:64 * hh + p0 + pn, :])
                            for hh in range(2):
                                b0 = 64 * hh; h = 2 * q + hh
                                MM(pb[b0:b0 + 64, o:o + 64], c["na"][b0:b0 + 64, 64 * h:64 * (h + 1)], PT[b0:b0 + 64, :], [c["na"], PT], [pb], start=True, stop=False)
                                MM(pb[b0:b0 + 64, o:o + 64], c["kt"][b0:b0 + 64, 64 * h:64 * (h + 1)], V2[b0:b0 + 64, 64 * h:64 * (h + 1)], [c["kt"], V2], [pb], start=False, stop=True)
                            TT(ST_.ap, ST_.ap, pb[:, o:o + 64], ALU.add, [ST_, pb], [ST_])
                            TS(ST_.ap, ST_.ap, c["gL"][:, q:q + 1], None, ALU.mult, ALU.bypass, [ST_, c["gL"]], [ST_])
                    if not samp:
                        for c in ctx:
                            d = c["d"]
                            for u in c["units"]:
                                q = u["q"]
                                TR(u["pa"][0:64, 0:128], u["ST"].ap, 128, [u["ST"]], [u["pa"]])
                                CP(sin3.ap, u["pa"][0:64, 0:128], [u["pa"]], [sin3])
                                DMA(o_rw[sidx, l, d, 2 * q:2 * q + 2].rearrange("h v k -> v h k"), sin3.ap.rearrange("p (a b) -> p a b", a=2), [sin3], [o_rw])
              k.barrier()
              with contextlib.ExitStack() as st:
                rp = B.sb(st, "rwp2", [128, 4, 256]); DMA(rp.ap, rwp_d[l], [], [rp])
                yf = B.sb(st, "ryf", [128, 256]); yb = B.sb(st, "ryb", [128, 256]); zz = B.sb(st, "rzz", [128, 256]); bo = B.sb(st, "rbo", [128, 256])
                q4 = B.sb(st, "rq4b", [128, 4]); jk = B.sb(st, "rjkb", [128, 256])
                def v4(ap): return ap.rearrange("p (a b) -> p a b", a=4)
                for ti in range(NTILE):
                    r0 = ti * 128
                    DMA(yf.ap, YR[0][r0:r0 + 128, :], [YR[0]], [yf]); DMA(yb.ap, YR[1][r0:r0 + 128, :], [YR[1]], [yb])
                    DMA(zz.ap, U[r0:r0 + 128, C_RWG:C_RWG + 256], [U], [zz]); DMA(bo.ap, RQ[r0:r0 + 128, 1280:1536], [RQ], [bo])
                    TT(yf.ap, yf.ap, yb.ap, ALU.add, [yf, yb], [yf])
                    ACT(jk.ap, yf.ap, AF.Square, [yf], [jk])
                    k.op("dve", lambda e: e.tensor_reduce(out=q4.ap, in_=v4(jk.ap), op=ALU.add, axis=AX.X), reads=_r([jk]), writes=_r([q4]))
                    TS(q4.ap, q4.ap, 1.0 / 64, EPS, ALU.mult, ALU.add, [q4], [q4])
                    ACT(q4.ap, q4.ap, AF.Sqrt, [q4], [q4])
                    k.op("dve", lambda e: e.reciprocal(out=q4.ap, in_=q4.ap), reads=_r([q4]), writes=_r([q4]))
                    TT(v4(yf.ap), v4(yf.ap), q4.ap.unsqueeze(2).to_broadcast([128, 4, 64]), ALU.mult, [yf, q4], [yf])
                    TT(yf.ap, yf.ap, rp[:, 3, :], ALU.mult, [yf, rp], [yf])
                    TT(yf.ap, yf.ap, bo.ap, ALU.add, [yf, bo], [yf])
                    ACT(zz.ap, zz.ap, AF.Sigmoid, [zz], [zz])
                    TT(yf.ap, yf.ap, zz.ap, ALU.mult, [yf, zz], [yf])
                    DMA(MIX[r0:r0 + 128, 0:256], yf.ap, [yf], [MIX])
              k.barrier()

            if stop_after != ("C", l):
              with contextlib.ExitStack() as st:
                wo = B.sb(st, "wo", [128, 8, D]); wq = wo
                pk = B.sb(st, "pk", [64, 2, 128])
                for sd in range(2): DMA(pk[:, sd, :], pk_d[l, sd], [], [pk])
                iot = B.sb(st, "iot", [128, 128]); DMA(iot.ap, iota_d.ap, [], [iot])
                mt = B.sb(st, "pmt", [128, D]); mT = B.sb(st, "pmT", [128, 8, 128]); x1 = B.sb(st, "px1", [128, D]); h2 = B.sb(st, "ph2", [128, D])
                sq = B.sb(st, "psq", [128, 2]); junk = B.sb(st, "pjunk", [128, D]); qT = B.sb(st, "pqT", [64, 16, 128])
                sc = B.sb(st, "psc", [128, 16, 128]); scw = B.sb(st, "pscw", [128, 128]); v16 = B.sb(st, "pv16", [128, 16, 16]); i16 = B.sb(st, "pi16", [128, 16, 16], U32)
                i16f = B.sb(st, "pi16f", [128, 16, 16]); cand = B.sb(st, "pcand", [128, 8, 256]); cidx = B.sb(st, "pcidx", [128, 8, 256]); cw = B.sb(st, "pcw", [128, 256])
                s16 = B.sb(st, "ps16", [128, 8, 16]); eqm = B.sb(st, "peqm", [128, 16, 256]); ef = B.sb(st, "pef", [128, 128]); gt = B.sb(st, "pgt", [128, 128])
                p16 = B.sb(st, "pp16", [128, 8, 16], U32); p16f = B.sb(st, "pp16f", [128, 8, 16]); pa16 = T(i16[:, 0:8, :]); pb16f = T(v16[:, 0:8, :]); gsel = scw; io256 = B.sb(st, "pio256", [128, 256]); DMA(io256.ap, iota256_d.ap, [], [io256])
                g8 = B.sb(st, "pg8", [128, 8]); eT = B.sb(st, "peT", [128, 128], I32)
                Ug = [B.sb(st, f"pUg{i}", [128, 2 * D]) for i in range(6)]
                eqf = eqm.ap.rearrange("p a b -> p (a b)")
                Ug += [T(eqf[:, 0:2 * D]), T(eqf[:, 2 * D:4 * D]), T(cand.ap.rearrange("p a b -> p (a b)")), T(cidx.ap.rearrange("p a b -> p (a b)"))]
                NGB = len(Ug)
                dots = B.sb(st, "pdots", [128, 128]); dots2 = T(cw.ap.rearrange("p (a b) -> p a b", a=2))
                Xs = xin if l == 0 else X
                for ti in range(NTILE):
                    s_ = 0 if ti < 4 else 1; r0 = ti * 128
                    DMA(mt.ap, MIX[r0:r0 + 128, :], [MIX], [mt]); DMA(x1.ap, Xs[r0:r0 + 128, :], [Xs], [x1])
                    for j in range(8): TR(ps[j // 4][:, (j % 4) * 128:(j % 4 + 1) * 128], mt[:, j * 128:(j + 1) * 128], 128, [mt], [ps[j // 4]])
                    for hh in range(2): CP(mT[:, hh * 4:(hh + 1) * 4, :], ps[hh].ap.rearrange("p (a b) -> p a b", a=4), [ps[hh]], [mT], eng="act")
                    DMA(wo.ap, wout_d[l].rearrange("(j p) n -> p j n", p=128), [], [wo])
                    for nb in range(2):
                        for j in range(8): MM(ps[2 + nb].ap, mT[:, j, :], wo[:, j, nb * 512:(nb + 1) * 512], [mT, wo], [ps[2 + nb]], start=(j == 0), stop=(j == 7))
                        TT(junk[:, nb * 512:(nb + 1) * 512], ps[2 + nb].ap, MOD[s_][:, 2, nb * 512:(nb + 1) * 512], ALU.mult, [ps[2 + nb], MOD[s_]], [junk])
                    TT(x1.ap, x1.ap, junk.ap, ALU.add, [x1, junk], [x1])
                    ACT(junk.ap, x1.ap, AF.Square, [x1], [junk, sq], accum_out=sq[:, 0:1])
                    TS(sq[:, 1:2], sq[:, 0:1], 1.0 / D, EPS, ALU.mult, ALU.add, [sq], [sq])
                    ACT(sq[:, 1:2], sq[:, 1:2], AF.Sqrt, [sq], [sq])
                    k.op("dve", lambda e: e.reciprocal(out=sq[:, 1:2], in_=sq[:, 1:2]), reads=_r([sq]), writes=_r([sq]))
                    STT(h2.ap, x1.ap, sq[:, 1:2], MOD[s_][:, 4, :], ALU.mult, ALU.mult, [x1, sq, MOD[s_]], [h2])
                    TT(h2.ap, h2.ap, MOD[s_][:, 3, :], ALU.add, [h2, MOD[s_]], [h2])
                    if "H2" in debug and l == 0:
                        if ti == 0: H2d = B.out("H2", [NT, D])
                        DMA(H2d[r0:r0 + 128, :], h2.ap, [h2], [H2d])
                    for j in range(8): TR(ps[j // 4][:, (j % 4) * 128:(j % 4 + 1) * 128], h2[:, j * 128:(j + 1) * 128], 128, [h2], [ps[j // 4]])
                    for hh in range(2): CP(mT[:, hh * 4:(hh + 1) * 4, :], ps[hh].ap.rearrange("p (a b) -> p a b", a=4), [ps[hh]], [mT], eng="act")
                    DMA(wq.ap, wq_d[l].rearrange("(j p) n -> p j n", p=128), [], [wq])
                    for g_ in range(16):
                        p = ps[g_ // 4]
                        for j in range(8): MM(p[0:64, (g_ % 4) * 128:(g_ % 4 + 1) * 128], wq[:, j, g_ * 64:(g_ + 1) * 64], mT[:, j, :], [wq, mT], [p], start=(j == 0), stop=(j == 7))
                    for b4 in range(4): CP(qT[:, b4 * 4:(b4 + 1) * 4, :], ps[b4][0:64, :].rearrange("p (a b) -> p a b", a=4), [ps[b4]], [qT], eng=("act" if b4 % 2 else "dve"))
                    for g_ in range(16):
                        p = ps[g_ // 4]
                        MM(p[:, (g_ % 4) * 128:(g_ % 4 + 1) * 128], qT[:, g_, :], pk[:, g_ % 2, :], [qT, pk], [p])
                    for b4 in range(4): CP(sc[:, b4 * 4:(b4 + 1) * 4, :], ps[b4].ap.rearrange("p (a b) -> p a b", a=4), [ps[b4]], [sc], eng=("act" if b4 % 2 else "dve"))
                    for g_ in range(16):
                        k.op("dve", lambda e, g_=g_: e.max(out=v16[:, g_, 0:8], in_=sc[:, g_, :]), reads=_r([sc]), writes=_r([v16]))
                        k.op("dve", lambda e, g_=g_: e.max_index(out=i16[:, g_, 0:8], in_max=v16[:, g_, 0:8], in_values=sc[:, g_, :]), reads=_r([sc, v16]), writes=_r([i16]))
                        k.op("dve", lambda e, g_=g_: e.match_replace(out=scw.ap, in_to_replace=v16[:, g_, 0:8], in_values=sc[:, g_, :], imm_value=-1e30), reads=_r([sc, v16]), writes=_r([scw]))
                        k.op("dve", lambda e, g_=g_: e.max(out=v16[:, g_, 8:16], in_=scw.ap), reads=_r([scw]), writes=_r([v16]))
                        k.op("dve", lambda e, g_=g_: e.max_index(out=i16[:, g_, 8:16], in_max=v16[:, g_, 8:16], in_values=scw.ap), reads=_r([scw, v16]), writes=_r([i16]))
                    CP(i16f.ap, i16.ap, [i16], [i16f])
                    for hp in range(8):
                        a_v = v16[:, 2 * hp, :].unsqueeze(2).to_broadcast([128, 16, 16]); b_v = v16[:, 2 * hp + 1, :].unsqueeze(1).to_broadcast([128, 16, 16])
                        TT(cand[:, hp, :].rearrange("p (a b) -> p a b", a=16), a_v, b_v, ALU.add, [v16], [cand])
                        a_i = i16f[:, 2 * hp, :].unsqueeze(2).to_broadcast([128, 16, 16]); b_i = i16f[:, 2 * hp + 1, :].unsqueeze(1).to_broadcast([128, 16, 16])
                        STT(cidx[:, hp, :].rearrange("p (a b) -> p a b", a=16), a_i, 128.0, b_i, ALU.mult, ALU.add, [i16f], [cidx])
                    for hp in range(8):
                        k.op("dve", lambda e, hp=hp: e.max(out=s16[:, hp, 0:8], in_=cand[:, hp, :]), reads=_r([cand]), writes=_r([s16]))
                        k.op("dve", lambda e, hp=hp: e.match_replace(out=cw.ap, in_to_replace=s16[:, hp, 0:8], in_values=cand[:, hp, :], imm_value=-1e30), reads=_r([cand, s16]), writes=_r([cw]))
                        k.op("dve", lambda e, hp=hp: e.max(out=s16[:, hp, 8:16], in_=cw.ap), reads=_r([cw]), writes=_r([s16]))
                        k.op("dve", lambda e, hp=hp: e.max_index(out=p16[:, hp, 0:8], in_max=s16[:, hp, 0:8], in_values=cand[:, hp, :]), reads=_r([cand, s16]), writes=_r([p16]))
                        k.op("dve", lambda e, hp=hp: e.max_index(out=p16[:, hp, 8:16], in_max=s16[:, hp, 8:16], in_values=cw.ap), reads=_r([cw, s16]), writes=_r([p16]))
                    k.op("dve", lambda e: e.tensor_scalar(out=pa16.ap, in0=p16.ap, scalar1=4, scalar2=None, op0=ALU.logical_shift_right), reads=_r([p16]), writes=_r([pa16]))
                    k.op("dve", lambda e: e.tensor_scalar(out=p16.ap, in0=p16.ap, scalar1=15, scalar2=None, op0=ALU.bitwise_and), reads=_r([p16]), writes=_r([p16]))
                    CP(p16f.ap, pa16.ap, [pa16], [p16f]); CP(pb16f.ap, p16.ap, [p16], [pb16f])
                    eq3 = eqm.ap.rearrange("p a b -> p (a b)")[:, 0:2048].rearrange("p (a b) -> p a b", b=16)
                    eq4 = eqm.ap.rearrange("p a b -> p (a b)")[:, 0:2048].rearrange("p (h a b) -> p h a b", h=8, b=16)
                    i4 = i16f.ap.rearrange("p (h s) m -> p h s m", s=2)
                    for side, pf in ((0, p16f), (1, pb16f)):
                        TT(eq3, io256[:, 0:16].unsqueeze(1).to_broadcast([128, 128, 16]), pf.ap.rearrange("p a b -> p (a b)").unsqueeze(2).to_broadcast([128, 128, 16]), ALU.is_equal, [io256, pf], [eqm])
                        TT(eq4, eq4, i4[:, :, side, :].unsqueeze(2).to_broadcast([128, 8, 16, 16]), ALU.mult, [eqm, i16f], [eqm])
                        k.op("dve", lambda e, side=side: e.tensor_reduce(out=(gsel if side == 0 else ef).ap, in_=eq3, op=ALU.add, axis=AX.X), reads=_r([eqm]), writes=_r([gsel if side == 0 else ef]))
                    STT(ef.ap, gsel.ap, 128.0, ef.ap, ALU.mult, ALU.add, [gsel, ef], [ef])
                    gt3 = gt.ap.rearrange("p (a b) -> p a b", a=8)
                    TT(gt3, s16.ap, s16[:, :, 0:1].to_broadcast([128, 8, 16]), ALU.subtract, [s16], [gt])
                    ACT(gt.ap, gt.ap, AF.Exp, [gt], [gt])
                    k.op("dve", lambda e: e.tensor_reduce(out=g8.ap, in_=gt3, op=ALU.add, axis=AX.X), reads=_r([gt]), writes=_r([g8]))
                    k.op("dve", lambda e: e.reciprocal(out=g8.ap, in_=g8.ap), reads=_r([g8]), writes=_r([g8]))
                    TT(gt3, gt3, g8.ap.unsqueeze(2).to_broadcast([128, 8, 16]), ALU.mult, [gt, g8], [gt])
                    CP(eT.ap, ef.ap, [ef], [eT])
                    GS = 4
                    for nb in range(2): CP(ps[4 + nb].ap, h2[:, nb * 512:(nb + 1) * 512], [h2], [ps[4 + nb]], eng="act")
                    ps[4].res.excl = False; ps[5].res.excl = False
                    dres = [T(None) for _ in range(128)]
                    ACC = [(ps[6], ps[7]), (ps[0], ps[1]), (ps[2], ps[3])]
                    for g0 in range(0, 128, GS):
                        for j_ in range(g0, g0 + GS):
                            u_ = Ug[j_ % NGB]
                            k.dma(None, None, reads=_r([eT]), writes=_r([u_]), q="pool", fn=lambda e, u_=u_, j_=j_: e.indirect_dma_start(out=u_.ap, out_offset=None, in_=puv_d[l],
                                  in_offset=bass.IndirectOffsetOnAxis(ap=eT[:, j_:j_ + 1], axis=0)))
                            for nb in range(2):
                                k.op("dve", lambda e, u_=u_, j_=j_, nb=nb: e.scalar_tensor_tensor(out=junk[:, nb * 512:(nb + 1) * 512], in0=u_[:, nb * 512:(nb + 1) * 512], scalar=1.0, in1=ps[4 + nb].ap,
                                     op0=ALU.mult, op1=ALU.mult, accum_out=dots2[:, nb, j_:j_ + 1]), reads=_r([u_, ps[4 + nb]]), writes=_r([dres[j_]]))
                        TT(dots[:, g0:g0 + GS], dots2[:, 0, g0:g0 + GS], dots2[:, 1, g0:g0 + GS], ALU.add, dres[g0:g0 + GS], [dots])
                        ACT(dots[:, g0:g0 + GS], dots[:, g0:g0 + GS], AF.Gelu, [dots], [dots])
                        TT(dots[:, g0:g0 + GS], dots[:, g0:g0 + GS], gt[:, g0:g0 + GS], ALU.mult, [dots, gt], [dots])
                        for j_ in range(g0, g0 + GS):
                            u_ = Ug[j_ % NGB]; A_ = ACC[j_ % 3]
                            for nb in range(2):
                                vv = u_[:, D + nb * 512:D + (nb + 1) * 512]
                                if j_ < 3:
                                    TS(A_[nb].ap, vv, dots[:, j_:j_ + 1], None, ALU.mult, ALU.bypass, [u_, dots], [A_[nb]])
                                else:
                                    STT(A_[nb].ap, vv, dots[:, j_:j_ + 1], A_[nb].ap, ALU.mult, ALU.add, [u_, dots, A_[nb]], [A_[nb]])
                    ps[4].res.excl = True; ps[5].res.excl = True
                    for nb in range(2):
                        CP(junk[:, nb * 512:(nb + 1) * 512], ACC[1][nb].ap, [ACC[1][nb]], [junk])
                        TT(junk[:, nb * 512:(nb + 1) * 512], junk[:, nb * 512:(nb + 1) * 512], ACC[2][nb].ap, ALU.add, [junk, ACC[2][nb]], [junk])
                        TT(junk[:, nb * 512:(nb + 1) * 512], junk[:, nb * 512:(nb + 1) * 512], ACC[0][nb].ap, ALU.add, [junk, ACC[0][nb]], [junk])
                        TT(junk[:, nb * 512:(nb + 1) * 512], junk[:, nb * 512:(nb + 1) * 512], MOD[s_][:, 5, nb * 512:(nb + 1) * 512], ALU.mult, [junk, MOD[s_]], [junk])
                    TT(x1.ap, x1.ap, junk.ap, ALU.add, [x1, junk], [x1])
                    DMA(X[r0:r0 + 128, :], x1.ap, [x1], [X])
              k.barrier()

            if stop_after in (("C", l), ("E", l)):
                break
        with contextlib.ExitStack() as st:
            fn = B.sb(st, "fn", [128, D]); DMA(fn.ap, fin_d.ap, [], [fn])
            xt = B.sb(st, "fxt", [128, D]); junk = B.sb(st, "fjunk", [128, D]); sq = B.sb(st, "fsq", [128, 2])
            for ti in range(NTILE):
                r0 = ti * 128
                DMA(xt.ap, X[r0:r0 + 128, :], [X], [xt])
                ACT(junk.ap, xt.ap, AF.Square, [xt], [junk, sq], accum_out=sq[:, 0:1])
                TS(sq[:, 1:2], sq[:, 0:1], 1.0 / D, EPS, ALU.mult, ALU.add, [sq], [sq])
                ACT(sq[:, 1:2], sq[:, 1:2], AF.Sqrt, [sq], [sq])
                k.op("dve", lambda e: e.reciprocal(out=sq[:, 1:2], in_=sq[:, 1:2]), reads=_r([sq]), writes=_r([sq]))
                STT(junk.ap, xt.ap, sq[:, 1:2], fn.ap, ALU.mult, ALU.mult, [xt, sq, fn], [junk])
                DMA(yp[r0:r0 + 128, :], junk.ap, [junk], [yp])
    k.finish()
    return B


def host_inputs(inputs, core):
    f = np.float32
    I = {k_: np.asarray(v) for k_, v in inputs.items()}
    b = core if core < 2 else core % 2
    xin = np.concatenate([I["x_prompt"][2 * core].astype(f), I["x_prompt"][2 * core + 1].astype(f), I["x_sample"][b].astype(f)], 0)
    cv = np.stack([I["c_ctx"], I["c"][b]], 0).astype(f)
    cvecs = cv.reshape(2, 8, 128).transpose(2, 0, 1).copy()
    m = {"xin": xin, "cvecs": cvecs}
    for nm, key in (("s_rw", "state_rwkv"), ("s_ss", "state_ssm"), ("s_mc", "state_mlstm_C"), ("s_mn", "state_mlstm_n"), ("s_mm", "state_mlstm_m"), ("s_lr", "state_lru")):
        m[nm] = np.ascontiguousarray(I[key][b].astype(f))
    m["s_mm_rep"] = np.ascontiguousarray(np.broadcast_to(m["s_mm"][:, :, None, :], (L_, 2, 64, 4)))
    return m


def host_shared(inputs):
    f = np.float32
    I = {k_: np.asarray(v).astype(f) if np.asarray(v).dtype != np.int32 else np.asarray(v) for k_, v in inputs.items()}
    rep = lambda a: np.ascontiguousarray(np.broadcast_to(a[:, None, ...], (a.shape[0], 128) + a.shape[1:]))
    m = {}
    m["ident"] = np.eye(128, dtype=f)
    m["mod_w"] = I["mod_w"]; m["mod_b_rep"] = rep(I["mod_b"])
    m["norm_rep"] = rep(np.stack([I["norm1"], I["norm2"]], 1))
    m["w_ext"] = np.ascontiguousarray(np.concatenate([I["w_in"], I["rw_wA"][:, 0], I["rw_wA"][:, 1], I["rw_aA"][:, 0], I["rw_aA"][:, 1]], axis=2))
    wB = np.zeros((L_, 128, 2, 256), f)
    wB[:, 0:64, 0] = I["rw_wB"][:, 0]; wB[:, 64:128, 0] = I["rw_wB"][:, 1]; wB[:, 0:64, 1] = I["rw_aB"][:, 0]; wB[:, 64:128, 1] = I["rw_aB"][:, 1]
    m["wB_sb"] = wB
    m["w0a0_rep"] = rep(np.concatenate([I["rw_w0"][:, 0], I["rw_w0"][:, 1], I["rw_a0"][:, 0], I["rw_a0"][:, 1]], 1))
    cvw = np.concatenate([I["ssm_conv_w"], I["lru_conv_w"]], axis=3).reshape(L_, 9, 768)
    m["cvw_rep"] = rep(cvw); m["cvb_rep"] = rep(np.concatenate([I["ssm_conv_b"], I["lru_conv_b"]], 1))
    tri = np.triu(np.ones((64, 64), f))
    m["tri"] = np.stack([tri, tri.T.copy()], 0)
    m["jmat"] = np.eye(128, dtype=f)[::-1].copy()
    incl = m["tri"]; strict = incl - np.eye(64, dtype=f)[None]
    m["rwm"] = np.ascontiguousarray(np.concatenate([np.concatenate([strict, incl], 2)] * 2, 1))
    t2_ = np.zeros((2, 128, 128), f); t2_[:, 0:64, 0:64] = incl; t2_[:, 64:128, 64:128] = incl; m["tri2"] = t2_
    m["rws"] = np.ascontiguousarray(np.concatenate([strict] * 2, 1))
    lw = np.zeros((L_, 2, 2, 2, 128, 128), f)
    for cb in range(2):
        for hh in range(2):
            h = cb * 2 + hh
            lw[:, :, 0, cb, hh * 64:(hh + 1) * 64, hh * 64:(hh + 1) * 64] = I["lru_wr"][:, :, h]
            lw[:, :, 1, cb, hh * 64:(hh + 1) * 64, hh * 64:(hh + 1) * 64] = I["lru_wi"][:, :, h]
    m["lruw"] = lw
    br = I["lru_br"].reshape(L_, 2, 256); bi = I["lru_bi"].reshape(L_, 2, 256); lam = I["lru_lam"]
    pb = np.stack([br, bi, lam], -1)
    m["lrub"] = np.ascontiguousarray(pb.reshape(L_, 2, 2, 128, 3).transpose(0, 3, 2, 1, 4))
    m["ssp_rep"] = rep(np.concatenate([I["ssm_dt_bias"].reshape(L_, 8), I["ssm_A_log"].reshape(L_, 8)], 1))
    m["w_out"] = I["w_out"]; m["peer_wq"] = I["peer_wq"];
    for i in range(L_):
        m[f"peer_uv{i}"] = np.ascontiguousarray(np.concatenate([I["peer_u"][i], I["peer_v"][i]], axis=1))
    m["peer_kT"] = np.ascontiguousarray(np.stack([I["peer_k1"], I["peer_k2"]], 1).transpose(0, 1, 3, 2))
    m["fin_rep"] = np.ascontiguousarray(np.broadcast_to(I["final_norm"][None, :], (128, D)))
    m["iota256"] = np.ascontiguousarray(np.broadcast_to(np.arange(256, dtype=f)[None, :], (128, 256)))
    m["iota128"] = np.ascontiguousarray(np.broadcast_to(np.arange(128, dtype=f)[None, :], (128, 128)))
    m["rwp_rep"] = rep(np.stack([I["rw_kk"], I["rw_ka"], I["rw_rk"].reshape(L_, 256), I["rw_ln"]], 1))
    m["mlp_rep"] = rep(np.concatenate([I["ml_ib"].reshape(L_, 8), I["ml_fb"].reshape(L_, 8)], 1)); m["mln_rep"] = rep(I["ml_norm"])
    m["ssD_rep"] = rep(I["ssm_D"]); m["ssn_rep"] = rep(I["ssm_norm"])
    p = np.arange(128)
    m["cmask"] = np.stack([(p % 64 != 0), (p % 64 != 63)], 1).astype(f)
    return m


_CACHE = {}

def kernel(**inputs):
    if "B" not in _CACHE:
        _CACHE["B"] = build()
    B = _CACHE["B"]
    shared = host_shared(inputs)
    in_maps = []
    for c in range(8):
        m = dict(shared); m.update(host_inputs(inputs, c)); in_maps.append(m)
    res = run_bass_kernel_spmd(B.nc, in_maps, core_ids=list(range(8)))
    R = res.results
    f = np.float32
    y_prompt = np.stack([R[c]["y_tok"][i * 256:(i + 1) * 256] for c in range(8) for i in range(2)], 0).astype(f)
    y_sample = np.stack([R[b]["y_tok"][NP:] for b in range(2)], 0).astype(f)
    cat = lambda nm: np.concatenate([R[c][nm] for c in range(8)], 0).astype(f)
    return (y_prompt, y_sample, cat("st_rwkv"), cat("st_ssm"), cat("st_mlC"), cat("st_mln"), cat("st_mlm"), cat("st_lru"))
```
